# Optimizing a Trainium2 kernel written in Bass

```python
import jax
import jax.numpy as jnp
from jax import lax
import numpy as np

D_MODEL = 2048
BATCH = 8
SEQ = 4096
DEPTH = 4

GRID_W = 64
CTX_LEN = 256
EPS = 1e-6
ROPE_THETA = 10000.0

MLA_HEADS = 6
MLA_Q_RANK = 512
MLA_KV_RANK = 512
MLA_NOPE = 128
MLA_ROPE = 64
MLA_V = 128
MLA_Q_BLOCK = 128

SWA_HEADS = 6
SWA_KV_HEADS = 2
SWA_GROUP = SWA_HEADS // SWA_KV_HEADS
SWA_HEAD_DIM = 128
SWA_WINDOW = 128
SWA_BLOCK = SWA_WINDOW

GLA_HEADS = 4
GLA_DK = 64
GLA_DV = 128
GLA_GATE_RANK = 16
GLA_TAU = 16.0
GLA_CHUNK = 64

MIX_WIDTH = MLA_HEADS * MLA_V + SWA_HEADS * SWA_HEAD_DIM + GLA_HEADS * GLA_DV
FFN_HIDDEN = -(-8 * D_MODEL // (3 * 256)) * 256
N_MOD = 6

IN_SIZES = (MLA_Q_RANK, MLA_KV_RANK, MLA_ROPE,
            SWA_HEADS * SWA_HEAD_DIM, SWA_KV_HEADS * SWA_HEAD_DIM, SWA_KV_HEADS * SWA_HEAD_DIM,
            GLA_HEADS * GLA_DK, GLA_HEADS * GLA_DK, GLA_HEADS * GLA_DV,
            2 * GLA_GATE_RANK, GLA_HEADS * GLA_DV)
IN_WIDTH = sum(IN_SIZES)

kernel_name = 'hybrid_mla_swa_gla_dit'


def rmsnorm(x, g):
    x32 = x.astype(jnp.float32)
    y = x32 * lax.rsqrt(jnp.mean(x32 * x32, axis=-1, keepdims=True) + EPS)
    return (y * g.astype(jnp.float32)).astype(x.dtype)


def modulate(x, g, shift, scale):
    return rmsnorm(x, g) * (1 + scale) + shift


def rope_1d(x, pos):
    half = x.shape[-1] // 2
    freqs = ROPE_THETA ** (-jnp.arange(half, dtype=jnp.float32) / half)
    ang = pos.astype(jnp.float32)[:, None] * freqs
    cos = jnp.cos(ang)[:, None, :].astype(x.dtype)
    sin = jnp.sin(ang)[:, None, :].astype(x.dtype)
    x1, x2 = x[..., :half], x[..., half:]
    return jnp.concatenate([x1 * cos - x2 * sin, x2 * cos + x1 * sin], axis=-1)


def rope_2d(x, rows, cols):
    d = x.shape[-1] // 2
    return jnp.concatenate([rope_1d(x[..., :d], rows), rope_1d(x[..., d:], cols)], axis=-1)


def mla_q(cq, g_q, w_uq, pos):
    B, N, _ = cq.shape
    q = (rmsnorm(cq, g_q) @ w_uq).reshape(B, N, MLA_HEADS, MLA_NOPE + MLA_ROPE)
    q_nope, q_rope = q[..., :MLA_NOPE], q[..., MLA_NOPE:]
    if pos is not None:
        q_rope = rope_2d(q_rope, *pos)
    return q_nope, q_rope


def mla_kv(ckv, kr, g_kv, w_ukv, pos):
    B, N, _ = ckv.shape
    kv = (rmsnorm(ckv, g_kv) @ w_ukv).reshape(B, N, MLA_HEADS, MLA_NOPE + MLA_V)
    k_rope = kr[:, :, None, :]
    if pos is not None:
        k_rope = rope_2d(k_rope, *pos)
    return kv[..., :MLA_NOPE], k_rope[:, :, 0], kv[..., MLA_NOPE:]


def mla_attend(qn, qr, kn, kr, v):
    scale = (MLA_NOPE + MLA_ROPE) ** -0.5
    s = (jnp.einsum('bqhd,bkhd->bhqk', qn, kn)
         + jnp.einsum('bqhr,bkr->bhqk', qr, kr)).astype(jnp.float32) * scale
    p = jax.nn.softmax(s, axis=-1).astype(v.dtype)
    return jnp.einsum('bhqk,bkhd->bqhd', p, v)


def mla_latent(qn, qr, kn, kr, v):
    B, N, H, _ = qn.shape
    nb = N // MLA_Q_BLOCK
    to_blocks = lambda t: jnp.moveaxis(t.reshape(B, nb, MLA_Q_BLOCK, *t.shape[2:]), 1, 0)
    out = lax.map(lambda qs: mla_attend(qs[0], qs[1], kn, kr, v), (to_blocks(qn), to_blocks(qr)))
    return jnp.moveaxis(out, 0, 1).reshape(B, N, H * MLA_V)


def sink_softmax(scores, sink):
    sink = jnp.broadcast_to(sink.astype(jnp.float32), scores.shape[:-1] + (1,))
    p = jax.nn.softmax(jnp.concatenate([scores, sink], axis=-1), axis=-1)
    return p[..., :-1]


def swa_latent(q, k, v, kc, vc, sink):
    B, N = q.shape[:2]
    nb = N // SWA_BLOCK
    band_len = 3 * SWA_BLOCK
    scale = SWA_HEAD_DIM ** -0.5
    qb = q.reshape(B, nb, SWA_BLOCK, SWA_KV_HEADS, SWA_GROUP, SWA_HEAD_DIM)

    def band(t):
        tp = jnp.pad(t, ((0, 0), (SWA_BLOCK, SWA_BLOCK), (0, 0), (0, 0)))
        tp = tp.reshape(B, nb + 2, SWA_BLOCK, SWA_KV_HEADS, SWA_HEAD_DIM)
        return jnp.concatenate([tp[:, :-2], tp[:, 1:-1], tp[:, 2:]], axis=2)

    kb, vb = band(k), band(v)
    blk = jnp.arange(nb)[:, None, None] * SWA_BLOCK
    qpos = blk + jnp.arange(SWA_BLOCK)[None, :, None]
    kpos = blk - SWA_BLOCK + jnp.arange(band_len)[None, None, :]
    valid = (jnp.abs(qpos - kpos) <= SWA_WINDOW) & (kpos >= 0) & (kpos < N)
    s_loc = jnp.einsum('bnqhgd,bnkhd->bnhgqk', qb, kb).astype(jnp.float32) * scale
    s_loc = jnp.where(valid[None, :, None, None], s_loc, -jnp.inf)
    s_ctx = jnp.einsum('bnqhgd,bkhd->bnhgqk', qb, kc).astype(jnp.float32) * scale
    sink_b = sink.reshape(SWA_KV_HEADS, SWA_GROUP)[None, None, :, :, None, None]
    p = sink_softmax(jnp.concatenate([s_loc, s_ctx], axis=-1), sink_b).astype(v.dtype)
    o = (jnp.einsum('bnhgqk,bnkhd->bnqhgd', p[..., :band_len], vb)
         + jnp.einsum('bnhgqk,bkhd->bnqhgd', p[..., band_len:], vc))
    return o.reshape(B, N, SWA_HEADS * SWA_HEAD_DIM)


def swa_context(qc, kc, vc, sink):
    B, C = qc.shape[:2]
    qg = qc.reshape(B, C, SWA_KV_HEADS, SWA_GROUP, SWA_HEAD_DIM)
    s = jnp.einsum('bqhgd,bkhd->bhgqk', qg, kc).astype(jnp.float32) * SWA_HEAD_DIM ** -0.5
    p = sink_softmax(s, sink.reshape(SWA_KV_HEADS, SWA_GROUP)[None, :, :, None, None])
    o = jnp.einsum('bhgqk,bkhd->bqhgd', p.astype(vc.dtype), vc)
    return o.reshape(B, C, SWA_HEADS * SWA_HEAD_DIM)


def gla_prepare(parts, w_f, b_f, w_b, b_b):
    q, k, v, glr, r = parts
    B, N, _ = q.shape
    shp = (B, N, GLA_HEADS, GLA_DK)
    la_f = jax.nn.log_sigmoid((glr[..., :GLA_GATE_RANK] @ w_f + b_f).astype(jnp.float32)) / GLA_TAU
    la_b = jax.nn.log_sigmoid((glr[..., GLA_GATE_RANK:] @ w_b + b_b).astype(jnp.float32)) / GLA_TAU
    return (q.reshape(shp), k.reshape(shp), v.reshape(B, N, GLA_HEADS, GLA_DV),
            la_f.reshape(shp), la_b.reshape(shp), r)


def gla_chunked(q, k, v, log_a, s0, strict, want_out):
    B, N, H, DK = q.shape
    DV = v.shape[-1]
    L = GLA_CHUNK
    nc = N // L
    f32 = jnp.float32
    qc = q.astype(f32).reshape(B, nc, L, H, DK) * DK ** -0.5
    kc = k.astype(f32).reshape(B, nc, L, H, DK)
    vc = v.astype(f32).reshape(B, nc, L, H, DV)
    b = jnp.cumsum(log_a.astype(f32).reshape(B, nc, L, H, DK), axis=2)
    b_last = b[:, :, -1]
    dS = jnp.einsum('bclhd,bclhe->bchde', kc * jnp.exp(b_last[:, :, None] - b), vc)
    decay = jnp.exp(b_last)[..., None]

    def step(S, inp):
        dec, ds = inp
        return dec * S + ds, S

    s_fin, s_in = lax.scan(step, s0, (jnp.moveaxis(decay, 1, 0), jnp.moveaxis(dS, 1, 0)))
    if not want_out:
        return None, s_fin
    s_in = jnp.moveaxis(s_in, 0, 1)
    q_in = qc * jnp.exp(b)
    k_in = kc * jnp.exp(-b)
    mask = jnp.tril(jnp.ones((L, L), dtype=bool), -1 if strict else 0)
    A = jnp.where(mask, jnp.einsum('bclhd,bcmhd->bchlm', q_in, k_in), 0.0)
    o = (jnp.einsum('bchlm,bcmhe->bclhe', A, vc)
         + jnp.einsum('bclhd,bchde->bclhe', q_in, s_in))
    return o.reshape(B, N, H, DV).astype(v.dtype), s_fin


def gla_bidir(q, k, v, la_f, la_b, s0_f, s0_b, want_out):
    flip = lambda t: jnp.flip(t, axis=1)
    o_f, s_f = gla_chunked(q, k, v, la_f, s0_f, False, want_out)
    o_b, s_b = gla_chunked(flip(q), flip(k), flip(v), flip(la_b), s0_b, True, want_out)
    o = o_f + flip(o_b) if want_out else None
    return o, s_f, s_b


def gla_output(o, r, g_out):
    B, N = o.shape[:2]
    o = rmsnorm(o, g_out.reshape(GLA_HEADS, GLA_DV)) * jax.nn.silu(r).reshape(B, N, GLA_HEADS, GLA_DV)
    return o.reshape(B, N, GLA_HEADS * GLA_DV)


def swiglu(h, w_gu, w_down):
    gu = h @ w_gu
    return (jax.nn.silu(gu[..., :FFN_HIDDEN]) * gu[..., FFN_HIDDEN:]) @ w_down


def setup_inputs(seed: int = 0) -> dict:
    key = jax.random.key(seed)
    ks = jax.random.split(key, 24)
    f32 = jnp.float32
    nrm = lambda k, shape, s: jax.random.normal(k, shape, f32) * s
    gain = lambda k, shape: 1.0 + 0.02 * jax.random.normal(k, shape, f32)
    D, L = D_MODEL, DEPTH
    return {
        'x': nrm(ks[0], (BATCH, SEQ, D), 1.0),
        'c': nrm(ks[1], (BATCH, D), 1.0),
        'ctx': nrm(ks[2], (BATCH, CTX_LEN, D), 1.0),
        'c_ctx': nrm(ks[3], (D,), 1.0),
        'w_mod': nrm(ks[4], (L, D, N_MOD * D), 0.5 * D ** -0.5),
        'b_mod': nrm(ks[5], (L, N_MOD * D), 0.02),
        'g_mix': gain(ks[6], (L, D)),
        'g_ffn': gain(ks[7], (L, D)),
        'w_in': nrm(ks[8], (L, D, IN_WIDTH), D ** -0.5),
        'g_mla_q': gain(ks[9], (L, MLA_Q_RANK)),
        'g_mla_kv': gain(ks[10], (L, MLA_KV_RANK)),
        'w_mla_uq': nrm(ks[11], (L, MLA_Q_RANK, MLA_HEADS * (MLA_NOPE + MLA_ROPE)), MLA_Q_RANK ** -0.5),
        'w_mla_ukv': nrm(ks[12], (L, MLA_KV_RANK, MLA_HEADS * (MLA_NOPE + MLA_V)), MLA_KV_RANK ** -0.5),
        'swa_sink': nrm(ks[13], (L, SWA_HEADS), 0.5),
        'w_gla_gate_f': nrm(ks[14], (L, GLA_GATE_RANK, GLA_HEADS * GLA_DK), GLA_GATE_RANK ** -0.5),
        'b_gla_gate_f': nrm(ks[15], (L, GLA_HEADS * GLA_DK), 0.1),
        'w_gla_gate_b': nrm(ks[16], (L, GLA_GATE_RANK, GLA_HEADS * GLA_DK), GLA_GATE_RANK ** -0.5),
        'b_gla_gate_b': nrm(ks[17], (L, GLA_HEADS * GLA_DK), 0.1),
        'g_gla_out': gain(ks[18], (L, GLA_HEADS * GLA_DV)),
        'w_out': nrm(ks[19], (L, MIX_WIDTH, D), MIX_WIDTH ** -0.5),
        'w_ffn_gu': nrm(ks[20], (L, D, 2 * FFN_HIDDEN), D ** -0.5),
        'w_ffn_down': nrm(ks[21], (L, FFN_HIDDEN, D), FFN_HIDDEN ** -0.5),
        'g_final': gain(ks[22], (D,)),
    }


def reference(x, c, ctx, c_ctx, w_mod, b_mod, g_mix, g_ffn, w_in, g_mla_q, g_mla_kv,
              w_mla_uq, w_mla_ukv, swa_sink, w_gla_gate_f, b_gla_gate_f, w_gla_gate_b,
              b_gla_gate_b, g_gla_out, w_out, w_ffn_gu, w_ffn_down, g_final):
    B, N, D = x.shape
    C = ctx.shape[1]
    rows = N // GRID_W
    row_ids = jnp.repeat(jnp.arange(rows, dtype=jnp.int32), GRID_W)
    col_ids = jnp.tile(jnp.arange(GRID_W, dtype=jnp.int32), rows)
    pos = (row_ids, col_ids)
    split_pts = np.cumsum(IN_SIZES)[:-1].tolist()
    state0 = jnp.zeros((B, GLA_HEADS, GLA_DK, GLA_DV), jnp.float32)
    xc = ctx
    for l in range(DEPTH):
        last = l == DEPTH - 1
        mod_l = (jax.nn.silu(c) @ w_mod[l] + b_mod[l]).reshape(B, 1, N_MOD, D)
        mod_c = (jax.nn.silu(c_ctx) @ w_mod[l] + b_mod[l]).reshape(1, 1, N_MOD, D)

        pl = jnp.split(modulate(x, g_mix[l], mod_l[:, :, 0], mod_l[:, :, 1]) @ w_in[l], split_pts, axis=-1)
        pc = jnp.split(modulate(xc, g_mix[l], mod_c[:, :, 0], mod_c[:, :, 1]) @ w_in[l], split_pts, axis=-1)

        kn_l, kr_l, v_l = mla_kv(pl[1], pl[2], g_mla_kv[l], w_mla_ukv[l], pos)
        kn_c, kr_c, v_c = mla_kv(pc[1], pc[2], g_mla_kv[l], w_mla_ukv[l], None)
        qn_l, qr_l = mla_q(pl[0], g_mla_q[l], w_mla_uq[l], pos)
        mla_l = mla_latent(qn_l, qr_l,
                           jnp.concatenate([kn_l, kn_c], axis=1),
                           jnp.concatenate([kr_l, kr_c], axis=1),
                           jnp.concatenate([v_l, v_c], axis=1))

        q_sl = rope_2d(pl[3].reshape(B, N, SWA_HEADS, SWA_HEAD_DIM), *pos)
        k_sl = rope_2d(pl[4].reshape(B, N, SWA_KV_HEADS, SWA_HEAD_DIM), *pos)
        v_sl = pl[5].reshape(B, N, SWA_KV_HEADS, SWA_HEAD_DIM)
        k_sc = pc[4].reshape(B, C, SWA_KV_HEADS, SWA_HEAD_DIM)
        v_sc = pc[5].reshape(B, C, SWA_KV_HEADS, SWA_HEAD_DIM)
        swa_l = swa_latent(q_sl, k_sl, v_sl, k_sc, v_sc, swa_sink[l])

        gw = (w_gla_gate_f[l], b_gla_gate_f[l], w_gla_gate_b[l], b_gla_gate_b[l])
        q_gc, k_gc, v_gc, af_c, ab_c, r_gc = gla_prepare(pc[6:], *gw)
        o_gc, s_f, s_b = gla_bidir(q_gc, k_gc, v_gc, af_c, ab_c, state0, state0, not last)
        q_gl, k_gl, v_gl, af_l, ab_l, r_gl = gla_prepare(pl[6:], *gw)
        o_gl, _, _ = gla_bidir(q_gl, k_gl, v_gl, af_l, ab_l, s_f, s_b, True)
        gla_l = gla_output(o_gl, r_gl, g_gla_out[l])

        mix_l = jnp.concatenate([mla_l, swa_l, gla_l], axis=-1) @ w_out[l]
        x = x + mod_l[:, :, 2] * mix_l
        h = modulate(x, g_ffn[l], mod_l[:, :, 3], mod_l[:, :, 4])
        x = x + mod_l[:, :, 5] * swiglu(h, w_ffn_gu[l], w_ffn_down[l])

        if not last:
            qn_c, qr_c = mla_q(pc[0], g_mla_q[l], w_mla_uq[l], None)
            mla_c = mla_attend(qn_c, qr_c, kn_c, kr_c, v_c).reshape(B, C, MLA_HEADS * MLA_V)
            swa_c = swa_context(pc[3], k_sc, v_sc, swa_sink[l])
            gla_c = gla_output(o_gc, r_gc, g_gla_out[l])
            mix_c = jnp.concatenate([mla_c, swa_c, gla_c], axis=-1) @ w_out[l]
            xc = xc + mod_c[:, :, 2] * mix_c
            hc = modulate(xc, g_ffn[l], mod_c[:, :, 3], mod_c[:, :, 4])
            xc = xc + mod_c[:, :, 5] * swiglu(hc, w_ffn_gu[l], w_ffn_down[l])
    return rmsnorm(x, g_final)
```

```python
import contextlib
import numpy as np
import concourse.bass as bass
import concourse.mybir as mybir
from concourse.bass_utils import run_bass_kernel_spmd

F32 = mybir.dt.float32
BF16 = mybir.dt.bfloat16
AF = mybir.ActivationFunctionType
ALU = mybir.AluOpType

D = 2048
NLAT = 4096
NCTX = 256
T = NLAT + NCTX
DEPTH = 4
EPS = 1e-6
FFN = 5632
NFC = FFN // 128
BLOCKS = [(i * 512, 512) for i in range(8)] + [(NLAT, NCTX)]

C_CQ, C_CKV, C_KR, C_SQ, C_SK, C_SV, C_GQ, C_GK, C_GV, C_GLR, C_R = (
    0, 512, 1024, 1088, 1856, 2112, 2368, 2624, 2880, 3392, 3424)
IN_W = 3936
OC_CQ, OC_CKV, OC_KR, OC_SQ, OC_SK, OC_GQ, OC_GK, OC_R, OC_GLF, OC_GLB = 0, 4, 8, 9, 15, 17, 19, 21, 25, 26
N_OC = 27


class DSem:
    def __init__(self, handle):
        self.h = handle
        self.cnt = 0
        self.is_dma = True


class Eng:
    def __init__(self, name, e, sem, is_pe=False):
        self.name = name
        self.e = e
        self.h = sem
        self.cnt = 0
        self.is_dma = False
        self.is_pe = is_pe
        self.seen = {}

    def wait(self, toks):
        best = {}
        for t in toks:
            if t is None:
                continue
            s, v = t
            if s.is_dma:
                v = s.cnt
            elif s is self and self.is_pe:
                continue
            if v > best.get(id(s), (None, 0))[1]:
                best[id(s)] = (s, v)
        for s, v in best.values():
            if self.seen.get(id(s), 0) >= v:
                continue
            self.e.wait_ge(s.h, v)
            self.seen[id(s)] = v

    def done(self, ins):
        self.cnt += 1
        ins.then_inc(self.h, 1)
        return (self, self.cnt)


class Buf:
    def __init__(self, t, name="", dram=False):
        self.t = t
        self.name = name
        self.w = []
        self.r = []
        self.dsem = None
        self.dram = dram
        self.psum = False


def _compact(toks):
    best = {}
    for t in toks:
        if t is None:
            continue
        s, v = t
        if id(s) not in best or best[id(s)][1] < v:
            best[id(s)] = (s, v)
    return list(best.values())


class K:
    def __init__(self, nc, es):
        self.nc = nc
        self.es = es
        mk = lambda n: es.enter_context(nc.semaphore(n))
        self.pe = Eng("pe", nc.tensor, mk("m_pe"), is_pe=True)
        self.act = Eng("act", nc.scalar, mk("m_act"))
        self.dve = Eng("dve", nc.vector, mk("m_dve"))
        self.pool = Eng("pool", nc.gpsimd, mk("m_pool"))
        self.sp = Eng("sp", nc.sync, None)
        self.free_dsems = []
        self.n_dsems = 0
        self.freed = []

    def new_dsem(self):
        if self.free_dsems:
            return self.free_dsems.pop()
        self.n_dsems += 1
        return DSem(self.es.enter_context(self.nc.semaphore(f"d{self.n_dsems}")))

    def release(self, bufs):
        for b in bufs:
            toks = list(b.w) + list(b.r)
            for c in getattr(b, "children", []):
                toks += list(c.w) + list(c.r)
            self.freed = _compact(self.freed + toks)
            if b.dsem is not None:
                self.free_dsems.append(b.dsem)
                b.dsem = None

    def dram(self, name, shape, dt, kind="Internal"):
        t = self.nc.dram_tensor(name, list(shape), dt, kind=kind)
        return Buf(t.ap(), name, dram=True)

    def sb(self, st, name, shape, dt):
        self.uid = getattr(self, "uid", 0) + 1
        name = f"{name}_u{self.uid}"
        b = Buf(st.enter_context(self.nc.sbuf_tensor(name, list(shape), dt)), name)
        b.r = list(self.freed)
        st.callback(lambda: self.release([b]))
        return b

    def ps(self, st, name, shape, dt=F32):
        self.uid = getattr(self, "uid", 0) + 1
        name = f"{name}_u{self.uid}"
        b = Buf(st.enter_context(self.nc.psum_tensor(name, list(shape), dt)), name)
        b.psum = True
        b.r = list(self.freed)
        st.callback(lambda: self.release([b]))
        return b

    def op(self, eng, fn, reads=(), writes=()):
        deps = []
        for b in reads:
            deps += b.w
            if b.psum:
                deps += b.r
        for b in writes:
            deps += b.w
            deps += b.r
        eng.wait(deps)
        tok = eng.done(fn(eng.e))
        for b in writes:
            b.w = [tok]
            b.r = []
        for b in reads:
            if b in writes:
                continue
            b.r = _compact(b.r + [tok])
        return tok

    def mm(self, out_buf, out_ap, terms, reads=(), first=True, last=True, transpose=False, start=None, mark=True):
        pe = self.pe
        deps = []
        if first:
            deps += list(out_buf.w) + list(out_buf.r)
        for b in reads:
            deps += b.w
        pe.wait(deps)
        n = len(terms)
        ins = None
        for i, (l, r) in enumerate(terms):
            if transpose:
                ins = pe.e.transpose(out_ap, l, r)
            else:
                st_ = (first and i == 0) if start is None else (start and i == 0)
                ins = pe.e.matmul(out_ap, l, r, start=st_, stop=(last and i == n - 1))
        if not mark:
            if first:
                out_buf.r = []
            return None
        tok = pe.done(ins)
        out_buf.w = [tok]
        if first:
            out_buf.r = []
        for b in reads:
            b.r = _compact(b.r + [tok])
        return tok

    def dma(self, out_buf, out_ap, in_buf, in_ap, q=None, more=False):
        q = q or self.sp
        in_bufs = in_buf if isinstance(in_buf, (list, tuple)) else [in_buf]
        deps = []
        for ib in in_bufs:
            deps += list(ib.w)
        if not out_buf.dram and not out_buf.w:
            more = False
        if not out_buf.dram and not more:
            deps += list(out_buf.w) + list(out_buf.r)
        q.wait(deps)
        if out_buf.dsem is None:
            out_buf.dsem = self.new_dsem()
        s = out_buf.dsem
        s.cnt += 16
        q.e.dma_start(out=out_ap, in_=in_ap).then_inc(s.h, 16)
        tok = (s, s.cnt)
        out_buf.w = [tok]
        if not out_buf.dram and not more:
            out_buf.r = []
        for ib in in_bufs:
            if not ib.dram:
                ib.r = _compact(ib.r + [tok])
        return tok

    def views(self, buf, n):
        vs = [Buf(buf.t, f"{buf.name}.{i}") for i in range(n)]
        for v in vs:
            v.r = list(buf.r)
            v.w = list(buf.w)
            v.psum = buf.psum
        buf.children = getattr(buf, "children", []) + vs
        return vs


def _rope_tables():
    t = np.arange(NLAT)
    rows = (t // 64).astype(np.float32)
    cols = (t % 64).astype(np.float32)

    def tab(dh):
        half = dh // 2
        freqs = (10000.0 ** (-np.arange(half, dtype=np.float32) / half)).astype(np.float32)
        cos = np.ones((2 * dh, T), np.float32)
        sin = np.zeros((2 * dh, T), np.float32)
        for a, pos in enumerate((rows, cols)):
            ang = (pos[None, :] * freqs[:, None]).astype(np.float32)
            c, s = np.cos(ang).astype(np.float32), np.sin(ang).astype(np.float32)
            base = a * dh
            cos[base:base + half, :NLAT] = c
            cos[base + half:base + dh, :NLAT] = c
            sin[base:base + half, :NLAT] = -s
            sin[base + half:base + dh, :NLAT] = s
        return cos, sin

    cm, sm = tab(32)
    cs, ss = tab(64)
    cm = np.concatenate([cm, cm], 0)
    sm = np.concatenate([sm, sm], 0)
    return np.ascontiguousarray(cm), np.ascontiguousarray(sm), cs, ss


def _perm(dh, n=128):
    half = dh // 2
    p = np.zeros((n, n), np.float32)
    for m in range(n):
        g, o = divmod(m, dh)
        k = g * dh + (o + half) % dh
        p[k, m] = 1.0
    return p


def _swa_masks():
    m = np.zeros((6, 128, 512), np.float32)
    for r in range(6):
        kpos = (r - 1) * 128 + np.arange(128)[:, None]
        qpos = np.arange(512)[None, :]
        ok = np.abs(qpos - kpos) <= 128
        m[r] = np.where(ok, 0.0, -30000.0)
    return m


def _gla_masks():
    tp = np.arange(128)[:, None]
    t = np.arange(128)[None, :]
    same = (tp // 64) == (t // 64)
    m = np.zeros((2, 128, 128), np.float32)
    m[0] = (same & (tp <= t)).astype(np.float32)
    m[1] = (same & (tp > t)).astype(np.float32)
    return m


def host_consts():
    cm, sm, cs, ss = _rope_tables()
    return {
        "k_ident": np.eye(128, dtype=np.float32),
        "k_permm": _perm(32),
        "k_perms": _perm(64),
        "k_cosm": cm, "k_sinm": sm, "k_coss": cs, "k_sins": ss,
        "k_mask": _swa_masks(),
        "k_gmask": _gla_masks(),
    }


W_SHAPES = {
    "w_mod": (DEPTH, D, 6 * D), "b_mod": (DEPTH, 6 * D), "g_mix": (DEPTH, D), "g_ffn": (DEPTH, D),
    "w_in": (DEPTH, D, IN_W), "g_mla_q": (DEPTH, 512), "g_mla_kv": (DEPTH, 512),
    "w_mla_uq": (DEPTH, 512, 1152), "w_mla_ukv": (DEPTH, 512, 1536), "swa_sink": (DEPTH, 6),
    "w_gla_gate_f": (DEPTH, 16, 256), "b_gla_gate_f": (DEPTH, 256),
    "w_gla_gate_b": (DEPTH, 16, 256), "b_gla_gate_b": (DEPTH, 256), "g_gla_out": (DEPTH, 512),
    "w_out": (DEPTH, D, D), "w_ffn_gu": (DEPTH, D, 2 * FFN), "w_ffn_down": (DEPTH, FFN, D),
    "g_final": (D,),
}


class Prog:
    def __init__(self, nc, es, n_layers=DEPTH, dbg=(), stop_after=None):
        self.nc = nc
        self.es = es
        self.k = K(nc, es)
        self.L = n_layers
        self.dbg = set(dbg)
        self.stop_after = stop_after
        import os
        self.p1_stage = int(os.environ.get("P1_STAGE", "0"))
        k = self.k
        ein = lambda n, s: k.dram(n, s, F32, kind="ExternalInput")
        self.x = ein("x", (NLAT, D))
        self.c = ein("c", (16, 128))
        self.ctx = ein("ctx", (NCTX, D))
        self.c_ctx = ein("c_ctx", (16, 128))
        self.W = {n: ein(n, s) for n, s in W_SHAPES.items()}
        self.C = {n: ein(n, v.shape) for n, v in host_consts().items()}
        self.out = k.dram("out", (NLAT, D), F32, kind="ExternalOutput")
        L = self.L

        def scr(n, s, dt):
            return k.dram(n, s, dt, kind=("ExternalOutput" if n in self.dbg else "Internal"))

        self.XT = scr("XT", (16, 128, T), F32)
        self.QN = scr("QN", (6, 128, T), BF16)
        self.QR = scr("QR", (3, 128, T), BF16)
        self.KN = scr("KN", (6, 128, T), BF16)
        self.KR = scr("KR", (128, T), BF16)
        self.VM = scr("VM", (T, 768), BF16)
        self.SQ = scr("SQ", (6, 128, T), BF16)
        self.SK = scr("SK", (2, 128, T), BF16)
        self.SV = scr("SV", (T, 256), BF16)
        self.GQ = scr("GQ", (2, 128, T), F32)
        self.GK = scr("GK", (2, 128, T), F32)
        self.GV = scr("GV", (T, 512), BF16)
        self.GLF = scr("GLF", (2, 128, T), F32)
        self.GLB = scr("GLB", (2, 128, T), F32)
        self.GR = scr("GR", (4, 128, T), F32)
        self.MIX = scr("MIX", (16, 128, T), BF16)
        self.WINFM = [scr(f"WINFM{l}", (N_OC, 128, 16, 128), BF16) for l in range(L)]
        self.WINTM = [scr(f"WINTM{l}", (128, 16, 768), BF16) for l in range(L)]
        self.WUQ = [scr(f"WUQ{l}", (9, 128, 4, 128), BF16) for l in range(L)]
        self.WUKVFM = [scr(f"WUKVFM{l}", (6, 128, 4, 128), BF16) for l in range(L)]
        self.WUKVTM = [scr(f"WUKVTM{l}", (128, 4, 768), BF16) for l in range(L)]
        self.WOUT = [scr(f"WOUT{l}", (16, 128, 16, 128), BF16) for l in range(L)]
        self.WGU = [scr(f"WGU{l}", (88, 128, 16, 128), BF16) for l in range(L)]
        self.WDN = [scr(f"WDN{l}", (16, 128, NFC, 128), BF16) for l in range(L)]
        for l in range(L):
            sh = k.new_dsem()
            for b in (self.WINFM[l], self.WINTM[l], self.WUQ[l], self.WUKVFM[l], self.WUKVTM[l],
                      self.WOUT[l], self.WGU[l], self.WDN[l]):
                b.dsem = sh

    def consts(self, st):
        k = self.k
        self.ident_f = k.sb(st, "ident_f", (128, 128), F32)
        self.ident_b = k.sb(st, "ident_b", (128, 128), BF16)
        self.ones_b = k.sb(st, "ones_b", (128, 128), BF16)
        self.ones_f = k.sb(st, "ones_f", (128, 128), F32)
        self.permm_b = k.sb(st, "permm_b", (128, 128), BF16)
        self.perms_b = k.sb(st, "perms_b", (128, 128), BF16)
        self.mask_b = k.sb(st, "mask_b", (128, 6, 512), BF16)
        with contextlib.ExitStack() as tmp:
            pf = k.sb(tmp, "c_pf", (128, 2, 128), F32)
            mf = k.sb(tmp, "c_mf", (128, 6, 512), F32)
            k.dma(self.ident_f, self.ident_f.t[:], self.C["k_ident"], self.C["k_ident"].t[:, :])
            k.dma(pf, pf.t[:, 0, :], self.C["k_permm"], self.C["k_permm"].t[:, :])
            k.dma(pf, pf.t[:, 1, :], self.C["k_perms"], self.C["k_perms"].t[:, :], more=True)
            k.dma(mf, mf.t[:], self.C["k_mask"], self.C["k_mask"].t.rearrange("r k q -> k r q"))
            k.op(k.dve, lambda e: e.tensor_copy(self.ident_b.t[:], self.ident_f.t[:]),
                 reads=[self.ident_f], writes=[self.ident_b])
            k.op(k.dve, lambda e: e.tensor_copy(self.permm_b.t[:], pf.t[:, 0, :]), reads=[pf], writes=[self.permm_b])
            k.op(k.dve, lambda e: e.tensor_copy(self.perms_b.t[:], pf.t[:, 1, :]), reads=[pf], writes=[self.perms_b])
            k.op(k.dve, lambda e: e.tensor_copy(self.mask_b.t[:], mf.t[:]), reads=[mf], writes=[self.mask_b])
            k.op(k.dve, lambda e: e.memset(self.ones_b.t[:], 1.0), writes=[self.ones_b])
            k.op(k.dve, lambda e: e.memset(self.ones_f.t[:], 1.0), writes=[self.ones_f])
        L = self.L
        self.vecT = k.sb(st, "vecT", (128, 2, 128), F32)
        self.sinkb = k.sb(st, "sinkb", (128, 4 * 6), F32)
        self.eps_c = k.sb(st, "eps_c", (128, 1), F32)
        self.one_c = k.sb(st, "one_c", (128, 1), F32)
        self.negb = [k.sb(st, f"negb{l}", (128, 4), F32) for l in range(L)]
        W = self.W
        with contextlib.ExitStack() as tmp:
            vr = [k.sb(tmp, f"vrows{i}", (128, 128), F32) for i in range(2)]
            self.vcol = {}
            items = []
            for l in range(L):
                for nm, nr in (("g_mix", 16), ("g_ffn", 16), ("g_mla_q", 4), ("g_mla_kv", 4),
                               ("g_gla_out", 4), ("b_gla_gate_f", 2), ("b_gla_gate_b", 2)):
                    items.append((nm, l, nr))
            items.append(("g_final", None, 16))
            row = 0
            for nm, l, nr in items:
                g, r0 = divmod(row, 128)
                if r0 + nr > 128:
                    row = (g + 1) * 128
                    g, r0 = divmod(row, 128)
                src = W[nm].t[l] if l is not None else W[nm].t
                k.dma(vr[g], vr[g].t[r0:r0 + nr, :], W[nm], src.rearrange("(r c) -> r c", c=128), more=True)
                self.vcol[(nm, l)] = (g, r0)
                row += nr
            assert row <= 256
            with contextlib.ExitStack() as pst:
                pp = k.ps(pst, "c_pp", (128, 512))
                for g in range(2):
                    k.mm(pp, pp.t[:, g * 128:(g + 1) * 128], [(vr[g].t[:], self.ident_f.t[:])],
                         reads=[vr[g], self.ident_f], transpose=True)
                k.op(k.dve, lambda e: e.tensor_copy(self.vecT.t[:].rearrange("p g c -> p (g c)"), pp.t[:, 0:256]),
                     reads=[pp], writes=[self.vecT])
            k.op(k.dve, lambda e: e.memset(self.eps_c.t[:], EPS), writes=[self.eps_c])
            k.op(k.dve, lambda e: e.memset(self.one_c.t[:], 1.0), writes=[self.one_c])
            for l in range(L):
                for d_, nm in enumerate(("b_gla_gate_f", "b_gla_gate_b")):
                    g, r0 = self.vcol[(nm, l)]
                    k.op(k.dve, lambda e, l=l, d_=d_, g=g, r0=r0: e.tensor_scalar(
                        self.negb[l].t[:, d_ * 2:d_ * 2 + 2], self.vecT.t[:, g, r0:r0 + 2], -1.0, None, ALU.mult),
                         reads=[self.vecT], writes=[self.negb[l]])
            k.dma(self.sinkb, self.sinkb.t[:], W["swa_sink"],
                  W["swa_sink"].t.rearrange("l h -> (l h)").partition_broadcast(128))
            k.op(k.act, lambda e: e.activation(self.sinkb.t[:], self.sinkb.t[:], AF.Exp),
                 reads=[self.sinkb], writes=[self.sinkb])

    def vec(self, nm, l, j):
        g, r0 = self.vcol[(nm, l)]
        return self.vecT.t[:, g, r0 + j:r0 + j + 1]

    def p0_transpose_in(self):
        k = self.k
        with contextlib.ExitStack() as st:
            xin = [k.sb(st, f"p0_xin{i}", (128, 4, D), F32) for i in range(2)]
            xo = [k.sb(st, f"p0_xo{i}", (128, 16, 512), F32) for i in range(2)]
            pp = [k.ps(st, f"p0_pp{i}", (128, 512)) for i in range(4)]
            xovs = [k.views(b, 16) for b in xo]
            cnt = 0
            for bi, (t0, n) in enumerate(BLOCKS):
                xi = xin[bi % 2]
                xob = xo[bi % 2]
                xov = xovs[bi % 2]
                nt = n // 128
                if t0 < NLAT:
                    src, sb_ = self.x, self.x.t[t0:t0 + n, :]
                else:
                    src, sb_ = self.ctx, self.ctx.t[:, :]
                k.dma(xi, xi.t[:, 0:nt, :], src, sb_.rearrange("(a p) d -> p a d", p=128))
                for kc in range(16):
                    p = pp[cnt % 4]
                    for ti in range(nt):
                        k.mm(p, p.t[:, ti * 128:(ti + 1) * 128],
                             [(xi.t[:, ti, kc * 128:(kc + 1) * 128], self.ident_f.t[:])],
                             reads=[xi, self.ident_f], transpose=True, first=(ti == 0))
                    eng = k.dve if cnt % 2 == 0 else k.act
                    if eng is k.dve:
                        k.op(eng, lambda e, p=p, kc=kc: e.tensor_copy(xob.t[:, kc, :n], p.t[:, :n]), reads=[p], writes=[xov[kc]])
                    else:
                        k.op(eng, lambda e, p=p, kc=kc: e.copy(xob.t[:, kc, :n], p.t[:, :n]), reads=[p], writes=[xov[kc]])
                    cnt += 1
                k.dma(self.XT, self.XT.t[:, :, t0:t0 + n].rearrange("kc p t -> p kc t"), xov, xob.t[:, :, :n])

    def p0_mod(self, st):
        k = self.k
        L = self.L
        self.modT = [k.sb(st, f"modT{l}", (128, 96, 2), F32) for l in range(L)]
        self.Amix = [k.sb(st, f"Amix{l}", (128, 16, 2), F32) for l in range(L)]
        self.Affn = [k.sb(st, f"Affn{l}", (128, 16, 2), F32) for l in range(L)]
        with contextlib.ExitStack() as tmp:
            crow = k.sb(tmp, "m_crow", (32, 128), F32)
            sc = k.sb(tmp, "m_sc", (128, 16, 2), F32)
            wm = [k.sb(tmp, f"m_wm{i}", (128, 4096), F32) for i in range(3)]
            bm = k.sb(tmp, "m_bm", (2, 4096), F32)
            mrow = k.sb(tmp, "m_mrow", (2, 4096), F32)
            pp = [k.ps(tmp, f"m_pp{i}", (128, 512)) for i in range(8)]
            k.dma(crow, crow.t[0:16, :], self.c, self.c.t[:, :])
            k.dma(crow, crow.t[16:32, :], self.c_ctx, self.c_ctx.t[:, :], more=True)
            k.mm(pp[0], pp[0].t[:, 0:32], [(crow.t[:], self.ident_f.t[0:32, 0:32])], reads=[crow, self.ident_f], transpose=True)
            k.op(k.act, lambda e: e.activation(sc.t[:, :, 0], pp[0].t[:, 0:16], AF.Silu), reads=[pp[0]], writes=[sc])
            k.op(k.act, lambda e: e.activation(sc.t[:, :, 1], pp[0].t[:, 16:32], AF.Silu), reads=[pp[0]], writes=[sc])
            cnt = 0
            for l in range(L):
                for cg in range(3):
                    c0 = cg * 4096
                    k.dma(bm, bm.t[0:1, :], self.W["b_mod"], self.W["b_mod"].t[l:l + 1, c0:c0 + 4096])
                    k.dma(bm, bm.t[1:2, :], self.W["b_mod"], self.W["b_mod"].t[l:l + 1, c0:c0 + 4096], more=True)
                    for kc in range(16):
                        w = wm[cnt % 3]
                        cnt += 1
                        k.dma(w, w.t[:], self.W["w_mod"], self.W["w_mod"].t[l, kc * 128:(kc + 1) * 128, c0:c0 + 4096])
                        for j in range(8):
                            k.mm(pp[j], pp[j].t[0:2, :], [(sc.t[:, kc, :], w.t[:, j * 512:(j + 1) * 512])],
                                 reads=[sc, w], first=(kc == 0), last=(kc == 15))
                    for j in range(8):
                        k.op(k.dve, lambda e, j=j: e.tensor_tensor(mrow.t[0:2, j * 512:(j + 1) * 512], pp[j].t[0:2, :],
                                                                   bm.t[0:2, j * 512:(j + 1) * 512], ALU.add),
                             reads=[pp[j], bm], writes=[mrow])
                    for jj in range(32):
                        k.mm(pp[0], pp[0].t[:, jj * 2:jj * 2 + 2],
                             [(mrow.t[0:2, jj * 128:(jj + 1) * 128], self.ident_f.t[0:2, 0:2])],
                             reads=[mrow, self.ident_f], transpose=True, first=(jj == 0))
                    k.op(k.dve, lambda e, l=l, cg=cg: e.tensor_copy(
                        self.modT[l].t[:, cg * 32:(cg + 1) * 32, :].rearrange("p a b -> p (a b)"), pp[0].t[:, 0:64]),
                         reads=[pp[0]], writes=[self.modT[l]])
                for (A, gname, off) in ((self.Amix[l], "g_mix", 16), (self.Affn[l], "g_ffn", 64)):
                    g, r0 = self.vcol[(gname, l)]
                    for b in range(2):
                        k.op(k.dve, lambda e, A=A, b=b, off=off, g=g, r0=r0, l=l: e.scalar_tensor_tensor(
                            A.t[:, :, b], self.modT[l].t[:, off:off + 16, b], 1.0, self.vecT.t[:, g, r0:r0 + 16],
                            ALU.add, ALU.mult), reads=[self.modT[l], self.vecT], writes=[A])

    def conv_flush(self):
        for th in self.cv_pending:
            th()
        self.cv_pending = []

    def conv_pieces(self, l, part):
        k = self.k
        W = self.W
        FMv = lambda buf, a, b, kc: buf.t[a:b, :, kc, :].rearrange("oc p m -> p oc m")

        def piece(src_buf, src_ap, ncols, stores):
            if self.cv_throttle:
                k.pool.wait([(k.pe, k.pe.cnt)])
            for (c0, c1, dbuf, dap, three) in stores:
                s_ap = src_ap[:, c0:c1]
                if three:
                    s_ap = s_ap.rearrange("p (oc m) -> p oc m", m=128)
                k.dma(dbuf, dap, src_buf, s_ap, q=k.pool)

        if part == 0:
            yield from self._conv_early(l, piece, FMv)
        else:
            yield from self._conv_late(l, piece, FMv)
        self.conv_flush()

    def _conv_early(self, l, piece, FMv):
        W = self.W

        win = W["w_in"]
        FM, TM = self.WINFM[l], self.WINTM[l]
        for kc in range(16):
            rows = slice(kc * 128, (kc + 1) * 128)
            piece(win, win.t[l, rows, 0:1024], 1024, [(0, 1024, FM, FMv(FM, 0, 8, kc), True)])
            yield
            piece(win, win.t[l, rows, 1024:2112], 1088, [
                (0, 64, FM, FM.t[OC_KR, :, kc, 0:64], False),
                (0, 64, FM, FM.t[OC_KR, :, kc, 64:128], False),
                (64, 1088, FM, FMv(FM, OC_SQ, OC_SQ + 8, kc), True)])
            yield
            piece(win, win.t[l, rows, 2112:2880], 768, [
                (0, 256, TM, TM.t[:, kc, 0:256], False),
                (256, 768, FM, FMv(FM, OC_GQ, OC_GQ + 4, kc), True)])
            yield
            piece(win, win.t[l, rows, 2880:3936], 1056, [
                (0, 512, TM, TM.t[:, kc, 256:768], False),
                (512, 528, FM, FM.t[OC_GLF, :, kc, 0:16], False),
                (528, 544, FM, FM.t[OC_GLB, :, kc, 0:16], False),
                (544, 1056, FM, FMv(FM, OC_R, OC_R + 4, kc), True)])
            yield
        wq = W["w_mla_uq"]
        for kc in range(4):
            rows = slice(kc * 128, (kc + 1) * 128)
            stores = []
            for h in range(6):
                stores.append((h * 192, h * 192 + 128, self.WUQ[l], self.WUQ[l].t[h, :, kc, :], False))
                stores.append((h * 192 + 128, h * 192 + 192, self.WUQ[l],
                               self.WUQ[l].t[6 + h // 2, :, kc, (h % 2) * 64:(h % 2) * 64 + 64], False))
            piece(wq, wq.t[l, rows, :], 1152, stores)
            yield
        wkv = W["w_mla_ukv"]
        for kc in range(4):
            rows = slice(kc * 128, (kc + 1) * 128)
            for half in range(2):
                stores = []
                for hh in range(3):
                    h = half * 3 + hh
                    stores.append((hh * 256, hh * 256 + 128, self.WUKVFM[l], self.WUKVFM[l].t[h, :, kc, :], False))
                    stores.append((hh * 256 + 128, hh * 256 + 256, self.WUKVTM[l],
                                   self.WUKVTM[l].t[:, kc, h * 128:(h + 1) * 128], False))
                piece(wkv, wkv.t[l, rows, half * 768:(half + 1) * 768], 768, stores)
                yield

    def _conv_late(self, l, piece, FMv):
        W = self.W
        wo = W["w_out"]
        for kc in range(16):
            rows = slice(kc * 128, (kc + 1) * 128)
            for j in range(2):
                piece(wo, wo.t[l, rows, j * 1024:(j + 1) * 1024], 1024,
                      [(0, 1024, self.WOUT[l], FMv(self.WOUT[l], j * 8, j * 8 + 8, kc), True)])
                yield
        wg = W["w_ffn_gu"]
        for kc in range(16):
            rows = slice(kc * 128, (kc + 1) * 128)
            for j in range(11):
                piece(wg, wg.t[l, rows, j * 1024:(j + 1) * 1024], 1024,
                      [(0, 1024, self.WGU[l], FMv(self.WGU[l], j * 8, j * 8 + 8, kc), True)])
                yield
        wd = W["w_ffn_down"]
        for kc in range(NFC):
            rows = slice(kc * 128, (kc + 1) * 128)
            for j in range(2):
                piece(wd, wd.t[l, rows, j * 1024:(j + 1) * 1024], 1024,
                      [(0, 1024, self.WDN[l], FMv(self.WDN[l], j * 8, j * 8 + 8, kc), True)])
                yield

    def conv_setup(self, st):
        k = self.k
        self.cv_i = 0
        self.cv_throttle = False
        self.cv_pending = []
        self.cv_marks = set()
        self.cv_gen = self.conv_all()

    def conv_all(self):
        for l in range(self.L):
            for part in range(2):
                yield from self.conv_pieces(l, part)
                self.cv_marks.add((l, part))

    def conv_until(self, l, part):
        while (l, part) not in self.cv_marks and self.cv_gen is not None:
            self.conv_step(1)

    def conv_step(self, n):
        if self.cv_gen is None:
            return
        for _ in range(n):
            try:
                next(self.cv_gen)
            except StopIteration:
                self.cv_gen = None
                self.conv_flush()
                return

    def conv_finish(self):
        self.conv_step(10 ** 9)

    def rms_rstd(self, src_tile, src_bufs, nchunks, n, dim, sqb, pS, rs, cnt0=0):
        k = self.k
        for j in range(nchunks):
            sq = sqb[(cnt0 + j) % len(sqb)]
            k.op(k.act, lambda e, j=j, sq=sq: e.activation(sq.t[:, :n], src_tile.t[:, j, :n], AF.Square),
                 reads=[src_bufs[j]], writes=[sq])
            k.mm(pS, pS.t[:, :n], [(self.ones_b.t[:], sq.t[:, :n])], reads=[self.ones_b, sq],
                 first=(j == 0), last=(j == nchunks - 1))
        k.op(k.act, lambda e: e.activation(rs.t[:, :n], pS.t[:, :n], AF.Sqrt, bias=self.eps_c.t[:, 0:1], scale=1.0 / dim),
             reads=[pS, self.eps_c], writes=[rs])
        k.op(k.dve, lambda e: e.reciprocal(rs.t[:, :n], rs.t[:, :n]), reads=[rs], writes=[rs])

    def p1(self, l):
        k = self.k
        W = self.W
        L = self.L
        with contextlib.ExitStack() as st:
            xb = k.sb(st, "p1_xb", (128, 16, 512), F32)
            xbv = k.views(xb, 16)
            hT = [k.sb(st, f"p1_hT{i}", (128, 16, 512), BF16) for i in range(2)]
            hTv = [k.views(b, 16) for b in hT]
            sqb = [k.sb(st, f"p1_sq{i}", (128, 512), BF16) for i in range(4)]
            tmpf = [k.sb(st, f"p1_tmp{i}", (128, 512), F32) for i in range(3)]
            rs = k.sb(st, "p1_rs", (128, 512), F32)
            rss = [k.sb(st, f"p1_rs2_{i}", (128, 512), F32) for i in range(2)]
            wfm = [k.sb(st, f"p1_wfm{i}", (128, 16, 128), BF16) for i in range(3)]
            wtm = k.sb(st, "p1_wtm", (128, 16, 768), BF16)
            wuq = k.sb(st, "p1_wuq", (128, 9, 4, 128), BF16)
            wkf = k.sb(st, "p1_wkf", (128, 6, 4, 128), BF16)
            wkt = k.sb(st, "p1_wkt", (128, 4, 768), BF16)
            wgf = k.sb(st, "p1_wgf", (16, 2, 256), F32)
            cqs = [k.sb(st, f"p1_cq{i}", (128, 4, 512), F32) for i in range(2)]
            cqvs = [k.views(b_, 4) for b_ in cqs]
            cqns = [k.sb(st, f"p1_cqn{i}", (128, 4, 512), BF16) for i in range(2)]
            tab = k.sb(st, "p1_tab", (128, 4, 512), F32)
            sbf = [k.sb(st, f"p1_sbf{i}", (128, 512), BF16) for i in range(4)]
            sff = [k.sb(st, f"p1_sff{i}", (128, 512), F32) for i in range(4)]
            stm = [k.sb(st, f"p1_stm{i}", (128, 768), BF16) for i in range(2)]
            glr = k.sb(st, "p1_glr", (16, 2, 512), F32)
            pA = [k.ps(st, f"p1_pA{i}", (128, 512)) for i in range(3)]
            pS = k.ps(st, "p1_pS", (128, 512))
            pW = k.ps(st, "p1_pW", (128, 512))
            pT = [k.ps(st, f"p1_pT{i}", (128, 512)) for i in range(2)]
            pG = k.ps(st, "p1_pG", (128, 512))

            k.dma(wtm, wtm.t[:], self.WINTM[l], self.WINTM[l].t[:, :, :])
            k.dma(wuq, wuq.t[:], self.WUQ[l], self.WUQ[l].t.rearrange("oc p kc m -> p oc kc m"))
            k.dma(wkf, wkf.t[:], self.WUKVFM[l], self.WUKVFM[l].t.rearrange("oc p kc m -> p oc kc m"))
            k.dma(wkt, wkt.t[:], self.WUKVTM[l], self.WUKVTM[l].t[:, :, :])
            k.dma(wgf, wgf.t[:, 0, :], W["w_gla_gate_f"], W["w_gla_gate_f"].t[l])
            k.dma(wgf, wgf.t[:, 1, :], W["w_gla_gate_b"], W["w_gla_gate_b"].t[l], more=True)

            cnt = {"a": 0, "w": 0, "s": 0, "f": 0, "e": 0}

            def nxt(key, lst):
                key = id(lst)
                cnt[key] = cnt.get(key, 0) + 1
                return lst[(cnt[key] - 1) % len(lst)]

            def evac_copy(p, n, dst_buf, dst_ap, func=None):
                cnt["e"] += 1
                if func is not None or cnt["e"] % 2 == 0:
                    f = func if func is not None else AF.Copy
                    k.op(k.act, lambda e: e.activation(dst_ap, p.t[:, :n], f), reads=[p], writes=[dst_buf])
                else:
                    k.op(k.dve, lambda e: e.tensor_copy(dst_ap, p.t[:, :n]), reads=[p], writes=[dst_buf])

            pend = []
            pcount = {"n": 0}

            def store(dst, dst_ap, sbuf, s_ap):
                pend.append((pcount["n"], sbuf, lambda: k.dma(dst, dst_ap, sbuf, s_ap)))

            def flush(upto_tag=None, buf=None):
                while pend:
                    tag, sb_, th = pend[0]
                    need = (upto_tag is not None and tag <= upto_tag) or \
                           (buf is not None and any(e[1] is buf for e in pend))
                    if not need:
                        break
                    pend.pop(0)
                    th()

            def stage(lst):
                b = nxt("x", lst)
                flush(buf=b)
                return b

            ld = {"n": 0, "use": 0}
            total_loads = len(BLOCKS) * N_OC

            def ensure(upto):
                while ld["n"] <= min(upto, total_loads - 1):
                    g = ld["n"]
                    w_ = wfm[g % 3]
                    k.dma(w_, w_.t[:], self.WINFM[l], self.WINFM[l].t[g % N_OC])
                    ld["n"] += 1

            tails = []

            def run_tails():
                while tails:
                    tails.pop(0)()

            def rope(p, n, ti, perm, dst, dst_ap):
                xr = stage(sbf)
                k.op(k.act, lambda e: e.activation(xr.t[:, :n], p.t[:, :n], AF.Copy), reads=[p], writes=[xr])
                run_tails()

                def tail():
                    k.mm(pW, pW.t[:, :n], [(perm.t[:], xr.t[:, :n])], reads=[perm, xr])
                    t1 = stage(sff)
                    k.op(k.dve, lambda e: e.tensor_tensor(t1.t[:, :n], p.t[:, :n], tab.t[:, ti, :n], ALU.mult),
                         reads=[p, tab], writes=[t1])
                    t2 = stage(sff)
                    k.op(k.dve, lambda e: e.tensor_tensor(t2.t[:, :n], pW.t[:, :n], tab.t[:, ti + 1, :n], ALU.mult),
                         reads=[pW, tab], writes=[t2])
                    ob = stage(sbf)
                    k.op(k.dve, lambda e: e.tensor_tensor(ob.t[:, :n], t1.t[:, :n], t2.t[:, :n], ALU.add),
                         reads=[t1, t2], writes=[ob])
                    store(dst, dst_ap, ob, ob.t[:, :n])
                tails.append(tail)

            def prologue(bi):
                t0, n = BLOCKS[bi]
                isc = 1 if t0 >= NLAT else 0
                h = hT[bi % 2]
                hv = hTv[bi % 2]
                k.dma(xb, xb.t[:, :, :n], self.XT, self.XT.t[:, :, t0:t0 + n].rearrange("kc p t -> p kc t"))
                for b_ in xbv:
                    b_.w = list(xb.w)
                self.rms_rstd(xb, xbv, 16, n, D, sqb, pS, rs)
                for kc in range(16):
                    tm_ = nxt("f", tmpf)
                    k.op(k.dve, lambda e, kc=kc, tm_=tm_: e.scalar_tensor_tensor(
                        tm_.t[:, :n], xb.t[:, kc, :n], self.Amix[l].t[:, kc, isc:isc + 1], rs.t[:, :n], ALU.mult, ALU.mult),
                         reads=[xbv[kc], rs, self.Amix[l]], writes=[tm_])
                    sh_ap = self.modT[l].t[:, kc, isc:isc + 1]
                    k.op(k.act, lambda e, kc=kc, tm_=tm_, sh_ap=sh_ap: e.activation(
                        h.t[:, kc, :n], tm_.t[:, :n], AF.Identity, bias=sh_ap), reads=[tm_, self.modT[l]], writes=[hv[kc]])
                for b_ in xbv:
                    xb.r = _compact(xb.r + b_.r)
                    b_.r = []

            prologue(0)
            for bi, (t0, n) in enumerate(BLOCKS):
                isc = 1 if t0 >= NLAT else 0
                tsl = slice(t0, t0 + n)
                nt = n // 128
                h = hT[bi % 2]
                hv = hTv[bi % 2]
                for i, nm in enumerate(("k_cosm", "k_sinm", "k_coss", "k_sins")):
                    k.dma(tab, tab.t[:, i, :n], self.C[nm], self.C[nm].t[:, tsl], more=(i > 0))

                def proj(oc, m=128):
                    g = ld["use"]
                    ld["use"] += 1
                    assert g % N_OC == oc
                    ensure(g + 2)
                    pcount["n"] += 1
                    flush(upto_tag=pcount["n"] - 2)
                    w = wfm[g % 3]
                    p = nxt("a", pA)
                    k.mm(p, p.t[0:m, :n], [(w.t[:, kc, 0:m], h.t[:, kc, :n]) for kc in range(16)], reads=[w] + hv)
                    run_tails()
                    return p

                for which in range(2):
                    oc0 = OC_CQ if which == 0 else OC_CKV
                    for j in range(4):
                        p = proj(oc0 + j)
                        evac_copy(p, n, cqvs[which][j], cqs[which].t[:, j, :n])
                for which in range(2):
                    self.rms_rstd(cqs[which], cqvs[which], 4, n, 512, sqb, pS, rss[which])
                for which in range(2):
                    gname = "g_mla_q" if which == 0 else "g_mla_kv"
                    cq, cqv, cqn, rs2 = cqs[which], cqvs[which], cqns[which], rss[which]
                    for j in range(4):
                        k.op(k.dve, lambda e, j=j: e.scalar_tensor_tensor(
                            cqn.t[:, j, :n], cq.t[:, j, :n], self.vec(gname, l, j), rs2.t[:, :n], ALU.mult, ALU.mult),
                             reads=[cqv[j], rs2, self.vecT], writes=[cqn])
                    if which == 0:
                        for hh in range(6):
                            p = nxt("a", pA)
                            k.mm(p, p.t[:, :n], [(wuq.t[:, hh, kc, :], cqn.t[:, kc, :n]) for kc in range(4)], reads=[wuq, cqn])
                            ob = stage(sbf)
                            evac_copy(p, n, ob, ob.t[:, :n])
                            store(self.QN, self.QN.t[hh, :, tsl], ob, ob.t[:, :n])
                        for pr in range(3):
                            p = nxt("a", pA)
                            k.mm(p, p.t[:, :n], [(wuq.t[:, 6 + pr, kc, :], cqn.t[:, kc, :n]) for kc in range(4)], reads=[wuq, cqn])
                            rope(p, n, 0, self.permm_b, self.QR, self.QR.t[pr, :, tsl])
                    else:
                        for hh in range(6):
                            p = nxt("a", pA)
                            k.mm(p, p.t[:, :n], [(wkf.t[:, hh, kc, :], cqn.t[:, kc, :n]) for kc in range(4)], reads=[wkf, cqn])
                            ob = stage(sbf)
                            evac_copy(p, n, ob, ob.t[:, :n])
                            store(self.KN, self.KN.t[hh, :, tsl], ob, ob.t[:, :n])
                        for ti in range(nt):
                            tt = slice(ti * 128, (ti + 1) * 128)
                            k.mm(pT[0], pT[0].t[:, :], [(cqn.t[:, kc, tt], wkt.t[:, kc, 0:512]) for kc in range(4)], reads=[wkt, cqn])
                            k.mm(pT[1], pT[1].t[:, 0:256], [(cqn.t[:, kc, tt], wkt.t[:, kc, 512:768]) for kc in range(4)], reads=[wkt, cqn])
                            so = stage(stm)
                            k.op(k.act, lambda e, so=so: e.activation(so.t[:, 0:512], pT[0].t[:, :], AF.Copy), reads=[pT[0]], writes=[so])
                            k.op(k.dve, lambda e, so=so: e.tensor_copy(so.t[:, 512:768], pT[1].t[:, 0:256]), reads=[pT[1]], writes=[so])
                            store(self.VM, self.VM.t[t0 + ti * 128:t0 + (ti + 1) * 128, :], so, so.t[:, :])
                if bi + 1 < len(BLOCKS):
                    prologue(bi + 1)
                p = proj(OC_KR)
                rope(p, n, 0, self.permm_b, self.KR, self.KR.t[:, tsl])
                for j in range(6):
                    p = proj(OC_SQ + j)
                    rope(p, n, 2, self.perms_b, self.SQ, self.SQ.t[j, :, tsl])
                for j in range(2):
                    p = proj(OC_SK + j)
                    rope(p, n, 2, self.perms_b, self.SK, self.SK.t[j, :, tsl])
                for j in range(4):
                    p = proj(OC_GQ + j)
                    of = stage(sff)
                    evac_copy(p, n, of, of.t[:, :n])
                    dst = self.GQ if j < 2 else self.GK
                    store(dst, dst.t[j % 2, :, tsl], of, of.t[:, :n])
                for j in range(4):
                    p = proj(OC_R + j)
                    of = stage(sff)
                    evac_copy(p, n, of, of.t[:, :n], func=AF.Silu)
                    store(self.GR, self.GR.t[j, :, tsl], of, of.t[:, :n])
                for d_ in range(2):
                    p = proj(OC_GLF + d_, m=16)
                    k.op(k.dve, lambda e, d_=d_, p=p: e.tensor_copy(glr.t[:, d_, :n], p.t[0:16, :n]), reads=[p], writes=[glr])
                    bname = "b_gla_gate_f" if d_ == 0 else "b_gla_gate_b"
                    dst = self.GLF if d_ == 0 else self.GLB
                    for j in range(2):
                        k.mm(pG, pG.t[:, :n], [(wgf.t[:, d_, j * 128:(j + 1) * 128], glr.t[:, d_, :n])], reads=[wgf, glr])
                        of = stage(sff)
                        k.op(k.act, lambda e, of=of, j=j, bname=bname: e.activation(
                            of.t[:, :n], pG.t[:, :n], AF.Exp, bias=self.negb[l].t[:, d_ * 2 + j:d_ * 2 + j + 1], scale=-1.0),
                             reads=[pG, self.negb[l]], writes=[of])
                        k.op(k.act, lambda e, of=of: e.activation(of.t[:, :n], of.t[:, :n], AF.Ln, bias=self.one_c.t[:, 0:1]),
                             reads=[of, self.one_c], writes=[of])
                        store(dst, dst.t[j, :, tsl], of, of.t[:, :n])
                run_tails()
                for ti in range(nt):
                    tt = slice(ti * 128, (ti + 1) * 128)
                    k.mm(pT[0], pT[0].t[:, :], [(h.t[:, kc, tt], wtm.t[:, kc, 0:512]) for kc in range(16)], reads=[wtm] + hv)
                    k.mm(pT[1], pT[1].t[:, 0:256], [(h.t[:, kc, tt], wtm.t[:, kc, 512:768]) for kc in range(16)], reads=[wtm] + hv)
                    so = stage(stm)
                    k.op(k.act, lambda e, so=so: e.activation(so.t[:, 0:512], pT[0].t[:, :], AF.Copy), reads=[pT[0]], writes=[so])
                    k.op(k.dve, lambda e, so=so: e.tensor_copy(so.t[:, 512:768], pT[1].t[:, 0:256]), reads=[pT[1]], writes=[so])
                    rows = slice(t0 + ti * 128, t0 + (ti + 1) * 128)
                    store(self.SV, self.SV.t[rows, :], so, so.t[:, 0:256])
                    store(self.GV, self.GV.t[rows, :], so, so.t[:, 256:768])
            flush(upto_tag=10 ** 9)

    def p2(self, l):
        k = self.k
        last = (l == DEPTH - 1)
        scale = float((128 + 64) ** -0.5)
        with contextlib.ExitStack() as st:
            knT = [k.sb(st, f"p2_kn{i}", (128, T), BF16) for i in range(2)]
            vT = [k.sb(st, f"p2_v{i}", (128, 34, 128), BF16) for i in range(2)]
            krT = k.sb(st, "p2_kr", (128, T), BF16)
            qn = [k.sb(st, f"p2_qn{i}", (128, 512), BF16) for i in range(3)]
            qr = [k.sb(st, f"p2_qr{i}", (128, 512), BF16) for i in range(3)]
            pt = [k.sb(st, f"p2_pt{i}", (128, 512), BF16) for i in range(4)]
            rden = k.sb(st, "p2_rden", (128, 512), F32)
            ob = [k.sb(st, f"p2_ob{i}", (128, 512), BF16) for i in range(2)]
            pS = [k.ps(st, f"p2_pS{i}", (128, 512)) for i in range(3)]
            pO = [k.ps(st, f"p2_pO{i}", (128, 512)) for i in range(2)]
            pD = [k.ps(st, f"p2_pD{i}", (128, 512)) for i in range(2)]
            dacc = [k.sb(st, f"p2_dacc{i}", (128, 512), F32) for i in range(2)]
            k.dma(krT, krT.t[:], self.KR, self.KR.t[:, :])
            blocks = BLOCKS if not last else BLOCKS[:8]
            iters = [(h, t0, n) for h in range(6) for (t0, n) in blocks]

            def load_kv(h):
                kn_, v_ = knT[h % 2], vT[h % 2]
                k.dma(kn_, kn_.t[:], self.KN, self.KN.t[h, :, :])
                k.dma(v_, v_.t[:], self.VM, self.VM.t[:, h * 128:(h + 1) * 128].rearrange("(kt p) d -> p kt d", p=128))

            def load_q(it):
                h, t0, n = iters[it]
                tsl = slice(t0, t0 + n)
                k.dma(qn[it % 3], qn[it % 3].t[:, :n], self.QN, self.QN.t[h, :, tsl])
                k.dma(qr[it % 3], qr[it % 3].t[:, :n], self.QR, self.QR.t[h // 2, :, tsl])

            load_kv(0)
            load_q(0)
            ipt = 0
            pend_store = None
            for it, (h, t0, n) in enumerate(iters):
                kn_, v_ = knT[h % 2], vT[h % 2]
                half = (h % 2) * 64
                tsl = slice(t0, t0 + n)
                qn_, qr_ = qn[it % 3], qr[it % 3]
                po, pd = pO[it % 2], pD[it % 2]
                da = dacc[it % 2]
                o_ = ob[it % 2]
                if it + 1 < len(iters):
                    load_q(it + 1)
                if t0 == 0 and h + 1 < 6:
                    load_kv(h + 1)
                if pend_store is not None:
                    pend_store()
                    pend_store = None
                kts = list(range(34)) if t0 < NLAT else [32, 33]

                def qk(kt, ps_):
                    ks = slice(kt * 128, (kt + 1) * 128)
                    k.mm(ps_, ps_.t[:, :n], [(kn_.t[:, ks], qn_.t[:, :n]),
                                             (krT.t[half:half + 64, ks], qr_.t[half:half + 64, :n])],
                         reads=[kn_, qn_, krT, qr_])

                ps_cur = pS[ipt % 3]
                qk(kts[0], ps_cur)
                if len(kts) > 1:
                    qk(kts[1], pS[(ipt + 1) % 3])
                for i, kt in enumerate(kts):
                    ps_nxt = pS[(ipt + 1) % 3]
                    if i + 2 < len(kts):
                        qk(kts[i + 2], pS[(ipt + 2) % 3])
                    p_ = pt[ipt % 4]
                    k.op(k.act, lambda e, p_=p_, ps_cur=ps_cur: e.activation(p_.t[:, :n], ps_cur.t[:, :n], AF.Exp, scale=scale),
                         reads=[ps_cur], writes=[p_])
                    k.mm(po, po.t[:, :n], [(v_.t[:, kt, :], p_.t[:, :n])], reads=[v_, p_],
                         first=(i == 0), last=(i == len(kts) - 1),
                         mark=(i % 3 == 0 or i == len(kts) - 1))
                    if i % 3 == 0:
                        if i == 0:
                            k.op(k.dve, lambda e, p_=p_, da=da: e.tensor_copy(da.t[:, :n], p_.t[:, :n]), reads=[p_], writes=[da])
                        else:
                            k.op(k.dve, lambda e, p_=p_, da=da: e.tensor_tensor(da.t[:, :n], da.t[:, :n], p_.t[:, :n], ALU.add),
                                 reads=[p_], writes=[da])
                    else:
                        k.mm(pd, pd.t[:, :n], [(self.ones_b.t[:], p_.t[:, :n])], reads=[self.ones_b, p_, v_],
                             first=(i == 1), last=False)
                    ipt += 1
                    ps_cur = ps_nxt
                    if i % 4 == 3:
                        self.conv_step(1)
                k.mm(pd, pd.t[:, :n], [(self.ones_f.t[:], da.t[:, :n])], reads=[self.ones_f, da], first=False, last=True)
                k.op(k.dve, lambda e, pd=pd: e.reciprocal(rden.t[:, :n], pd.t[:, :n]), reads=[pd], writes=[rden])
                k.op(k.dve, lambda e, po=po, o_=o_: e.tensor_tensor(o_.t[:, :n], po.t[:, :n], rden.t[:, :n], ALU.mult),
                     reads=[po, rden], writes=[o_])
                pend_store = (lambda h=h, tsl=tsl, o_=o_, n=n: k.dma(self.MIX, self.MIX.t[h, :, tsl], o_, o_.t[:, :n]))
            if pend_store is not None:
                pend_store()

    def p3(self, l):
        k = self.k
        last = (l == DEPTH - 1)
        scale = float(128 ** -0.5)
        with contextlib.ExitStack() as st:
            kT = [k.sb(st, f"p3_k{i}", (128, T), BF16) for i in range(2)]
            vT = [k.sb(st, f"p3_v{i}", (128, 34, 128), BF16) for i in range(2)]
            q = [k.sb(st, f"p3_q{i}", (128, 512), BF16) for i in range(3)]
            pt = [k.sb(st, f"p3_pt{i}", (128, 512), BF16) for i in range(4)]
            rden = k.sb(st, "p3_rden", (128, 512), F32)
            ob = [k.sb(st, f"p3_ob{i}", (128, 512), BF16) for i in range(2)]
            pS = [k.ps(st, f"p3_pS{i}", (128, 512)) for i in range(3)]
            pO = [k.ps(st, f"p3_pO{i}", (128, 512)) for i in range(2)]
            pD = [k.ps(st, f"p3_pD{i}", (128, 512)) for i in range(2)]
            blocks = BLOCKS if not last else BLOCKS[:8]
            iters = [(g, j, wi, t0, n) for g in range(2) for j in range(3) for wi, (t0, n) in enumerate(blocks)]

            def load_kv(g):
                k_, v_ = kT[g % 2], vT[g % 2]
                k.dma(k_, k_.t[:], self.SK, self.SK.t[g, :, :])
                k.dma(v_, v_.t[:], self.SV, self.SV.t[:, g * 128:(g + 1) * 128].rearrange("(kt p) d -> p kt d", p=128))

            def load_q(it):
                g, j, wi, t0, n = iters[it]
                k.dma(q[it % 3], q[it % 3].t[:, :n], self.SQ, self.SQ.t[g * 3 + j, :, t0:t0 + n])

            load_kv(0)
            load_kv(1)
            load_q(0)
            ipt = 0
            pend_store = None
            for it, (g, j, wi, t0, n) in enumerate(iters):
                k_, v_ = kT[g % 2], vT[g % 2]
                hq = g * 3 + j
                tsl = slice(t0, t0 + n)
                q_ = q[it % 3]
                po, pd = pO[it % 2], pD[it % 2]
                o_ = ob[it % 2]
                if it + 1 < len(iters):
                    load_q(it + 1)
                if pend_store is not None:
                    pend_store()
                    pend_store = None
                if t0 < NLAT:
                    kts = [(4 * wi - 1 + r, r) for r in range(6) if 0 <= 4 * wi - 1 + r < 32]
                    kts += [(32, None), (33, None)]
                else:
                    kts = [(32, None), (33, None)]

                def qk(kt, r, ps_):
                    ks = slice(kt * 128, (kt + 1) * 128)
                    terms = [(k_.t[:, ks], q_.t[:, :n])]
                    rd = [k_, q_]
                    if r is not None:
                        terms.append((self.ident_b.t[:], self.mask_b.t[:, r, :n]))
                        rd += [self.ident_b, self.mask_b]
                    k.mm(ps_, ps_.t[:, :n], terms, reads=rd)

                ps_cur = pS[ipt % 3]
                qk(kts[0][0], kts[0][1], ps_cur)
                if len(kts) > 1:
                    qk(kts[1][0], kts[1][1], pS[(ipt + 1) % 3])
                for i, (kt, r) in enumerate(kts):
                    ps_nxt = pS[(ipt + 1) % 3]
                    if i + 2 < len(kts):
                        qk(kts[i + 2][0], kts[i + 2][1], pS[(ipt + 2) % 3])
                    p_ = pt[ipt % 4]
                    k.op(k.act, lambda e, p_=p_, ps_cur=ps_cur: e.activation(p_.t[:, :n], ps_cur.t[:, :n], AF.Exp, scale=scale),
                         reads=[ps_cur], writes=[p_])
                    k.mm(po, po.t[:, :n], [(v_.t[:, kt, :], p_.t[:, :n])], reads=[v_, p_],
                         first=(i == 0), last=(i == len(kts) - 1), mark=(i == 0 or i == len(kts) - 1))
                    k.mm(pd, pd.t[:, :n], [(self.ones_b.t[:], p_.t[:, :n])], reads=[self.ones_b, p_, v_],
                         first=(i == 0), last=(i == len(kts) - 1))
                    ipt += 1
                    ps_cur = ps_nxt
                    if i == 3:
                        self.conv_step(1)
                sk_ap = self.sinkb.t[:, l * 6 + hq:l * 6 + hq + 1]
                k.op(k.dve, lambda e, pd=pd, sk_ap=sk_ap: e.tensor_scalar(rden.t[:, :n], pd.t[:, :n], sk_ap, None, ALU.add),
                     reads=[pd, self.sinkb], writes=[rden])
                k.op(k.dve, lambda e: e.reciprocal(rden.t[:, :n], rden.t[:, :n]), reads=[rden], writes=[rden])
                k.op(k.dve, lambda e, po=po, o_=o_: e.tensor_tensor(o_.t[:, :n], po.t[:, :n], rden.t[:, :n], ALU.mult),
                     reads=[po, rden], writes=[o_])
                pend_store = (lambda hq=hq, tsl=tsl, o_=o_, n=n: k.dma(self.MIX, self.MIX.t[6 + hq, :, tsl], o_, o_.t[:, :n]))
            if pend_store is not None:
                pend_store()

    def p4(self, l):
        k = self.k
        last = (l == DEPTH - 1)
        SEG = 1024
        with contextlib.ExitStack() as st:
            bsets = []
            for u in range(2):
                bsets.append((
                    [k.sb(st, f"p4_sp{u}_{i}", (128, SEG), F32) for i in range(2)],
                    k.sb(st, f"p4_q32_{u}", (128, SEG), F32),
                    k.sb(st, f"p4_k32_{u}", (128, SEG), F32),
                    k.sb(st, f"p4_e1_{u}", (128, SEG), F32),
                    k.sb(st, f"p4_e2_{u}", (128, SEG), F32),
                    k.sb(st, f"p4_qb_{u}", (128, SEG), BF16),
                    k.sb(st, f"p4_kb_{u}", (128, SEG), BF16),
                    k.sb(st, f"p4_kd_{u}", (128, SEG), BF16),
                    k.sb(st, f"p4_kdT_{u}", (128, SEG // 128, 128), BF16),
                    k.sb(st, f"p4_v_{u}", (128, SEG // 128, 256), BF16),
                    k.sb(st, f"p4_dec_{u}", (128, SEG // 64), F32),
                ))
            self._p4_unit = 0
            S = [k.sb(st, f"p4_S{d}", (128, 128), F32) for d in range(2)]
            Sb = [[k.sb(st, f"p4_Sb{d}_{i}", (128, 128), BF16) for i in range(2)] for d in range(2)]
            at = [k.sb(st, f"p4_at{i}", (128, 128), BF16) for i in range(4)]
            oacc = k.sb(st, "p4_oacc", (128, 2, NLAT), F32)
            oacv = k.views(oacc, 2 * (NLAT // 128))
            gm = k.sb(st, "p4_gm", (128, 2, 128), F32)
            rseg = k.sb(st, "p4_r", (128, 2, 512), F32)
            sq = [k.sb(st, f"p4_sq{i}", (128, 512), BF16) for i in range(2)]
            rs = k.sb(st, "p4_rs", (128, 512), F32)
            yo = [k.sb(st, f"p4_yo{i}", (128, 512), BF16) for i in range(2)]
            pA = [k.ps(st, f"p4_pA{i}", (128, 512)) for i in range(2)]
            pO = [k.ps(st, f"p4_pO{i}", (128, 512)) for i in range(2)]
            pS = k.ps(st, "p4_pS", (128, 512))
            pT = k.ps(st, "p4_pT", (128, 1024), BF16)
            pN = k.ps(st, "p4_pN", (128, 512))
            k.dma(gm, gm.t[:], self.C["k_gmask"], self.C["k_gmask"].t.rearrange("d a b -> a d b"))
            cnt = {}

            def nxt(lst):
                key = id(lst)
                cnt[key] = cnt.get(key, 0) + 1
                return lst[(cnt[key] - 1) % len(lst)]

            def gla_pass(pr, d, t0, ntok, add_to_acc, acc0):
                GL = self.GLF if d == 0 else self.GLB
                nseg = (ntok + SEG - 1) // SEG
                segs = list(range(nseg))
                if d == 1:
                    segs = segs[::-1]
                def seg_body(sg, bs):
                    spb, qf32, kf32, e1, e2, qb_, kb_, kd_, kdT, vseg, dec = bs
                    s0 = t0 + sg * SEG
                    ns = min(SEG, ntok - sg * SEG)
                    nch = ns // 64
                    ntile = ns // 128
                    ssl = slice(s0, s0 + ns)
                    a, b = spb
                    k.dma(a, a.t[:, :ns], GL, GL.t[pr, :, ssl])
                    k.dma(qf32, qf32.t[:, :ns], self.GQ, self.GQ.t[pr, :, ssl])
                    k.dma(kf32, kf32.t[:, :ns], self.GK, self.GK.t[pr, :, ssl])
                    k.dma(vseg, vseg.t[:, :ntile, :], self.GV,
                          self.GV.t[ssl, pr * 256:(pr + 1) * 256].rearrange("(a p) c -> p a c", p=128))
                    v3 = lambda buf: buf.t[:, :ns].rearrange("p (c l) -> p c l", l=64)
                    s_ = 1
                    while s_ < 64:
                        A3, B3 = v3(a), v3(b)
                        if d == 0:
                            k.op(k.dve, lambda e, A3=A3, B3=B3, s_=s_: e.tensor_tensor(B3[:, :, s_:], A3[:, :, s_:], A3[:, :, :64 - s_], ALU.add),
                                 reads=[a], writes=[b])
                            k.op(k.act, lambda e, A3=A3, B3=B3, s_=s_: e.copy(B3[:, :, :s_], A3[:, :, :s_]), reads=[a], writes=[b])
                        else:
                            k.op(k.dve, lambda e, A3=A3, B3=B3, s_=s_: e.tensor_tensor(B3[:, :, :64 - s_], A3[:, :, :64 - s_], A3[:, :, s_:], ALU.add),
                                 reads=[a], writes=[b])
                            k.op(k.act, lambda e, A3=A3, B3=B3, s_=s_: e.copy(B3[:, :, 64 - s_:], A3[:, :, 64 - s_:]), reads=[a], writes=[b])
                        a, b = b, a
                        s_ *= 2
                    cs = a
                    k.op(k.act, lambda e: e.activation(e1.t[:, :ns], cs.t[:, :ns], AF.Exp, scale=-1.0 / 16), reads=[cs], writes=[e1])
                    k.op(k.act, lambda e: e.activation(e2.t[:, :ns], cs.t[:, :ns], AF.Exp, scale=1.0 / 16), reads=[cs], writes=[e2])
                    e13 = e1.t[:, :ns].rearrange("p (c l) -> p c l", l=64)
                    edge = 63 if d == 0 else 0
                    k.op(k.act, lambda e: e.copy(dec.t[:, :nch], e13[:, :, edge]), reads=[e1], writes=[dec])
                    k.op(k.dve, lambda e: e.scalar_tensor_tensor(qb_.t[:, :ns], qf32.t[:, :ns], 0.125, e1.t[:, :ns], ALU.mult, ALU.mult),
                         reads=[qf32, e1], writes=[qb_])
                    k.op(k.dve, lambda e: e.tensor_tensor(kf32.t[:, :ns], kf32.t[:, :ns], e2.t[:, :ns], ALU.mult),
                         reads=[kf32, e2], writes=[kf32])
                    k.op(k.act, lambda e: e.copy(kb_.t[:, :ns], kf32.t[:, :ns]), reads=[kf32], writes=[kb_])
                    k3 = kf32.t[:, :ns].rearrange("p (c l) -> p c l", l=64)
                    kd3 = kd_.t[:, :ns].rearrange("p (c l) -> p c l", l=64)
                    k.op(k.dve, lambda e: e.tensor_tensor(kd3, k3, dec.t[:, :nch].unsqueeze(2).to_broadcast([128, nch, 64]), ALU.mult),
                         reads=[kf32, dec], writes=[kd_])
                    for ti in range(ntile):
                        k.mm(pT, pT.t[:, ti * 128:(ti + 1) * 128], [(kd_.t[:, ti * 128:(ti + 1) * 128], self.ident_b.t[:])],
                             reads=[kd_, self.ident_b], transpose=True, first=(ti == 0))
                    k.op(k.act, lambda e: e.activation(kdT.t[:, :ntile, :].rearrange("p a c -> p (a c)"), pT.t[:, :ntile * 128], AF.Copy),
                         reads=[pT], writes=[kdT])
                    self.conv_step(4)
                    yield
                    tiles = list(range(ntile))
                    if d == 1:
                        tiles = tiles[::-1]
                    for ti in tiles:
                        tsl = slice(ti * 128, (ti + 1) * 128)
                        gt = (s0 - t0) // 128 + ti
                        for hh in range(2):
                            hp = slice(hh * 64, (hh + 1) * 64)
                            k.mm(pA[hh], pA[hh].t[:, 0:128], [(kb_.t[hp, tsl], qb_.t[hp, tsl])], reads=[kb_, qb_])
                        ats = []
                        for hh in range(2):
                            a_ = nxt(at)
                            k.op(k.dve, lambda e, a_=a_, hh=hh: e.tensor_tensor(a_.t[:], pA[hh].t[:, 0:128], gm.t[:, d, :], ALU.mult),
                                 reads=[pA[hh], gm], writes=[a_])
                            ats.append(a_)
                        chunks = [0, 1] if d == 0 else [1, 0]
                        for hh in range(2):
                            k.mm(pO[hh], pO[hh].t[:, 0:128], [(vseg.t[:, ti, hh * 128:(hh + 1) * 128], ats[hh].t[:])],
                                 reads=[vseg, ats[hh]], first=True, last=False)
                        for ci, c in enumerate(chunks):
                            csl = slice(ti * 128 + c * 64, ti * 128 + (c + 1) * 64)
                            sb_cur = self._p4_sb[d]
                            for hh in range(2):
                                hp = slice(hh * 64, (hh + 1) * 64)
                                k.mm(pO[hh], pO[hh].t[:, c * 64:(c + 1) * 64], [(sb_cur.t[hp, :], qb_.t[hp, csl])],
                                     reads=[sb_cur, qb_], first=False, last=(ci == 1))
                            cp = slice(c * 64, (c + 1) * 64)
                            for hh in range(2):
                                k.mm(pS, pS.t[hh * 64:(hh + 1) * 64, 0:128],
                                     [(kdT.t[cp, ti, hh * 64:(hh + 1) * 64], vseg.t[cp, ti, hh * 128:(hh + 1) * 128])],
                                     reads=[kdT, vseg], first=(hh == 0), start=True)
                            cidx = ti * 2 + c
                            nb = nxt(Sb[d])
                            k.op(k.dve, lambda e, cidx=cidx, nb=nb: e.scalar_tensor_tensor(
                                nb.t[:], S[d].t[:], dec.t[:, cidx:cidx + 1], pS.t[:, 0:128], ALU.mult, ALU.add),
                                 reads=[S[d], dec, pS], writes=[nb])
                            k.op(k.dve, lambda e, cidx=cidx: e.scalar_tensor_tensor(
                                S[d].t[:], S[d].t[:], dec.t[:, cidx:cidx + 1], pS.t[:, 0:128], ALU.mult, ALU.add),
                                 reads=[S[d], dec, pS], writes=[S[d]])
                            self._p4_sb[d] = nb
                        for hh in range(2):
                            ov = oacv[hh * (NLAT // 128) + acc0 + gt]
                            o_ap = oacc.t[:, hh, (acc0 + gt) * 128:(acc0 + gt + 1) * 128]
                            if add_to_acc:
                                k.op(k.dve, lambda e, hh=hh, o_ap=o_ap: e.tensor_tensor(o_ap, pO[hh].t[:, 0:128], o_ap, ALU.add),
                                     reads=[pO[hh]], writes=[ov])
                            else:
                                k.op(k.act, lambda e, hh=hh, o_ap=o_ap: e.activation(o_ap, pO[hh].t[:, 0:128], AF.Copy),
                                     reads=[pO[hh]], writes=[ov])

                gens = []
                for sg in segs:
                    gens.append(seg_body(sg, bsets[self._p4_unit % 2]))
                    self._p4_unit += 1
                next(gens[0])
                for i in range(len(gens)):
                    if i + 1 < len(gens):
                        next(gens[i + 1])
                    for _ in gens[i]:
                        pass

            def finalize(pr, t0, ntok, acc0):
                for b0 in range(0, ntok, 512):
                    n = min(512, ntok - b0)
                    tsl = slice(t0 + b0, t0 + b0 + n)
                    k.dma(rseg, rseg.t[:, :, :n], self.GR, self.GR.t[2 * pr:2 * pr + 2, :, tsl].rearrange("j p t -> p j t"))
                    for hh in range(2):
                        a0 = acc0 * 128 + b0
                        o_ap = oacc.t[:, hh, a0:a0 + n]
                        ovs = [oacv[hh * (NLAT // 128) + acc0 + b0 // 128 + i] for i in range(n // 128)]
                        s_ = nxt(sq)
                        k.op(k.act, lambda e, s_=s_, o_ap=o_ap: e.activation(s_.t[:, :n], o_ap, AF.Square), reads=ovs, writes=[s_])
                        k.mm(pN, pN.t[:, :n], [(self.ones_b.t[:], s_.t[:, :n])], reads=[self.ones_b, s_])
                        k.op(k.act, lambda e: e.activation(rs.t[:, :n], pN.t[:, :n], AF.Sqrt, bias=self.eps_c.t[:, 0:1], scale=1.0 / 128),
                             reads=[pN, self.eps_c], writes=[rs])
                        k.op(k.dve, lambda e: e.reciprocal(rs.t[:, :n], rs.t[:, :n]), reads=[rs], writes=[rs])
                        k.op(k.dve, lambda e, o_ap=o_ap, hh=hh: e.scalar_tensor_tensor(
                            rs.t[:, :n], o_ap, self.vec("g_gla_out", l, 2 * pr + hh), rs.t[:, :n], ALU.mult, ALU.mult),
                             reads=ovs + [rs, self.vecT], writes=[rs])
                        y_ = nxt(yo)
                        k.op(k.dve, lambda e, y_=y_, hh=hh: e.tensor_tensor(y_.t[:, :n], rs.t[:, :n], rseg.t[:, hh, :n], ALU.mult),
                             reads=[rs, rseg], writes=[y_])
                        k.dma(self.MIX, self.MIX.t[12 + 2 * pr + hh, :, tsl], y_, y_.t[:, :n])

            for pr in range(2):
                self._p4_sb = [None, None]
                for d in range(2):
                    k.op(k.dve, lambda e, d=d: e.memset(S[d].t[:], 0.0), writes=[S[d]])
                    nb = nxt(Sb[d])
                    k.op(k.dve, lambda e, nb=nb: e.memset(nb.t[:], 0.0), writes=[nb])
                    self._p4_sb[d] = nb
                gla_pass(pr, 0, NLAT, NCTX, False, 0)
                gla_pass(pr, 1, NLAT, NCTX, True, 0)
                if not last:
                    finalize(pr, NLAT, NCTX, 0)
                gla_pass(pr, 0, 0, NLAT, False, 0)
                gla_pass(pr, 1, 0, NLAT, True, 0)
                finalize(pr, 0, NLAT, 0)
                self.conv_step(4)

    def p56(self, l):
        k = self.k
        last = (l == DEPTH - 1)
        final = (l == self.L - 1)
        with contextlib.ExitStack() as st:
            xb = k.sb(st, "p5_xb", (128, 16, 512), F32)
            xbv = k.views(xb, 16)
            mh = k.sb(st, "p5_mh", (128, 16, 512), BF16)
            mhv = k.views(mh, 16)
            aT = k.sb(st, "p5_aT", (128, NFC, 512), BF16)
            aTv = k.views(aT, NFC)
            wo = [k.sb(st, f"p5_wo{i}", (128, 16, 128), BF16) for i in range(3)]
            wg = [k.sb(st, f"p5_wg{i}", (128, 16, 128), BF16) for i in range(2)]
            wu = [k.sb(st, f"p5_wu{i}", (128, 16, 128), BF16) for i in range(2)]
            wd = [k.sb(st, f"p5_wd{i}", (128, NFC, 128), BF16) for i in range(2)]
            sg = [k.sb(st, f"p5_sg{i}", (128, 512), F32) for i in range(2)]
            sqb = [k.sb(st, f"p5_sq{i}", (128, 512), BF16) for i in range(4)]
            tmpf = [k.sb(st, f"p5_tmp{i}", (128, 512), F32) for i in range(3)]
            rs = k.sb(st, "p5_rs", (128, 512), F32)
            ostg = [k.sb(st, f"p5_os{i}", (128, D), F32) for i in range(2)] if final else None
            pA = [k.ps(st, f"p5_pA{i}", (128, 512)) for i in range(2)]
            pG = [k.ps(st, f"p5_pG{i}", (128, 512)) for i in range(2)]
            pU = [k.ps(st, f"p5_pU{i}", (128, 512)) for i in range(2)]
            pS = k.ps(st, "p5_pS", (128, 512))
            cnt = {}

            def nxt(lst):
                key = id(lst)
                cnt[key] = cnt.get(key, 0) + 1
                return lst[(cnt[key] - 1) % len(lst)]

            for bi, (t0, n) in enumerate(BLOCKS):
                isc = 1 if t0 >= NLAT else 0
                if isc and last:
                    continue
                tsl = slice(t0, t0 + n)
                nt = n // 128
                k.dma(xb, xb.t[:, :, :n], self.XT, self.XT.t[:, :, tsl].rearrange("kc p t -> p kc t"))
                for b_ in xbv:
                    b_.w = list(xb.w)
                for b_ in mhv:
                    mh.r = _compact(mh.r + b_.r + b_.w)
                k.dma(mh, mh.t[:, :, :n], self.MIX, self.MIX.t[:, :, tsl].rearrange("kc p t -> p kc t"))
                for b_ in mhv:
                    b_.w = list(mh.w)
                    b_.r = []
                for oc in range(16):
                    w = nxt(wo)
                    k.dma(w, w.t[:], self.WOUT[l], self.WOUT[l].t[oc])
                    p = nxt(pA)
                    k.mm(p, p.t[:, :n], [(w.t[:, kc, :], mh.t[:, kc, :n]) for kc in range(16)], reads=[w] + mhv)
                    g_ap = self.modT[l].t[:, 32 + oc, isc:isc + 1]
                    k.op(k.dve, lambda e, p=p, oc=oc, g_ap=g_ap: e.scalar_tensor_tensor(
                        xb.t[:, oc, :n], p.t[:, :n], g_ap, xb.t[:, oc, :n], ALU.mult, ALU.add),
                         reads=[p, self.modT[l]], writes=[xbv[oc]])
                self.rms_rstd(xb, xbv, 16, n, D, sqb, pS, rs)
                for kc in range(16):
                    tm_ = nxt(tmpf)
                    k.op(k.dve, lambda e, kc=kc, tm_=tm_: e.scalar_tensor_tensor(
                        tm_.t[:, :n], xb.t[:, kc, :n], self.Affn[l].t[:, kc, isc:isc + 1], rs.t[:, :n], ALU.mult, ALU.mult),
                         reads=[xbv[kc], rs, self.Affn[l]], writes=[tm_])
                    sh_ap = self.modT[l].t[:, 48 + kc, isc:isc + 1]
                    k.op(k.act, lambda e, kc=kc, tm_=tm_, sh_ap=sh_ap: e.activation(
                        mh.t[:, kc, :n], tm_.t[:, :n], AF.Identity, bias=sh_ap), reads=[tm_, self.modT[l]], writes=[mhv[kc]])
                for fc in range(NFC):
                    w1, w2 = nxt(wg), nxt(wu)
                    k.dma(w1, w1.t[:], self.WGU[l], self.WGU[l].t[fc])
                    k.dma(w2, w2.t[:], self.WGU[l], self.WGU[l].t[NFC + fc])
                    p1, p2 = nxt(pG), nxt(pU)
                    k.mm(p1, p1.t[:, :n], [(w1.t[:, kc, :], mh.t[:, kc, :n]) for kc in range(16)], reads=[w1] + mhv)
                    k.mm(p2, p2.t[:, :n], [(w2.t[:, kc, :], mh.t[:, kc, :n]) for kc in range(16)], reads=[w2] + mhv)
                    s_ = nxt(sg)
                    k.op(k.act, lambda e, s_=s_, p1=p1: e.activation(s_.t[:, :n], p1.t[:, :n], AF.Silu), reads=[p1], writes=[s_])
                    k.op(k.dve, lambda e, s_=s_, p2=p2, fc=fc: e.tensor_tensor(aT.t[:, fc, :n], s_.t[:, :n], p2.t[:, :n], ALU.mult),
                         reads=[s_, p2], writes=[aTv[fc]])
                for oc in range(16):
                    w = nxt(wd)
                    k.dma(w, w.t[:], self.WDN[l], self.WDN[l].t[oc])
                    p = nxt(pA)
                    k.mm(p, p.t[:, :n], [(w.t[:, fc, :], aT.t[:, fc, :n]) for fc in range(NFC)], reads=[w] + aTv)
                    g_ap = self.modT[l].t[:, 80 + oc, isc:isc + 1]
                    k.op(k.dve, lambda e, p=p, oc=oc, g_ap=g_ap: e.scalar_tensor_tensor(
                        xb.t[:, oc, :n], p.t[:, :n], g_ap, xb.t[:, oc, :n], ALU.mult, ALU.add),
                         reads=[p, self.modT[l]], writes=[xbv[oc]])
                if not final or "XT" in self.dbg:
                    k.dma(self.XT, self.XT.t[:, :, tsl].rearrange("kc p t -> p kc t"), xbv, xb.t[:, :, :n])
                if final and not isc:
                    self.rms_rstd(xb, xbv, 16, n, D, sqb, pS, rs)
                    for kc in range(16):
                        k.op(k.dve, lambda e, kc=kc: e.scalar_tensor_tensor(
                            xb.t[:, kc, :n], xb.t[:, kc, :n], self.vec("g_final", None, kc), rs.t[:, :n], ALU.mult, ALU.mult),
                             reads=[rs, self.vecT], writes=[xbv[kc]])
                    for ti in range(nt):
                        og = nxt(ostg)
                        for q4 in range(4):
                            p = nxt(pA)
                            for j in range(4):
                                kc = q4 * 4 + j
                                k.mm(p, p.t[:, j * 128:(j + 1) * 128], [(xb.t[:, kc, ti * 128:(ti + 1) * 128], self.ident_f.t[:])],
                                     reads=[xbv[kc], self.ident_f], transpose=True, first=(j == 0))
                            if q4 % 2 == 0:
                                k.op(k.act, lambda e, p=p, og=og, q4=q4: e.activation(og.t[:, q4 * 512:(q4 + 1) * 512], p.t[:, :], AF.Copy),
                                     reads=[p], writes=[og])
                            else:
                                k.op(k.dve, lambda e, p=p, og=og, q4=q4: e.tensor_copy(og.t[:, q4 * 512:(q4 + 1) * 512], p.t[:, :]),
                                     reads=[p], writes=[og])
                        k.dma(self.out, self.out.t[t0 + ti * 128:t0 + (ti + 1) * 128, :], og, og.t[:, :])
                for b_ in xbv:
                    xb.r = _compact(xb.r + b_.r + b_.w)
                    b_.r = []
                for b_ in aTv:
                    pass

    def build(self):
        k = self.k
        self.conv_rate = 0
        with contextlib.ExitStack() as st:
            self.consts(st)
            self.conv_setup(st)
            self.conv_until(0, 0)
            self.cv_throttle = True
            self.p0_transpose_in()
            self.p0_mod(st)
            if self.stop_after == "p0":
                return self.finish()
            for l in range(self.L):
                self.conv_until(l, 0)
                if self.stop_after == "conv":
                    return self.finish()
                if "skip_p1" not in self.dbg:
                    self.p1(l)
                if self.stop_after == "p1":
                    return self.finish()
                if "skip_p2" not in self.dbg:
                    self.p2(l)
                if self.stop_after == "p2":
                    return self.finish()
                if "skip_p3" not in self.dbg:
                    self.p3(l)
                if self.stop_after == "p3":
                    return self.finish()
                if "skip_p4" not in self.dbg:
                    self.p4(l)
                if self.stop_after == "p4":
                    return self.finish()
                self.conv_until(l, 1)
                self.p56(l)
            return self.finish()

    def finish(self):
        k = self.k
        toks = []
        for b in [self.out, self.XT, self.QN, self.QR, self.KN, self.KR, self.VM, self.SQ, self.SK, self.SV,
                  self.GQ, self.GK, self.GV, self.GLF, self.GLB, self.GR, self.MIX]:
            toks += b.w
        k.sp.wait(toks)
        k.sp.wait([(k.pe, k.pe.cnt), (k.act, k.act.cnt), (k.dve, k.dve.cnt), (k.pool, k.pool.cnt)])


def build_nc(n_layers=DEPTH, dbg=(), stop_after=None):
    nc = bass.Bass("TRN2", target_bir_lowering=False)
    with contextlib.ExitStack() as es:
        prog = Prog(nc, es, n_layers=n_layers, dbg=dbg, stop_after=stop_after)
        prog.build()
    return nc


def make_in_maps(inputs, cores):
    consts = host_consts()
    maps = []
    for b in cores:
        m = {
            "x": np.ascontiguousarray(inputs["x"][b]),
            "c": np.ascontiguousarray(inputs["c"][b]).reshape(16, 128),
            "ctx": np.ascontiguousarray(inputs["ctx"][b]),
            "c_ctx": np.ascontiguousarray(inputs["c_ctx"]).reshape(16, 128),
        }
        for n in W_SHAPES:
            m[n] = np.ascontiguousarray(inputs[n])
        m.update(consts)
        maps.append(m)
    return maps


def kernel(**inputs):
    inputs = {k_: np.asarray(v) for k_, v in inputs.items()}
    nc = build_nc()
    maps = make_in_maps(inputs, list(range(8)))
    res = run_bass_kernel_spmd(nc, maps, core_ids=list(range(8)))
    return np.stack([np.asarray(r["out"]) for r in res.results], axis=0).astype(np.float32)
```

```python
import contextlib
import numpy as np
import concourse.bass as bass
import concourse.mybir as mybir
from concourse.bass_utils import run_bass_kernel_spmd

F32 = mybir.dt.float32
BF16 = mybir.dt.bfloat16
AF = mybir.ActivationFunctionType
ALU = mybir.AluOpType

D = 2048
NLAT = 4096
NCTX = 256
T = NLAT + NCTX
DEPTH = 4
EPS = 1e-6
FFN = 5632
NFC = FFN // 128
BLOCKS = [(i * 512, 512) for i in range(8)] + [(NLAT, NCTX)]

C_CQ, C_CKV, C_KR, C_SQ, C_SK, C_SV, C_GQ, C_GK, C_GV, C_GLR, C_R = (
    0, 512, 1024, 1088, 1856, 2112, 2368, 2624, 2880, 3392, 3424)
IN_W = 3936
OC_CQ, OC_CKV, OC_KR, OC_SQ, OC_SK, OC_GQ, OC_GK, OC_R, OC_GLF, OC_GLB = 0, 4, 8, 9, 15, 17, 19, 21, 25, 26
N_OC = 27


class DSem:
    def __init__(self, handle):
        self.h = handle
        self.cnt = 0
        self.is_dma = True


class Eng:
    def __init__(self, name, e, sem, is_pe=False):
        self.name = name
        self.e = e
        self.h = sem
        self.cnt = 0
        self.is_dma = False
        self.is_pe = is_pe
        self.seen = {}

    def wait(self, toks):
        best = {}
        for t in toks:
            if t is None:
                continue
            s, v = t
            if s.is_dma:
                v = s.cnt
            elif s is self and self.is_pe:
                continue
            if v > best.get(id(s), (None, 0))[1]:
                best[id(s)] = (s, v)
        for s, v in best.values():
            if self.seen.get(id(s), 0) >= v:
                continue
            self.e.wait_ge(s.h, v)
            self.seen[id(s)] = v

    def done(self, ins):
        self.cnt += 1
        ins.then_inc(self.h, 1)
        return (self, self.cnt)


class Buf:
    def __init__(self, t, name="", dram=False):
        self.t = t
        self.name = name
        self.w = []
        self.r = []
        self.dsem = None
        self.dram = dram
        self.psum = False


def _compact(toks):
    best = {}
    for t in toks:
        if t is None:
            continue
        s, v = t
        if id(s) not in best or best[id(s)][1] < v:
            best[id(s)] = (s, v)
    return list(best.values())


class K:
    def __init__(self, nc, es):
        self.nc = nc
        self.es = es
        mk = lambda n: es.enter_context(nc.semaphore(n))
        self.pe = Eng("pe", nc.tensor, mk("m_pe"), is_pe=True)
        self.act = Eng("act", nc.scalar, mk("m_act"))
        self.dve = Eng("dve", nc.vector, mk("m_dve"))
        self.pool = Eng("pool", nc.gpsimd, mk("m_pool"))
        self.sp = Eng("sp", nc.sync, None)
        self.free_dsems = []
        self.n_dsems = 0
        self.freed = []

    def new_dsem(self):
        if self.free_dsems:
            return self.free_dsems.pop()
        self.n_dsems += 1
        return DSem(self.es.enter_context(self.nc.semaphore(f"d{self.n_dsems}")))

    def release(self, bufs):
        for b in bufs:
            toks = list(b.w) + list(b.r)
            for c in getattr(b, "children", []):
                toks += list(c.w) + list(c.r)
            self.freed = _compact(self.freed + toks)
            if b.dsem is not None:
                self.free_dsems.append(b.dsem)
                b.dsem = None

    def dram(self, name, shape, dt, kind="Internal"):
        t = self.nc.dram_tensor(name, list(shape), dt, kind=kind)
        return Buf(t.ap(), name, dram=True)

    def sb(self, st, name, shape, dt):
        self.uid = getattr(self, "uid", 0) + 1
        name = f"{name}_u{self.uid}"
        b = Buf(st.enter_context(self.nc.sbuf_tensor(name, list(shape), dt)), name)
        b.r = list(self.freed)
        st.callback(lambda: self.release([b]))
        return b

    def ps(self, st, name, shape, dt=F32):
        self.uid = getattr(self, "uid", 0) + 1
        name = f"{name}_u{self.uid}"
        b = Buf(st.enter_context(self.nc.psum_tensor(name, list(shape), dt)), name)
        b.psum = True
        b.r = list(self.freed)
        st.callback(lambda: self.release([b]))
        return b

    def op(self, eng, fn, reads=(), writes=()):
        deps = []
        for b in reads:
            deps += b.w
            if b.psum:
                deps += b.r
        for b in writes:
            deps += b.w
            deps += b.r
        eng.wait(deps)
        tok = eng.done(fn(eng.e))
        for b in writes:
            b.w = [tok]
            b.r = []
        for b in reads:
            if b in writes:
                continue
            b.r = _compact(b.r + [tok])
        return tok

    def mm(self, out_buf, out_ap, terms, reads=(), first=True, last=True, transpose=False, start=None, mark=True):
        pe = self.pe
        deps = []
        if first:
            deps += list(out_buf.w) + list(out_buf.r)
        for b in reads:
            deps += b.w
        pe.wait(deps)
        n = len(terms)
        ins = None
        for i, (l, r) in enumerate(terms):
            if transpose:
                ins = pe.e.transpose(out_ap, l, r)
            else:
                st_ = (first and i == 0) if start is None else (start and i == 0)
                ins = pe.e.matmul(out_ap, l, r, start=st_, stop=(last and i == n - 1))
        if not mark:
            if first:
                out_buf.r = []
            return None
        tok = pe.done(ins)
        out_buf.w = [tok]
        if first:
            out_buf.r = []
        for b in reads:
            b.r = _compact(b.r + [tok])
        return tok

    def dma(self, out_buf, out_ap, in_buf, in_ap, q=None, more=False):
        q = q or self.sp
        in_bufs = in_buf if isinstance(in_buf, (list, tuple)) else [in_buf]
        deps = []
        for ib in in_bufs:
            deps += list(ib.w)
        if not out_buf.dram and not out_buf.w:
            more = False
        if not out_buf.dram and not more:
            deps += list(out_buf.w) + list(out_buf.r)
        q.wait(deps)
        if out_buf.dsem is None:
            out_buf.dsem = self.new_dsem()
        s = out_buf.dsem
        s.cnt += 16
        q.e.dma_start(out=out_ap, in_=in_ap).then_inc(s.h, 16)
        tok = (s, s.cnt)
        out_buf.w = [tok]
        if not out_buf.dram and not more:
            out_buf.r = []
        for ib in in_bufs:
            if not ib.dram:
                ib.r = _compact(ib.r + [tok])
        return tok

    def views(self, buf, n):
        vs = [Buf(buf.t, f"{buf.name}.{i}") for i in range(n)]
        for v in vs:
            v.r = list(buf.r)
            v.w = list(buf.w)
            v.psum = buf.psum
        buf.children = getattr(buf, "children", []) + vs
        return vs


def _rope_tables():
    t = np.arange(NLAT)
    rows = (t // 64).astype(np.float32)
    cols = (t % 64).astype(np.float32)

    def tab(dh):
        half = dh // 2
        freqs = (10000.0 ** (-np.arange(half, dtype=np.float32) / half)).astype(np.float32)
        cos = np.ones((2 * dh, T), np.float32)
        sin = np.zeros((2 * dh, T), np.float32)
        for a, pos in enumerate((rows, cols)):
            ang = (pos[None, :] * freqs[:, None]).astype(np.float32)
            c, s = np.cos(ang).astype(np.float32), np.sin(ang).astype(np.float32)
            base = a * dh
            cos[base:base + half, :NLAT] = c
            cos[base + half:base + dh, :NLAT] = c
            sin[base:base + half, :NLAT] = -s
            sin[base + half:base + dh, :NLAT] = s
        return cos, sin

    cm, sm = tab(32)
    cs, ss = tab(64)
    cm = np.concatenate([cm, cm], 0)
    sm = np.concatenate([sm, sm], 0)
    return np.ascontiguousarray(cm), np.ascontiguousarray(sm), cs, ss


def _perm(dh, n=128):
    half = dh // 2
    p = np.zeros((n, n), np.float32)
    for m in range(n):
        g, o = divmod(m, dh)
        k = g * dh + (o + half) % dh
        p[k, m] = 1.0
    return p


def _swa_masks():
    m = np.zeros((6, 128, 512), np.float32)
    for r in range(6):
        kpos = (r - 1) * 128 + np.arange(128)[:, None]
        qpos = np.arange(512)[None, :]
        ok = np.abs(qpos - kpos) <= 128
        m[r] = np.where(ok, 0.0, -30000.0)
    return m


def _gla_masks():
    tp = np.arange(128)[:, None]
    t = np.arange(128)[None, :]
    same = (tp // 64) == (t // 64)
    m = np.zeros((2, 128, 128), np.float32)
    m[0] = (same & (tp <= t)).astype(np.float32)
    m[1] = (same & (tp > t)).astype(np.float32)
    return m


def host_consts():
    cm, sm, cs, ss = _rope_tables()
    return {
        "k_ident": np.eye(128, dtype=np.float32),
        "k_permm": _perm(32),
        "k_perms": _perm(64),
        "k_cosm": cm, "k_sinm": sm, "k_coss": cs, "k_sins": ss,
        "k_mask": _swa_masks(),
        "k_gmask": _gla_masks(),
    }


W_SHAPES = {
    "w_mod": (DEPTH, D, 6 * D), "b_mod": (DEPTH, 6 * D), "g_mix": (DEPTH, D), "g_ffn": (DEPTH, D),
    "w_in": (DEPTH, D, IN_W), "g_mla_q": (DEPTH, 512), "g_mla_kv": (DEPTH, 512),
    "w_mla_uq": (DEPTH, 512, 1152), "w_mla_ukv": (DEPTH, 512, 1536), "swa_sink": (DEPTH, 6),
    "w_gla_gate_f": (DEPTH, 16, 256), "b_gla_gate_f": (DEPTH, 256),
    "w_gla_gate_b": (DEPTH, 16, 256), "b_gla_gate_b": (DEPTH, 256), "g_gla_out": (DEPTH, 512),
    "w_out": (DEPTH, D, D), "w_ffn_gu": (DEPTH, D, 2 * FFN), "w_ffn_down": (DEPTH, FFN, D),
    "g_final": (D,),
}


class Prog:
    def __init__(self, nc, es, n_layers=DEPTH, dbg=(), stop_after=None):
        self.nc = nc
        self.es = es
        self.k = K(nc, es)
        self.L = n_layers
        self.dbg = set(dbg)
        self.stop_after = stop_after
        import os
        self.p1_stage = int(os.environ.get("P1_STAGE", "0"))
        k = self.k
        ein = lambda n, s: k.dram(n, s, F32, kind="ExternalInput")
        self.x = ein("x", (NLAT, D))
        self.c = ein("c", (16, 128))
        self.ctx = ein("ctx", (NCTX, D))
        self.c_ctx = ein("c_ctx", (16, 128))
        self.W = {n: ein(n, s) for n, s in W_SHAPES.items()}
        self.C = {n: ein(n, v.shape) for n, v in host_consts().items()}
        self.out = k.dram("out", (NLAT, D), F32, kind="ExternalOutput")
        L = self.L

        def scr(n, s, dt):
            return k.dram(n, s, dt, kind=("ExternalOutput" if n in self.dbg else "Internal"))

        self.XT = scr("XT", (16, 128, T), F32)
        self.QN = scr("QN", (6, 128, T), BF16)
        self.QR = scr("QR", (3, 128, T), BF16)
        self.KN = scr("KN", (6, 128, T), BF16)
        self.KR = scr("KR", (128, T), BF16)
        self.VM = scr("VM", (T, 768), BF16)
        self.SQ = scr("SQ", (6, 128, T), BF16)
        self.SK = scr("SK", (2, 128, T), BF16)
        self.SV = scr("SV", (T, 256), BF16)
        self.GQ = scr("GQ", (2, 128, T), F32)
        self.GK = scr("GK", (2, 128, T), F32)
        self.GV = scr("GV", (T, 512), BF16)
        self.GLF = scr("GLF", (2, 128, T), F32)
        self.GLB = scr("GLB", (2, 128, T), F32)
        self.GR = scr("GR", (4, 128, T), F32)
        self.MIX = scr("MIX", (16, 128, T), BF16)
        self.WINFM = [scr(f"WINFM{l}", (N_OC, 128, 16, 128), BF16) for l in range(L)]
        self.WINTM = [scr(f"WINTM{l}", (128, 16, 768), BF16) for l in range(L)]
        self.WUQ = [scr(f"WUQ{l}", (9, 128, 4, 128), BF16) for l in range(L)]
        self.WUKVFM = [scr(f"WUKVFM{l}", (6, 128, 4, 128), BF16) for l in range(L)]
        self.WUKVTM = [scr(f"WUKVTM{l}", (128, 4, 768), BF16) for l in range(L)]
        self.WOUT = [scr(f"WOUT{l}", (16, 128, 16, 128), BF16) for l in range(L)]
        self.WGU = [scr(f"WGU{l}", (88, 128, 16, 128), BF16) for l in range(L)]
        self.WDN = [scr(f"WDN{l}", (16, 128, NFC, 128), BF16) for l in range(L)]
        for l in range(L):
            sh = k.new_dsem()
            for b in (self.WINFM[l], self.WINTM[l], self.WUQ[l], self.WUKVFM[l], self.WUKVTM[l],
                      self.WOUT[l], self.WGU[l], self.WDN[l]):
                b.dsem = sh

    def consts(self, st):
        k = self.k
        self.ident_f = k.sb(st, "ident_f", (128, 128), F32)
        self.ident_b = k.sb(st, "ident_b", (128, 128), BF16)
        self.ones_b = k.sb(st, "ones_b", (128, 128), BF16)
        self.ones_f = k.sb(st, "ones_f", (128, 128), F32)
        self.permm_b = k.sb(st, "permm_b", (128, 128), BF16)
        self.perms_b = k.sb(st, "perms_b", (128, 128), BF16)
        self.mask_b = k.sb(st, "mask_b", (128, 6, 512), BF16)
        with contextlib.ExitStack() as tmp:
            pf = k.sb(tmp, "c_pf", (128, 2, 128), F32)
            mf = k.sb(tmp, "c_mf", (128, 6, 512), F32)
            k.dma(self.ident_f, self.ident_f.t[:], self.C["k_ident"], self.C["k_ident"].t[:, :])
            k.dma(pf, pf.t[:, 0, :], self.C["k_permm"], self.C["k_permm"].t[:, :])
            k.dma(pf, pf.t[:, 1, :], self.C["k_perms"], self.C["k_perms"].t[:, :], more=True)
            k.dma(mf, mf.t[:], self.C["k_mask"], self.C["k_mask"].t.rearrange("r k q -> k r q"))
            k.op(k.dve, lambda e: e.tensor_copy(self.ident_b.t[:], self.ident_f.t[:]),
                 reads=[self.ident_f], writes=[self.ident_b])
            k.op(k.dve, lambda e: e.tensor_copy(self.permm_b.t[:], pf.t[:, 0, :]), reads=[pf], writes=[self.permm_b])
            k.op(k.dve, lambda e: e.tensor_copy(self.perms_b.t[:], pf.t[:, 1, :]), reads=[pf], writes=[self.perms_b])
            k.op(k.dve, lambda e: e.tensor_copy(self.mask_b.t[:], mf.t[:]), reads=[mf], writes=[self.mask_b])
            k.op(k.dve, lambda e: e.memset(self.ones_b.t[:], 1.0), writes=[self.ones_b])
            k.op(k.dve, lambda e: e.memset(self.ones_f.t[:], 1.0), writes=[self.ones_f])
        L = self.L
        self.vecT = k.sb(st, "vecT", (128, 2, 128), F32)
        self.sinkb = k.sb(st, "sinkb", (128, 4 * 6), F32)
        self.eps_c = k.sb(st, "eps_c", (128, 1), F32)
        self.one_c = k.sb(st, "one_c", (128, 1), F32)
        self.negb = [k.sb(st, f"negb{l}", (128, 4), F32) for l in range(L)]
        W = self.W
        with contextlib.ExitStack() as tmp:
            vr = [k.sb(tmp, f"vrows{i}", (128, 128), F32) for i in range(2)]
            self.vcol = {}
            items = []
            for l in range(L):
                for nm, nr in (("g_mix", 16), ("g_ffn", 16), ("g_mla_q", 4), ("g_mla_kv", 4),
                               ("g_gla_out", 4), ("b_gla_gate_f", 2), ("b_gla_gate_b", 2)):
                    items.append((nm, l, nr))
            items.append(("g_final", None, 16))
            row = 0
            for nm, l, nr in items:
                g, r0 = divmod(row, 128)
                if r0 + nr > 128:
                    row = (g + 1) * 128
                    g, r0 = divmod(row, 128)
                src = W[nm].t[l] if l is not None else W[nm].t
                k.dma(vr[g], vr[g].t[r0:r0 + nr, :], W[nm], src.rearrange("(r c) -> r c", c=128), more=True)
                self.vcol[(nm, l)] = (g, r0)
                row += nr
            assert row <= 256
            with contextlib.ExitStack() as pst:
                pp = k.ps(pst, "c_pp", (128, 512))
                for g in range(2):
                    k.mm(pp, pp.t[:, g * 128:(g + 1) * 128], [(vr[g].t[:], self.ident_f.t[:])],
                         reads=[vr[g], self.ident_f], transpose=True)
                k.op(k.dve, lambda e: e.tensor_copy(self.vecT.t[:].rearrange("p g c -> p (g c)"), pp.t[:, 0:256]),
                     reads=[pp], writes=[self.vecT])
            k.op(k.dve, lambda e: e.memset(self.eps_c.t[:], EPS), writes=[self.eps_c])
            k.op(k.dve, lambda e: e.memset(self.one_c.t[:], 1.0), writes=[self.one_c])
            for l in range(L):
                for d_, nm in enumerate(("b_gla_gate_f", "b_gla_gate_b")):
                    g, r0 = self.vcol[(nm, l)]
                    k.op(k.dve, lambda e, l=l, d_=d_, g=g, r0=r0: e.tensor_scalar(
                        self.negb[l].t[:, d_ * 2:d_ * 2 + 2], self.vecT.t[:, g, r0:r0 + 2], -1.0, None, ALU.mult),
                         reads=[self.vecT], writes=[self.negb[l]])
            k.dma(self.sinkb, self.sinkb.t[:], W["swa_sink"],
                  W["swa_sink"].t.rearrange("l h -> (l h)").partition_broadcast(128))
            k.op(k.act, lambda e: e.activation(self.sinkb.t[:], self.sinkb.t[:], AF.Exp),
                 reads=[self.sinkb], writes=[self.sinkb])

    def vec(self, nm, l, j):
        g, r0 = self.vcol[(nm, l)]
        return self.vecT.t[:, g, r0 + j:r0 + j + 1]

    def p0_transpose_in(self):
        k = self.k
        with contextlib.ExitStack() as st:
            xin = [k.sb(st, f"p0_xin{i}", (128, 4, D), F32) for i in range(2)]
            xo = [k.sb(st, f"p0_xo{i}", (128, 16, 512), F32) for i in range(2)]
            pp = [k.ps(st, f"p0_pp{i}", (128, 512)) for i in range(4)]
            xovs = [k.views(b, 16) for b in xo]
            cnt = 0
            for bi, (t0, n) in enumerate(BLOCKS):
                xi = xin[bi % 2]
                xob = xo[bi % 2]
                xov = xovs[bi % 2]
                nt = n // 128
                if t0 < NLAT:
                    src, sb_ = self.x, self.x.t[t0:t0 + n, :]
                else:
                    src, sb_ = self.ctx, self.ctx.t[:, :]
                k.dma(xi, xi.t[:, 0:nt, :], src, sb_.rearrange("(a p) d -> p a d", p=128))
                for kc in range(16):
                    p = pp[cnt % 4]
                    for ti in range(nt):
                        k.mm(p, p.t[:, ti * 128:(ti + 1) * 128],
                             [(xi.t[:, ti, kc * 128:(kc + 1) * 128], self.ident_f.t[:])],
                             reads=[xi, self.ident_f], transpose=True, first=(ti == 0))
                    eng = k.dve if cnt % 2 == 0 else k.act
                    if eng is k.dve:
                        k.op(eng, lambda e, p=p, kc=kc: e.tensor_copy(xob.t[:, kc, :n], p.t[:, :n]), reads=[p], writes=[xov[kc]])
                    else:
                        k.op(eng, lambda e, p=p, kc=kc: e.copy(xob.t[:, kc, :n], p.t[:, :n]), reads=[p], writes=[xov[kc]])
                    cnt += 1
                k.dma(self.XT, self.XT.t[:, :, t0:t0 + n].rearrange("kc p t -> p kc t"), xov, xob.t[:, :, :n])

    def p0_mod(self, st):
        k = self.k
        L = self.L
        self.modT = [k.sb(st, f"modT{l}", (128, 96, 2), F32) for l in range(L)]
        self.Amix = [k.sb(st, f"Amix{l}", (128, 16, 2), F32) for l in range(L)]
        self.Affn = [k.sb(st, f"Affn{l}", (128, 16, 2), F32) for l in range(L)]
        with contextlib.ExitStack() as tmp:
            crow = k.sb(tmp, "m_crow", (32, 128), F32)
            sc = k.sb(tmp, "m_sc", (128, 16, 2), F32)
            wm = [k.sb(tmp, f"m_wm{i}", (128, 4096), F32) for i in range(3)]
            bm = k.sb(tmp, "m_bm", (2, 4096), F32)
            mrow = k.sb(tmp, "m_mrow", (2, 4096), F32)
            pp = [k.ps(tmp, f"m_pp{i}", (128, 512)) for i in range(8)]
            k.dma(crow, crow.t[0:16, :], self.c, self.c.t[:, :])
            k.dma(crow, crow.t[16:32, :], self.c_ctx, self.c_ctx.t[:, :], more=True)
            k.mm(pp[0], pp[0].t[:, 0:32], [(crow.t[:], self.ident_f.t[0:32, 0:32])], reads=[crow, self.ident_f], transpose=True)
            k.op(k.act, lambda e: e.activation(sc.t[:, :, 0], pp[0].t[:, 0:16], AF.Silu), reads=[pp[0]], writes=[sc])
            k.op(k.act, lambda e: e.activation(sc.t[:, :, 1], pp[0].t[:, 16:32], AF.Silu), reads=[pp[0]], writes=[sc])
            cnt = 0
            for l in range(L):
                for cg in range(3):
                    c0 = cg * 4096
                    k.dma(bm, bm.t[0:1, :], self.W["b_mod"], self.W["b_mod"].t[l:l + 1, c0:c0 + 4096])
                    k.dma(bm, bm.t[1:2, :], self.W["b_mod"], self.W["b_mod"].t[l:l + 1, c0:c0 + 4096], more=True)
                    for kc in range(16):
                        w = wm[cnt % 3]
                        cnt += 1
                        k.dma(w, w.t[:], self.W["w_mod"], self.W["w_mod"].t[l, kc * 128:(kc + 1) * 128, c0:c0 + 4096])
                        for j in range(8):
                            k.mm(pp[j], pp[j].t[0:2, :], [(sc.t[:, kc, :], w.t[:, j * 512:(j + 1) * 512])],
                                 reads=[sc, w], first=(kc == 0), last=(kc == 15))
                    for j in range(8):
                        k.op(k.dve, lambda e, j=j: e.tensor_tensor(mrow.t[0:2, j * 512:(j + 1) * 512], pp[j].t[0:2, :],
                                                                   bm.t[0:2, j * 512:(j + 1) * 512], ALU.add),
                             reads=[pp[j], bm], writes=[mrow])
                    for jj in range(32):
                        k.mm(pp[0], pp[0].t[:, jj * 2:jj * 2 + 2],
                             [(mrow.t[0:2, jj * 128:(jj + 1) * 128], self.ident_f.t[0:2, 0:2])],
                             reads=[mrow, self.ident_f], transpose=True, first=(jj == 0))
                    k.op(k.dve, lambda e, l=l, cg=cg: e.tensor_copy(
                        self.modT[l].t[:, cg * 32:(cg + 1) * 32, :].rearrange("p a b -> p (a b)"), pp[0].t[:, 0:64]),
                         reads=[pp[0]], writes=[self.modT[l]])
                for (A, gname, off) in ((self.Amix[l], "g_mix", 16), (self.Affn[l], "g_ffn", 64)):
                    g, r0 = self.vcol[(gname, l)]
                    for b in range(2):
                        k.op(k.dve, lambda e, A=A, b=b, off=off, g=g, r0=r0, l=l: e.scalar_tensor_tensor(
                            A.t[:, :, b], self.modT[l].t[:, off:off + 16, b], 1.0, self.vecT.t[:, g, r0:r0 + 16],
                            ALU.add, ALU.mult), reads=[self.modT[l], self.vecT], writes=[A])

    def conv_flush(self):
        for th in self.cv_pending:
            th()
        self.cv_pending = []

    def conv_pieces(self, l, part):
        k = self.k
        W = self.W
        FMv = lambda buf, a, b, kc: buf.t[a:b, :, kc, :].rearrange("oc p m -> p oc m")

        def piece(src_buf, src_ap, ncols, stores):
            if self.cv_throttle:
                k.pool.wait([(k.pe, k.pe.cnt)])
            for (c0, c1, dbuf, dap, three) in stores:
                s_ap = src_ap[:, c0:c1]
                if three:
                    s_ap = s_ap.rearrange("p (oc m) -> p oc m", m=128)
                k.dma(dbuf, dap, src_buf, s_ap, q=k.pool)

        if part == 0:
            yield from self._conv_early(l, piece, FMv)
        else:
            yield from self._conv_late(l, piece, FMv)
        self.conv_flush()

    def _conv_early(self, l, piece, FMv):
        W = self.W

        win = W["w_in"]
        FM, TM = self.WINFM[l], self.WINTM[l]
        for kc in range(16):
            rows = slice(kc * 128, (kc + 1) * 128)
            piece(win, win.t[l, rows, 0:1024], 1024, [(0, 1024, FM, FMv(FM, 0, 8, kc), True)])
            yield
            piece(win, win.t[l, rows, 1024:2112], 1088, [
                (0, 64, FM, FM.t[OC_KR, :, kc, 0:64], False),
                (0, 64, FM, FM.t[OC_KR, :, kc, 64:128], False),
                (64, 1088, FM, FMv(FM, OC_SQ, OC_SQ + 8, kc), True)])
            yield
            piece(win, win.t[l, rows, 2112:2880], 768, [
                (0, 256, TM, TM.t[:, kc, 0:256], False),
                (256, 768, FM, FMv(FM, OC_GQ, OC_GQ + 4, kc), True)])
            yield
            piece(win, win.t[l, rows, 2880:3936], 1056, [
                (0, 512, TM, TM.t[:, kc, 256:768], False),
                (512, 528, FM, FM.t[OC_GLF, :, kc, 0:16], False),
                (528, 544, FM, FM.t[OC_GLB, :, kc, 0:16], False),
                (544, 1056, FM, FMv(FM, OC_R, OC_R + 4, kc), True)])
            yield
        wq = W["w_mla_uq"]
        for kc in range(4):
            rows = slice(kc * 128, (kc + 1) * 128)
            stores = []
            for h in range(6):
                stores.append((h * 192, h * 192 + 128, self.WUQ[l], self.WUQ[l].t[h, :, kc, :], False))
                stores.append((h * 192 + 128, h * 192 + 192, self.WUQ[l],
                               self.WUQ[l].t[6 + h // 2, :, kc, (h % 2) * 64:(h % 2) * 64 + 64], False))
            piece(wq, wq.t[l, rows, :], 1152, stores)
            yield
        wkv = W["w_mla_ukv"]
        for kc in range(4):
            rows = slice(kc * 128, (kc + 1) * 128)
            for half in range(2):
                stores = []
                for hh in range(3):
                    h = half * 3 + hh
                    stores.append((hh * 256, hh * 256 + 128, self.WUKVFM[l], self.WUKVFM[l].t[h, :, kc, :], False))
                    stores.append((hh * 256 + 128, hh * 256 + 256, self.WUKVTM[l],
                                   self.WUKVTM[l].t[:, kc, h * 128:(h + 1) * 128], False))
                piece(wkv, wkv.t[l, rows, half * 768:(half + 1) * 768], 768, stores)
                yield

    def _conv_late(self, l, piece, FMv):
        W = self.W
        wo = W["w_out"]
        for kc in range(16):
            rows = slice(kc * 128, (kc + 1) * 128)
            for j in range(2):
                piece(wo, wo.t[l, rows, j * 1024:(j + 1) * 1024], 1024,
                      [(0, 1024, self.WOUT[l], FMv(self.WOUT[l], j * 8, j * 8 + 8, kc), True)])
                yield
        wg = W["w_ffn_gu"]
        for kc in range(16):
            rows = slice(kc * 128, (kc + 1) * 128)
            for j in range(11):
                piece(wg, wg.t[l, rows, j * 1024:(j + 1) * 1024], 1024,
                      [(0, 1024, self.WGU[l], FMv(self.WGU[l], j * 8, j * 8 + 8, kc), True)])
                yield
        wd = W["w_ffn_down"]
        for kc in range(NFC):
            rows = slice(kc * 128, (kc + 1) * 128)
            for j in range(2):
                piece(wd, wd.t[l, rows, j * 1024:(j + 1) * 1024], 1024,
                      [(0, 1024, self.WDN[l], FMv(self.WDN[l], j * 8, j * 8 + 8, kc), True)])
                yield

    def conv_setup(self, st):
        k = self.k
        self.cv_i = 0
        self.cv_throttle = False
        self.cv_pending = []
        self.cv_marks = set()
        self.cv_gen = self.conv_all()

    def conv_all(self):
        for l in range(self.L):
            for part in range(2):
                yield from self.conv_pieces(l, part)
                self.cv_marks.add((l, part))

    def conv_until(self, l, part):
        while (l, part) not in self.cv_marks and self.cv_gen is not None:
            self.conv_step(1)

    def conv_step(self, n):
        if self.cv_gen is None:
            return
        for _ in range(n):
            try:
                next(self.cv_gen)
            except StopIteration:
                self.cv_gen = None
                self.conv_flush()
                return

    def conv_finish(self):
        self.conv_step(10 ** 9)

    def rms_rstd(self, src_tile, src_bufs, nchunks, n, dim, sqb, pS, rs, cnt0=0):
        k = self.k
        for j in range(nchunks):
            sq = sqb[(cnt0 + j) % len(sqb)]
            k.op(k.act, lambda e, j=j, sq=sq: e.activation(sq.t[:, :n], src_tile.t[:, j, :n], AF.Square),
                 reads=[src_bufs[j]], writes=[sq])
            k.mm(pS, pS.t[:, :n], [(self.ones_b.t[:], sq.t[:, :n])], reads=[self.ones_b, sq],
                 first=(j == 0), last=(j == nchunks - 1))
        k.op(k.act, lambda e: e.activation(rs.t[:, :n], pS.t[:, :n], AF.Sqrt, bias=self.eps_c.t[:, 0:1], scale=1.0 / dim),
             reads=[pS, self.eps_c], writes=[rs])
        k.op(k.dve, lambda e: e.reciprocal(rs.t[:, :n], rs.t[:, :n]), reads=[rs], writes=[rs])

    def p1(self, l):
        k = self.k
        W = self.W
        L = self.L
        with contextlib.ExitStack() as st:
            xb = k.sb(st, "p1_xb", (128, 16, 512), F32)
            xbv = k.views(xb, 16)
            hT = [k.sb(st, f"p1_hT{i}", (128, 16, 512), BF16) for i in range(2)]
            hTv = [k.views(b, 16) for b in hT]
            sqb = [k.sb(st, f"p1_sq{i}", (128, 512), BF16) for i in range(4)]
            tmpf = [k.sb(st, f"p1_tmp{i}", (128, 512), F32) for i in range(3)]
            rs = k.sb(st, "p1_rs", (128, 512), F32)
            rss = [k.sb(st, f"p1_rs2_{i}", (128, 512), F32) for i in range(2)]
            wfm = [k.sb(st, f"p1_wfm{i}", (128, 16, 128), BF16) for i in range(3)]
            wtm = k.sb(st, "p1_wtm", (128, 16, 768), BF16)
            wuq = k.sb(st, "p1_wuq", (128, 9, 4, 128), BF16)
            wkf = k.sb(st, "p1_wkf", (128, 6, 4, 128), BF16)
            wkt = k.sb(st, "p1_wkt", (128, 4, 768), BF16)
            wgf = k.sb(st, "p1_wgf", (16, 2, 256), F32)
            cqs = [k.sb(st, f"p1_cq{i}", (128, 4, 512), F32) for i in range(2)]
            cqvs = [k.views(b_, 4) for b_ in cqs]
            cqns = [k.sb(st, f"p1_cqn{i}", (128, 4, 512), BF16) for i in range(2)]
            tab = k.sb(st, "p1_tab", (128, 4, 512), F32)
            sbf = [k.sb(st, f"p1_sbf{i}", (128, 512), BF16) for i in range(4)]
            sff = [k.sb(st, f"p1_sff{i}", (128, 512), F32) for i in range(4)]
            stm = [k.sb(st, f"p1_stm{i}", (128, 768), BF16) for i in range(2)]
            glr = k.sb(st, "p1_glr", (16, 2, 512), F32)
            pA = [k.ps(st, f"p1_pA{i}", (128, 512)) for i in range(3)]
            pS = k.ps(st, "p1_pS", (128, 512))
            pW = k.ps(st, "p1_pW", (128, 512))
            pT = [k.ps(st, f"p1_pT{i}", (128, 512)) for i in range(2)]
            pG = k.ps(st, "p1_pG", (128, 512))

            k.dma(wtm, wtm.t[:], self.WINTM[l], self.WINTM[l].t[:, :, :])
            k.dma(wuq, wuq.t[:], self.WUQ[l], self.WUQ[l].t.rearrange("oc p kc m -> p oc kc m"))
            k.dma(wkf, wkf.t[:], self.WUKVFM[l], self.WUKVFM[l].t.rearrange("oc p kc m -> p oc kc m"))
            k.dma(wkt, wkt.t[:], self.WUKVTM[l], self.WUKVTM[l].t[:, :, :])
            k.dma(wgf, wgf.t[:, 0, :], W["w_gla_gate_f"], W["w_gla_gate_f"].t[l])
            k.dma(wgf, wgf.t[:, 1, :], W["w_gla_gate_b"], W["w_gla_gate_b"].t[l], more=True)

            cnt = {"a": 0, "w": 0, "s": 0, "f": 0, "e": 0}

            def nxt(key, lst):
                key = id(lst)
                cnt[key] = cnt.get(key, 0) + 1
                return lst[(cnt[key] - 1) % len(lst)]

            def evac_copy(p, n, dst_buf, dst_ap, func=None):
                cnt["e"] += 1
                if func is not None or cnt["e"] % 2 == 0:
                    f = func if func is not None else AF.Copy
                    k.op(k.act, lambda e: e.activation(dst_ap, p.t[:, :n], f), reads=[p], writes=[dst_buf])
                else:
                    k.op(k.dve, lambda e: e.tensor_copy(dst_ap, p.t[:, :n]), reads=[p], writes=[dst_buf])

            pend = []
            pcount = {"n": 0}

            def store(dst, dst_ap, sbuf, s_ap):
                pend.append((pcount["n"], sbuf, lambda: k.dma(dst, dst_ap, sbuf, s_ap)))

            def flush(upto_tag=None, buf=None):
                while pend:
                    tag, sb_, th = pend[0]
                    need = (upto_tag is not None and tag <= upto_tag) or \
                           (buf is not None and any(e[1] is buf for e in pend))
                    if not need:
                        break
                    pend.pop(0)
                    th()

            def stage(lst):
                b = nxt("x", lst)
                flush(buf=b)
                return b

            ld = {"n": 0, "use": 0}
            total_loads = len(BLOCKS) * N_OC

            def ensure(upto):
                while ld["n"] <= min(upto, total_loads - 1):
                    g = ld["n"]
                    w_ = wfm[g % 3]
                    k.dma(w_, w_.t[:], self.WINFM[l], self.WINFM[l].t[g % N_OC])
                    ld["n"] += 1

            tails = []

            def run_tails():
                while tails:
                    tails.pop(0)()

            def rope(p, n, ti, perm, dst, dst_ap):
                xr = stage(sbf)
                k.op(k.act, lambda e: e.activation(xr.t[:, :n], p.t[:, :n], AF.Copy), reads=[p], writes=[xr])
                run_tails()

                def tail():
                    k.mm(pW, pW.t[:, :n], [(perm.t[:], xr.t[:, :n])], reads=[perm, xr])
                    t1 = stage(sff)
                    k.op(k.dve, lambda e: e.tensor_tensor(t1.t[:, :n], p.t[:, :n], tab.t[:, ti, :n], ALU.mult),
                         reads=[p, tab], writes=[t1])
                    t2 = stage(sff)
                    k.op(k.dve, lambda e: e.tensor_tensor(t2.t[:, :n], pW.t[:, :n], tab.t[:, ti + 1, :n], ALU.mult),
                         reads=[pW, tab], writes=[t2])
                    ob = stage(sbf)
                    k.op(k.dve, lambda e: e.tensor_tensor(ob.t[:, :n], t1.t[:, :n], t2.t[:, :n], ALU.add),
                         reads=[t1, t2], writes=[ob])
                    store(dst, dst_ap, ob, ob.t[:, :n])
                tails.append(tail)

            def prologue(bi):
                t0, n = BLOCKS[bi]
                isc = 1 if t0 >= NLAT else 0
                h = hT[bi % 2]
                hv = hTv[bi % 2]
                k.dma(xb, xb.t[:, :, :n], self.XT, self.XT.t[:, :, t0:t0 + n].rearrange("kc p t -> p kc t"))
                for b_ in xbv:
                    b_.w = list(xb.w)
                self.rms_rstd(xb, xbv, 16, n, D, sqb, pS, rs)
                for kc in range(16):
                    tm_ = nxt("f", tmpf)
                    k.op(k.dve, lambda e, kc=kc, tm_=tm_: e.scalar_tensor_tensor(
                        tm_.t[:, :n], xb.t[:, kc, :n], self.Amix[l].t[:, kc, isc:isc + 1], rs.t[:, :n], ALU.mult, ALU.mult),
                         reads=[xbv[kc], rs, self.Amix[l]], writes=[tm_])
                    sh_ap = self.modT[l].t[:, kc, isc:isc + 1]
                    k.op(k.act, lambda e, kc=kc, tm_=tm_, sh_ap=sh_ap: e.activation(
                        h.t[:, kc, :n], tm_.t[:, :n], AF.Identity, bias=sh_ap), reads=[tm_, self.modT[l]], writes=[hv[kc]])
                for b_ in xbv:
                    xb.r = _compact(xb.r + b_.r)
                    b_.r = []

            prologue(0)
            for bi, (t0, n) in enumerate(BLOCKS):
                isc = 1 if t0 >= NLAT else 0
                tsl = slice(t0, t0 + n)
                nt = n // 128
                h = hT[bi % 2]
                hv = hTv[bi % 2]
                for i, nm in enumerate(("k_cosm", "k_sinm", "k_coss", "k_sins")):
                    k.dma(tab, tab.t[:, i, :n], self.C[nm], self.C[nm].t[:, tsl], more=(i > 0))

                def proj(oc, m=128):
                    g = ld["use"]
                    ld["use"] += 1
                    assert g % N_OC == oc
                    ensure(g + 2)
                    pcount["n"] += 1
                    flush(upto_tag=pcount["n"] - 2)
                    w = wfm[g % 3]
                    p = nxt("a", pA)
                    k.mm(p, p.t[0:m, :n], [(w.t[:, kc, 0:m], h.t[:, kc, :n]) for kc in range(16)], reads=[w] + hv)
                    run_tails()
                    return p

                for which in range(2):
                    oc0 = OC_CQ if which == 0 else OC_CKV
                    for j in range(4):
                        p = proj(oc0 + j)
                        evac_copy(p, n, cqvs[which][j], cqs[which].t[:, j, :n])
                for which in range(2):
                    self.rms_rstd(cqs[which], cqvs[which], 4, n, 512, sqb, pS, rss[which])
                for which in range(2):
                    gname = "g_mla_q" if which == 0 else "g_mla_kv"
                    cq, cqv, cqn, rs2 = cqs[which], cqvs[which], cqns[which], rss[which]
                    for j in range(4):
                        k.op(k.dve, lambda e, j=j: e.scalar_tensor_tensor(
                            cqn.t[:, j, :n], cq.t[:, j, :n], self.vec(gname, l, j), rs2.t[:, :n], ALU.mult, ALU.mult),
                             reads=[cqv[j], rs2, self.vecT], writes=[cqn])
                    if which == 0:
                        for hh in range(6):
                            p = nxt("a", pA)
                            k.mm(p, p.t[:, :n], [(wuq.t[:, hh, kc, :], cqn.t[:, kc, :n]) for kc in range(4)], reads=[wuq, cqn])
                            ob = stage(sbf)
                            evac_copy(p, n, ob, ob.t[:, :n])
                            store(self.QN, self.QN.t[hh, :, tsl], ob, ob.t[:, :n])
                        for pr in range(3):
                            p = nxt("a", pA)
                            k.mm(p, p.t[:, :n], [(wuq.t[:, 6 + pr, kc, :], cqn.t[:, kc, :n]) for kc in range(4)], reads=[wuq, cqn])
                            rope(p, n, 0, self.permm_b, self.QR, self.QR.t[pr, :, tsl])
                        run_tails()
                    else:
                        for hh in range(6):
                            p = nxt("a", pA)
                            k.mm(p, p.t[:, :n], [(wkf.t[:, hh, kc, :], cqn.t[:, kc, :n]) for kc in range(4)], reads=[wkf, cqn])
                            ob = stage(sbf)
                            evac_copy(p, n, ob, ob.t[:, :n])
                            store(self.KN, self.KN.t[hh, :, tsl], ob, ob.t[:, :n])
                        for ti in range(nt):
                            tt = slice(ti * 128, (ti + 1) * 128)
                            k.mm(pT[0], pT[0].t[:, :], [(cqn.t[:, kc, tt], wkt.t[:, kc, 0:512]) for kc in range(4)], reads=[wkt, cqn])
                            k.mm(pT[1], pT[1].t[:, 0:256], [(cqn.t[:, kc, tt], wkt.t[:, kc, 512:768]) for kc in range(4)], reads=[wkt, cqn])
                            so = stage(stm)
                            k.op(k.act, lambda e, so=so: e.activation(so.t[:, 0:512], pT[0].t[:, :], AF.Copy), reads=[pT[0]], writes=[so])
                            k.op(k.dve, lambda e, so=so: e.tensor_copy(so.t[:, 512:768], pT[1].t[:, 0:256]), reads=[pT[1]], writes=[so])
                            store(self.VM, self.VM.t[t0 + ti * 128:t0 + (ti + 1) * 128, :], so, so.t[:, :])
                if bi + 1 < len(BLOCKS):
                    prologue(bi + 1)
                p = proj(OC_KR)
                rope(p, n, 0, self.permm_b, self.KR, self.KR.t[:, tsl])
                for j in range(6):
                    p = proj(OC_SQ + j)
                    rope(p, n, 2, self.perms_b, self.SQ, self.SQ.t[j, :, tsl])
                for j in range(2):
                    p = proj(OC_SK + j)
                    rope(p, n, 2, self.perms_b, self.SK, self.SK.t[j, :, tsl])
                for j in range(4):
                    p = proj(OC_GQ + j)
                    of = stage(sff)
                    evac_copy(p, n, of, of.t[:, :n])
                    dst = self.GQ if j < 2 else self.GK
                    store(dst, dst.t[j % 2, :, tsl], of, of.t[:, :n])
                for j in range(4):
                    p = proj(OC_R + j)
                    of = stage(sff)
                    evac_copy(p, n, of, of.t[:, :n], func=AF.Silu)
                    store(self.GR, self.GR.t[j, :, tsl], of, of.t[:, :n])
                for d_ in range(2):
                    p = proj(OC_GLF + d_, m=16)
                    k.op(k.dve, lambda e, d_=d_, p=p: e.tensor_copy(glr.t[:, d_, :n], p.t[0:16, :n]), reads=[p], writes=[glr])
                    bname = "b_gla_gate_f" if d_ == 0 else "b_gla_gate_b"
                    dst = self.GLF if d_ == 0 else self.GLB
                    for j in range(2):
                        k.mm(pG, pG.t[:, :n], [(wgf.t[:, d_, j * 128:(j + 1) * 128], glr.t[:, d_, :n])], reads=[wgf, glr])
                        of = stage(sff)
                        k.op(k.act, lambda e, of=of, j=j, bname=bname: e.activation(
                            of.t[:, :n], pG.t[:, :n], AF.Exp, bias=self.negb[l].t[:, d_ * 2 + j:d_ * 2 + j + 1], scale=-1.0),
                             reads=[pG, self.negb[l]], writes=[of])
                        k.op(k.act, lambda e, of=of: e.activation(of.t[:, :n], of.t[:, :n], AF.Ln, bias=self.one_c.t[:, 0:1]),
                             reads=[of, self.one_c], writes=[of])
                        store(dst, dst.t[j, :, tsl], of, of.t[:, :n])
                run_tails()
                for ti in range(nt):
                    tt = slice(ti * 128, (ti + 1) * 128)
                    k.mm(pT[0], pT[0].t[:, :], [(h.t[:, kc, tt], wtm.t[:, kc, 0:512]) for kc in range(16)], reads=[wtm] + hv)
                    k.mm(pT[1], pT[1].t[:, 0:256], [(h.t[:, kc, tt], wtm.t[:, kc, 512:768]) for kc in range(16)], reads=[wtm] + hv)
                    so = stage(stm)
                    k.op(k.act, lambda e, so=so: e.activation(so.t[:, 0:512], pT[0].t[:, :], AF.Copy), reads=[pT[0]], writes=[so])
                    k.op(k.dve, lambda e, so=so: e.tensor_copy(so.t[:, 512:768], pT[1].t[:, 0:256]), reads=[pT[1]], writes=[so])
                    rows = slice(t0 + ti * 128, t0 + (ti + 1) * 128)
                    store(self.SV, self.SV.t[rows, :], so, so.t[:, 0:256])
                    store(self.GV, self.GV.t[rows, :], so, so.t[:, 256:768])
            flush(upto_tag=10 ** 9)

    def p2(self, l):
        k = self.k
        last = (l == DEPTH - 1)
        scale = float((128 + 64) ** -0.5)
        with contextlib.ExitStack() as st:
            knT = [k.sb(st, f"p2_kn{i}", (128, T), BF16) for i in range(2)]
            vT = [k.sb(st, f"p2_v{i}", (128, 34, 128), BF16) for i in range(2)]
            krT = k.sb(st, "p2_kr", (128, T), BF16)
            qn = [k.sb(st, f"p2_qn{i}", (128, 512), BF16) for i in range(3)]
            qr = [k.sb(st, f"p2_qr{i}", (128, 512), BF16) for i in range(3)]
            pt = [k.sb(st, f"p2_pt{i}", (128, 512), BF16) for i in range(4)]
            rden = k.sb(st, "p2_rden", (128, 512), F32)
            ob = [k.sb(st, f"p2_ob{i}", (128, 512), BF16) for i in range(2)]
            pS = [k.ps(st, f"p2_pS{i}", (128, 512)) for i in range(3)]
            pO = [k.ps(st, f"p2_pO{i}", (128, 512)) for i in range(2)]
            pD = [k.ps(st, f"p2_pD{i}", (128, 512)) for i in range(2)]
            dacc = [k.sb(st, f"p2_dacc{i}", (128, 512), F32) for i in range(2)]
            k.dma(krT, krT.t[:], self.KR, self.KR.t[:, :])
            blocks = BLOCKS if not last else BLOCKS[:8]
            iters = [(h, t0, n) for h in range(6) for (t0, n) in blocks]

            def load_kv(h):
                kn_, v_ = knT[h % 2], vT[h % 2]
                k.dma(kn_, kn_.t[:], self.KN, self.KN.t[h, :, :])
                k.dma(v_, v_.t[:], self.VM, self.VM.t[:, h * 128:(h + 1) * 128].rearrange("(kt p) d -> p kt d", p=128))

            def load_q(it):
                h, t0, n = iters[it]
                tsl = slice(t0, t0 + n)
                k.dma(qn[it % 3], qn[it % 3].t[:, :n], self.QN, self.QN.t[h, :, tsl])
                k.dma(qr[it % 3], qr[it % 3].t[:, :n], self.QR, self.QR.t[h // 2, :, tsl])

            load_kv(0)
            load_q(0)
            ipt = 0
            pend_store = None
            for it, (h, t0, n) in enumerate(iters):
                kn_, v_ = knT[h % 2], vT[h % 2]
                half = (h % 2) * 64
                tsl = slice(t0, t0 + n)
                qn_, qr_ = qn[it % 3], qr[it % 3]
                po, pd = pO[it % 2], pD[it % 2]
                da = dacc[it % 2]
                o_ = ob[it % 2]
                if it + 1 < len(iters):
                    load_q(it + 1)
                if t0 == 0 and h + 1 < 6:
                    load_kv(h + 1)
                if pend_store is not None:
                    pend_store()
                    pend_store = None
                kts = list(range(34)) if t0 < NLAT else [32, 33]

                def qk(kt, ps_):
                    ks = slice(kt * 128, (kt + 1) * 128)
                    k.mm(ps_, ps_.t[:, :n], [(kn_.t[:, ks], qn_.t[:, :n]),
                                             (krT.t[half:half + 64, ks], qr_.t[half:half + 64, :n])],
                         reads=[kn_, qn_, krT, qr_])

                ps_cur = pS[ipt % 3]
                qk(kts[0], ps_cur)
                if len(kts) > 1:
                    qk(kts[1], pS[(ipt + 1) % 3])
                for i, kt in enumerate(kts):
                    ps_nxt = pS[(ipt + 1) % 3]
                    if i + 2 < len(kts):
                        qk(kts[i + 2], pS[(ipt + 2) % 3])
                    p_ = pt[ipt % 4]
                    k.op(k.act, lambda e, p_=p_, ps_cur=ps_cur: e.activation(p_.t[:, :n], ps_cur.t[:, :n], AF.Exp, scale=scale),
                         reads=[ps_cur], writes=[p_])
                    k.mm(po, po.t[:, :n], [(v_.t[:, kt, :], p_.t[:, :n])], reads=[v_, p_],
                         first=(i == 0), last=(i == len(kts) - 1),
                         mark=(i % 3 == 0 or i == len(kts) - 1))
                    if i % 3 == 0:
                        if i == 0:
                            k.op(k.dve, lambda e, p_=p_, da=da: e.tensor_copy(da.t[:, :n], p_.t[:, :n]), reads=[p_], writes=[da])
                        else:
                            k.op(k.dve, lambda e, p_=p_, da=da: e.tensor_tensor(da.t[:, :n], da.t[:, :n], p_.t[:, :n], ALU.add),
                                 reads=[p_], writes=[da])
                    else:
                        k.mm(pd, pd.t[:, :n], [(self.ones_b.t[:], p_.t[:, :n])], reads=[self.ones_b, p_, v_],
                             first=(i == 1), last=False)
                    ipt += 1
                    ps_cur = ps_nxt
                    if i % 4 == 3:
                        self.conv_step(1)
                k.mm(pd, pd.t[:, :n], [(self.ones_f.t[:], da.t[:, :n])], reads=[self.ones_f, da], first=False, last=True)
                k.op(k.dve, lambda e, pd=pd: e.reciprocal(rden.t[:, :n], pd.t[:, :n]), reads=[pd], writes=[rden])
                k.op(k.dve, lambda e, po=po, o_=o_: e.tensor_tensor(o_.t[:, :n], po.t[:, :n], rden.t[:, :n], ALU.mult),
                     reads=[po, rden], writes=[o_])
                pend_store = (lambda h=h, tsl=tsl, o_=o_, n=n: k.dma(self.MIX, self.MIX.t[h, :, tsl], o_, o_.t[:, :n]))
            if pend_store is not None:
                pend_store()

    def p3(self, l):
        k = self.k
        last = (l == DEPTH - 1)
        scale = float(128 ** -0.5)
        with contextlib.ExitStack() as st:
            kT = [k.sb(st, f"p3_k{i}", (128, T), BF16) for i in range(2)]
            vT = [k.sb(st, f"p3_v{i}", (128, 34, 128), BF16) for i in range(2)]
            q = [k.sb(st, f"p3_q{i}", (128, 512), BF16) for i in range(3)]
            pt = [k.sb(st, f"p3_pt{i}", (128, 512), BF16) for i in range(4)]
            rden = k.sb(st, "p3_rden", (128, 512), F32)
            ob = [k.sb(st, f"p3_ob{i}", (128, 512), BF16) for i in range(2)]
            pS = [k.ps(st, f"p3_pS{i}", (128, 512)) for i in range(3)]
            pO = [k.ps(st, f"p3_pO{i}", (128, 512)) for i in range(2)]
            pD = [k.ps(st, f"p3_pD{i}", (128, 512)) for i in range(2)]
            blocks = BLOCKS if not last else BLOCKS[:8]
            iters = [(g, j, wi, t0, n) for g in range(2) for j in range(3) for wi, (t0, n) in enumerate(blocks)]

            def load_kv(g):
                k_, v_ = kT[g % 2], vT[g % 2]
                k.dma(k_, k_.t[:], self.SK, self.SK.t[g, :, :])
                k.dma(v_, v_.t[:], self.SV, self.SV.t[:, g * 128:(g + 1) * 128].rearrange("(kt p) d -> p kt d", p=128))

            def load_q(it):
                g, j, wi, t0, n = iters[it]
                k.dma(q[it % 3], q[it % 3].t[:, :n], self.SQ, self.SQ.t[g * 3 + j, :, t0:t0 + n])

            load_kv(0)
            load_kv(1)
            load_q(0)
            ipt = 0
            pend_store = None
            for it, (g, j, wi, t0, n) in enumerate(iters):
                k_, v_ = kT[g % 2], vT[g % 2]
                hq = g * 3 + j
                tsl = slice(t0, t0 + n)
                q_ = q[it % 3]
                po, pd = pO[it % 2], pD[it % 2]
                o_ = ob[it % 2]
                if it + 1 < len(iters):
                    load_q(it + 1)
                if pend_store is not None:
                    pend_store()
                    pend_store = None
                if t0 < NLAT:
                    kts = [(4 * wi - 1 + r, r) for r in range(6) if 0 <= 4 * wi - 1 + r < 32]
                    kts += [(32, None), (33, None)]
                else:
                    kts = [(32, None), (33, None)]

                def qk(kt, r, ps_):
                    ks = slice(kt * 128, (kt + 1) * 128)
                    terms = [(k_.t[:, ks], q_.t[:, :n])]
                    rd = [k_, q_]
                    if r is not None:
                        terms.append((self.ident_b.t[:], self.mask_b.t[:, r, :n]))
                        rd += [self.ident_b, self.mask_b]
                    k.mm(ps_, ps_.t[:, :n], terms, reads=rd)

                ps_cur = pS[ipt % 3]
                qk(kts[0][0], kts[0][1], ps_cur)
                if len(kts) > 1:
                    qk(kts[1][0], kts[1][1], pS[(ipt + 1) % 3])
                for i, (kt, r) in enumerate(kts):
                    ps_nxt = pS[(ipt + 1) % 3]
                    if i + 2 < len(kts):
                        qk(kts[i + 2][0], kts[i + 2][1], pS[(ipt + 2) % 3])
                    p_ = pt[ipt % 4]
                    k.op(k.act, lambda e, p_=p_, ps_cur=ps_cur: e.activation(p_.t[:, :n], ps_cur.t[:, :n], AF.Exp, scale=scale),
                         reads=[ps_cur], writes=[p_])
                    k.mm(po, po.t[:, :n], [(v_.t[:, kt, :], p_.t[:, :n])], reads=[v_, p_],
                         first=(i == 0), last=(i == len(kts) - 1), mark=(i == 0 or i == len(kts) - 1))
                    k.mm(pd, pd.t[:, :n], [(self.ones_b.t[:], p_.t[:, :n])], reads=[self.ones_b, p_, v_],
                         first=(i == 0), last=(i == len(kts) - 1))
                    ipt += 1
                    ps_cur = ps_nxt
                    if i == 3:
                        self.conv_step(1)
                sk_ap = self.sinkb.t[:, l * 6 + hq:l * 6 + hq + 1]
                k.op(k.dve, lambda e, pd=pd, sk_ap=sk_ap: e.tensor_scalar(rden.t[:, :n], pd.t[:, :n], sk_ap, None, ALU.add),
                     reads=[pd, self.sinkb], writes=[rden])
                k.op(k.dve, lambda e: e.reciprocal(rden.t[:, :n], rden.t[:, :n]), reads=[rden], writes=[rden])
                k.op(k.dve, lambda e, po=po, o_=o_: e.tensor_tensor(o_.t[:, :n], po.t[:, :n], rden.t[:, :n], ALU.mult),
                     reads=[po, rden], writes=[o_])
                pend_store = (lambda hq=hq, tsl=tsl, o_=o_, n=n: k.dma(self.MIX, self.MIX.t[6 + hq, :, tsl], o_, o_.t[:, :n]))
            if pend_store is not None:
                pend_store()

    def p4(self, l):
        k = self.k
        last = (l == DEPTH - 1)
        SEG = 1024
        with contextlib.ExitStack() as st:
            bsets = []
            for u in range(2):
                bsets.append((
                    [k.sb(st, f"p4_sp{u}_{i}", (128, SEG), F32) for i in range(2)],
                    k.sb(st, f"p4_q32_{u}", (128, SEG), F32),
                    k.sb(st, f"p4_k32_{u}", (128, SEG), F32),
                    k.sb(st, f"p4_e1_{u}", (128, SEG), F32),
                    k.sb(st, f"p4_e2_{u}", (128, SEG), F32),
                    k.sb(st, f"p4_qb_{u}", (128, SEG), BF16),
                    k.sb(st, f"p4_kb_{u}", (128, SEG), BF16),
                    k.sb(st, f"p4_kd_{u}", (128, SEG), BF16),
                    k.sb(st, f"p4_kdT_{u}", (128, SEG // 128, 128), BF16),
                    k.sb(st, f"p4_v_{u}", (128, SEG // 128, 256), BF16),
                    k.sb(st, f"p4_dec_{u}", (128, SEG // 64), F32),
                ))
            self._p4_unit = 0
            S = [k.sb(st, f"p4_S{d}", (128, 128), F32) for d in range(2)]
            Sb = [[k.sb(st, f"p4_Sb{d}_{i}", (128, 128), BF16) for i in range(2)] for d in range(2)]
            at = [k.sb(st, f"p4_at{i}", (128, 128), BF16) for i in range(4)]
            oacc = k.sb(st, "p4_oacc", (128, 2, NLAT), F32)
            oacv = k.views(oacc, 2 * (NLAT // 128))
            gm = k.sb(st, "p4_gm", (128, 2, 128), F32)
            rseg = k.sb(st, "p4_r", (128, 2, 512), F32)
            sq = [k.sb(st, f"p4_sq{i}", (128, 512), BF16) for i in range(2)]
            rs = k.sb(st, "p4_rs", (128, 512), F32)
            yo = [k.sb(st, f"p4_yo{i}", (128, 512), BF16) for i in range(2)]
            pA = [k.ps(st, f"p4_pA{i}", (128, 512)) for i in range(2)]
            pO = [k.ps(st, f"p4_pO{i}", (128, 512)) for i in range(2)]
            pS = k.ps(st, "p4_pS", (128, 512))
            pT = k.ps(st, "p4_pT", (128, 1024), BF16)
            pN = k.ps(st, "p4_pN", (128, 512))
            k.dma(gm, gm.t[:], self.C["k_gmask"], self.C["k_gmask"].t.rearrange("d a b -> a d b"))
            cnt = {}

            def nxt(lst):
                key = id(lst)
                cnt[key] = cnt.get(key, 0) + 1
                return lst[(cnt[key] - 1) % len(lst)]

            def gla_pass(pr, d, t0, ntok, add_to_acc, acc0):
                GL = self.GLF if d == 0 else self.GLB
                nseg = (ntok + SEG - 1) // SEG
                segs = list(range(nseg))
                if d == 1:
                    segs = segs[::-1]
                def seg_body(sg, bs):
                    spb, qf32, kf32, e1, e2, qb_, kb_, kd_, kdT, vseg, dec = bs
                    s0 = t0 + sg * SEG
                    ns = min(SEG, ntok - sg * SEG)
                    nch = ns // 64
                    ntile = ns // 128
                    ssl = slice(s0, s0 + ns)
                    a, b = spb
                    k.dma(a, a.t[:, :ns], GL, GL.t[pr, :, ssl])
                    k.dma(qf32, qf32.t[:, :ns], self.GQ, self.GQ.t[pr, :, ssl])
                    k.dma(kf32, kf32.t[:, :ns], self.GK, self.GK.t[pr, :, ssl])
                    k.dma(vseg, vseg.t[:, :ntile, :], self.GV,
                          self.GV.t[ssl, pr * 256:(pr + 1) * 256].rearrange("(a p) c -> p a c", p=128))
                    v3 = lambda buf: buf.t[:, :ns].rearrange("p (c l) -> p c l", l=64)
                    s_ = 1
                    while s_ < 64:
                        A3, B3 = v3(a), v3(b)
                        if d == 0:
                            k.op(k.dve, lambda e, A3=A3, B3=B3, s_=s_: e.tensor_tensor(B3[:, :, s_:], A3[:, :, s_:], A3[:, :, :64 - s_], ALU.add),
                                 reads=[a], writes=[b])
                            k.op(k.act, lambda e, A3=A3, B3=B3, s_=s_: e.copy(B3[:, :, :s_], A3[:, :, :s_]), reads=[a], writes=[b])
                        else:
                            k.op(k.dve, lambda e, A3=A3, B3=B3, s_=s_: e.tensor_tensor(B3[:, :, :64 - s_], A3[:, :, :64 - s_], A3[:, :, s_:], ALU.add),
                                 reads=[a], writes=[b])
                            k.op(k.act, lambda e, A3=A3, B3=B3, s_=s_: e.copy(B3[:, :, 64 - s_:], A3[:, :, 64 - s_:]), reads=[a], writes=[b])
                        a, b = b, a
                        s_ *= 2
                    cs = a
                    k.op(k.act, lambda e: e.activation(e1.t[:, :ns], cs.t[:, :ns], AF.Exp, scale=-1.0 / 16), reads=[cs], writes=[e1])
                    k.op(k.act, lambda e: e.activation(e2.t[:, :ns], cs.t[:, :ns], AF.Exp, scale=1.0 / 16), reads=[cs], writes=[e2])
                    e13 = e1.t[:, :ns].rearrange("p (c l) -> p c l", l=64)
                    edge = 63 if d == 0 else 0
                    k.op(k.act, lambda e: e.copy(dec.t[:, :nch], e13[:, :, edge]), reads=[e1], writes=[dec])
                    k.op(k.dve, lambda e: e.scalar_tensor_tensor(qb_.t[:, :ns], qf32.t[:, :ns], 0.125, e1.t[:, :ns], ALU.mult, ALU.mult),
                         reads=[qf32, e1], writes=[qb_])
                    k.op(k.dve, lambda e: e.tensor_tensor(kf32.t[:, :ns], kf32.t[:, :ns], e2.t[:, :ns], ALU.mult),
                         reads=[kf32, e2], writes=[kf32])
                    k.op(k.act, lambda e: e.copy(kb_.t[:, :ns], kf32.t[:, :ns]), reads=[kf32], writes=[kb_])
                    k3 = kf32.t[:, :ns].rearrange("p (c l) -> p c l", l=64)
                    kd3 = kd_.t[:, :ns].rearrange("p (c l) -> p c l", l=64)
                    k.op(k.dve, lambda e: e.tensor_tensor(kd3, k3, dec.t[:, :nch].unsqueeze(2).to_broadcast([128, nch, 64]), ALU.mult),
                         reads=[kf32, dec], writes=[kd_])
                    for ti in range(ntile):
                        k.mm(pT, pT.t[:, ti * 128:(ti + 1) * 128], [(kd_.t[:, ti * 128:(ti + 1) * 128], self.ident_b.t[:])],
                             reads=[kd_, self.ident_b], transpose=True, first=(ti == 0))
                    k.op(k.act, lambda e: e.activation(kdT.t[:, :ntile, :].rearrange("p a c -> p (a c)"), pT.t[:, :ntile * 128], AF.Copy),
                         reads=[pT], writes=[kdT])
                    self.conv_step(4)
                    yield
                    tiles = list(range(ntile))
                    if d == 1:
                        tiles = tiles[::-1]
                    for ti in tiles:
                        tsl = slice(ti * 128, (ti + 1) * 128)
                        gt = (s0 - t0) // 128 + ti
                        for hh in range(2):
                            hp = slice(hh * 64, (hh + 1) * 64)
                            k.mm(pA[hh], pA[hh].t[:, 0:128], [(kb_.t[hp, tsl], qb_.t[hp, tsl])], reads=[kb_, qb_])
                        ats = []
                        for hh in range(2):
                            a_ = nxt(at)
                            k.op(k.dve, lambda e, a_=a_, hh=hh: e.tensor_tensor(a_.t[:], pA[hh].t[:, 0:128], gm.t[:, d, :], ALU.mult),
                                 reads=[pA[hh], gm], writes=[a_])
                            ats.append(a_)
                        chunks = [0, 1] if d == 0 else [1, 0]
                        for hh in range(2):
                            k.mm(pO[hh], pO[hh].t[:, 0:128], [(vseg.t[:, ti, hh * 128:(hh + 1) * 128], ats[hh].t[:])],
                                 reads=[vseg, ats[hh]], first=True, last=False)
                        for ci, c in enumerate(chunks):
                            csl = slice(ti * 128 + c * 64, ti * 128 + (c + 1) * 64)
                            sb_cur = self._p4_sb[d]
                            for hh in range(2):
                                hp = slice(hh * 64, (hh + 1) * 64)
                                k.mm(pO[hh], pO[hh].t[:, c * 64:(c + 1) * 64], [(sb_cur.t[hp, :], qb_.t[hp, csl])],
                                     reads=[sb_cur, qb_], first=False, last=(ci == 1))
                            cp = slice(c * 64, (c + 1) * 64)
                            for hh in range(2):
                                k.mm(pS, pS.t[hh * 64:(hh + 1) * 64, 0:128],
                                     [(kdT.t[cp, ti, hh * 64:(hh + 1) * 64], vseg.t[cp, ti, hh * 128:(hh + 1) * 128])],
                                     reads=[kdT, vseg], first=(hh == 0), start=True)
                            cidx = ti * 2 + c
                            nb = nxt(Sb[d])
                            k.op(k.dve, lambda e, cidx=cidx, nb=nb: e.scalar_tensor_tensor(
                                nb.t[:], S[d].t[:], dec.t[:, cidx:cidx + 1], pS.t[:, 0:128], ALU.mult, ALU.add),
                                 reads=[S[d], dec, pS], writes=[nb])
                            k.op(k.dve, lambda e, cidx=cidx: e.scalar_tensor_tensor(
                                S[d].t[:], S[d].t[:], dec.t[:, cidx:cidx + 1], pS.t[:, 0:128], ALU.mult, ALU.add),
                                 reads=[S[d], dec, pS], writes=[S[d]])
                            self._p4_sb[d] = nb
                        for hh in range(2):
                            ov = oacv[hh * (NLAT // 128) + acc0 + gt]
                            o_ap = oacc.t[:, hh, (acc0 + gt) * 128:(acc0 + gt + 1) * 128]
                            if add_to_acc:
                                k.op(k.dve, lambda e, hh=hh, o_ap=o_ap: e.tensor_tensor(o_ap, pO[hh].t[:, 0:128], o_ap, ALU.add),
                                     reads=[pO[hh]], writes=[ov])
                            else:
                                k.op(k.act, lambda e, hh=hh, o_ap=o_ap: e.activation(o_ap, pO[hh].t[:, 0:128], AF.Copy),
                                     reads=[pO[hh]], writes=[ov])

                gens = []
                for sg in segs:
                    gens.append(seg_body(sg, bsets[self._p4_unit % 2]))
                    self._p4_unit += 1
                next(gens[0])
                for i in range(len(gens)):
                    if i + 1 < len(gens):
                        next(gens[i + 1])
                    for _ in gens[i]:
                        pass

            def finalize(pr, t0, ntok, acc0):
                for b0 in range(0, ntok, 512):
                    n = min(512, ntok - b0)
                    tsl = slice(t0 + b0, t0 + b0 + n)
                    k.dma(rseg, rseg.t[:, :, :n], self.GR, self.GR.t[2 * pr:2 * pr + 2, :, tsl].rearrange("j p t -> p j t"))
                    for hh in range(2):
                        a0 = acc0 * 128 + b0
                        o_ap = oacc.t[:, hh, a0:a0 + n]
                        ovs = [oacv[hh * (NLAT // 128) + acc0 + b0 // 128 + i] for i in range(n // 128)]
                        s_ = nxt(sq)
                        k.op(k.act, lambda e, s_=s_, o_ap=o_ap: e.activation(s_.t[:, :n], o_ap, AF.Square), reads=ovs, writes=[s_])
                        k.mm(pN, pN.t[:, :n], [(self.ones_b.t[:], s_.t[:, :n])], reads=[self.ones_b, s_])
                        k.op(k.act, lambda e: e.activation(rs.t[:, :n], pN.t[:, :n], AF.Sqrt, bias=self.eps_c.t[:, 0:1], scale=1.0 / 128),
                             reads=[pN, self.eps_c], writes=[rs])
                        k.op(k.dve, lambda e: e.reciprocal(rs.t[:, :n], rs.t[:, :n]), reads=[rs], writes=[rs])
                        k.op(k.dve, lambda e, o_ap=o_ap, hh=hh: e.scalar_tensor_tensor(
                            rs.t[:, :n], o_ap, self.vec("g_gla_out", l, 2 * pr + hh), rs.t[:, :n], ALU.mult, ALU.mult),
                             reads=ovs + [rs, self.vecT], writes=[rs])
                        y_ = nxt(yo)
                        k.op(k.dve, lambda e, y_=y_, hh=hh: e.tensor_tensor(y_.t[:, :n], rs.t[:, :n], rseg.t[:, hh, :n], ALU.mult),
                             reads=[rs, rseg], writes=[y_])
                        k.dma(self.MIX, self.MIX.t[12 + 2 * pr + hh, :, tsl], y_, y_.t[:, :n])

            for pr in range(2):
                self._p4_sb = [None, None]
                for d in range(2):
                    k.op(k.dve, lambda e, d=d: e.memset(S[d].t[:], 0.0), writes=[S[d]])
                    nb = nxt(Sb[d])
                    k.op(k.dve, lambda e, nb=nb: e.memset(nb.t[:], 0.0), writes=[nb])
                    self._p4_sb[d] = nb
                gla_pass(pr, 0, NLAT, NCTX, False, 0)
                gla_pass(pr, 1, NLAT, NCTX, True, 0)
                if not last:
                    finalize(pr, NLAT, NCTX, 0)
                gla_pass(pr, 0, 0, NLAT, False, 0)
                gla_pass(pr, 1, 0, NLAT, True, 0)
                finalize(pr, 0, NLAT, 0)
                self.conv_step(4)

    def p56(self, l):
        k = self.k
        last = (l == DEPTH - 1)
        final = (l == self.L - 1)
        with contextlib.ExitStack() as st:
            xb = k.sb(st, "p5_xb", (128, 16, 512), F32)
            xbv = k.views(xb, 16)
            mh = k.sb(st, "p5_mh", (128, 16, 512), BF16)
            mhv = k.views(mh, 16)
            aT = k.sb(st, "p5_aT", (128, NFC, 512), BF16)
            aTv = k.views(aT, NFC)
            wo = [k.sb(st, f"p5_wo{i}", (128, 16, 128), BF16) for i in range(3)]
            wg = [k.sb(st, f"p5_wg{i}", (128, 16, 128), BF16) for i in range(2)]
            wu = [k.sb(st, f"p5_wu{i}", (128, 16, 128), BF16) for i in range(2)]
            wd = [k.sb(st, f"p5_wd{i}", (128, NFC, 128), BF16) for i in range(2)]
            sg = [k.sb(st, f"p5_sg{i}", (128, 512), F32) for i in range(2)]
            sqb = [k.sb(st, f"p5_sq{i}", (128, 512), BF16) for i in range(4)]
            tmpf = [k.sb(st, f"p5_tmp{i}", (128, 512), F32) for i in range(3)]
            rs = k.sb(st, "p5_rs", (128, 512), F32)
            ostg = [k.sb(st, f"p5_os{i}", (128, D), F32) for i in range(2)] if final else None
            pA = [k.ps(st, f"p5_pA{i}", (128, 512)) for i in range(2)]
            pG = [k.ps(st, f"p5_pG{i}", (128, 512)) for i in range(2)]
            pU = [k.ps(st, f"p5_pU{i}", (128, 512)) for i in range(2)]
            pS = k.ps(st, "p5_pS", (128, 512))
            cnt = {}

            def nxt(lst):
                key = id(lst)
                cnt[key] = cnt.get(key, 0) + 1
                return lst[(cnt[key] - 1) % len(lst)]

            for bi, (t0, n) in enumerate(BLOCKS):
                isc = 1 if t0 >= NLAT else 0
                if isc and last:
                    continue
                tsl = slice(t0, t0 + n)
                nt = n // 128
                k.dma(xb, xb.t[:, :, :n], self.XT, self.XT.t[:, :, tsl].rearrange("kc p t -> p kc t"))
                for b_ in xbv:
                    b_.w = list(xb.w)
                for b_ in mhv:
                    mh.r = _compact(mh.r + b_.r + b_.w)
                k.dma(mh, mh.t[:, :, :n], self.MIX, self.MIX.t[:, :, tsl].rearrange("kc p t -> p kc t"))
                for b_ in mhv:
                    b_.w = list(mh.w)
                    b_.r = []
                for oc in range(16):
                    w = nxt(wo)
                    k.dma(w, w.t[:], self.WOUT[l], self.WOUT[l].t[oc])
                    p = nxt(pA)
                    k.mm(p, p.t[:, :n], [(w.t[:, kc, :], mh.t[:, kc, :n]) for kc in range(16)], reads=[w] + mhv)
                    g_ap = self.modT[l].t[:, 32 + oc, isc:isc + 1]
                    k.op(k.dve, lambda e, p=p, oc=oc, g_ap=g_ap: e.scalar_tensor_tensor(
                        xb.t[:, oc, :n], p.t[:, :n], g_ap, xb.t[:, oc, :n], ALU.mult, ALU.add),
                         reads=[p, self.modT[l]], writes=[xbv[oc]])
                self.rms_rstd(xb, xbv, 16, n, D, sqb, pS, rs)
                for kc in range(16):
                    tm_ = nxt(tmpf)
                    k.op(k.dve, lambda e, kc=kc, tm_=tm_: e.scalar_tensor_tensor(
                        tm_.t[:, :n], xb.t[:, kc, :n], self.Affn[l].t[:, kc, isc:isc + 1], rs.t[:, :n], ALU.mult, ALU.mult),
                         reads=[xbv[kc], rs, self.Affn[l]], writes=[tm_])
                    sh_ap = self.modT[l].t[:, 48 + kc, isc:isc + 1]
                    k.op(k.act, lambda e, kc=kc, tm_=tm_, sh_ap=sh_ap: e.activation(
                        mh.t[:, kc, :n], tm_.t[:, :n], AF.Identity, bias=sh_ap), reads=[tm_, self.modT[l]], writes=[mhv[kc]])
                for fc in range(NFC):
                    w1, w2 = nxt(wg), nxt(wu)
                    k.dma(w1, w1.t[:], self.WGU[l], self.WGU[l].t[fc])
                    k.dma(w2, w2.t[:], self.WGU[l], self.WGU[l].t[NFC + fc])
                    p1, p2 = nxt(pG), nxt(pU)
                    k.mm(p1, p1.t[:, :n], [(w1.t[:, kc, :], mh.t[:, kc, :n]) for kc in range(16)], reads=[w1] + mhv)
                    k.mm(p2, p2.t[:, :n], [(w2.t[:, kc, :], mh.t[:, kc, :n]) for kc in range(16)], reads=[w2] + mhv)
                    s_ = nxt(sg)
                    k.op(k.act, lambda e, s_=s_, p1=p1: e.activation(s_.t[:, :n], p1.t[:, :n], AF.Silu), reads=[p1], writes=[s_])
                    k.op(k.dve, lambda e, s_=s_, p2=p2, fc=fc: e.tensor_tensor(aT.t[:, fc, :n], s_.t[:, :n], p2.t[:, :n], ALU.mult),
                         reads=[s_, p2], writes=[aTv[fc]])
                for oc in range(16):
                    w = nxt(wd)
                    k.dma(w, w.t[:], self.WDN[l], self.WDN[l].t[oc])
                    p = nxt(pA)
                    k.mm(p, p.t[:, :n], [(w.t[:, fc, :], aT.t[:, fc, :n]) for fc in range(NFC)], reads=[w] + aTv)
                    g_ap = self.modT[l].t[:, 80 + oc, isc:isc + 1]
                    k.op(k.dve, lambda e, p=p, oc=oc, g_ap=g_ap: e.scalar_tensor_tensor(
                        xb.t[:, oc, :n], p.t[:, :n], g_ap, xb.t[:, oc, :n], ALU.mult, ALU.add),
                         reads=[p, self.modT[l]], writes=[xbv[oc]])
                if not final or "XT" in self.dbg:
                    k.dma(self.XT, self.XT.t[:, :, tsl].rearrange("kc p t -> p kc t"), xbv, xb.t[:, :, :n])
                if final and not isc:
                    self.rms_rstd(xb, xbv, 16, n, D, sqb, pS, rs)
                    for kc in range(16):
                        k.op(k.dve, lambda e, kc=kc: e.scalar_tensor_tensor(
                            xb.t[:, kc, :n], xb.t[:, kc, :n], self.vec("g_final", None, kc), rs.t[:, :n], ALU.mult, ALU.mult),
                             reads=[rs, self.vecT], writes=[xbv[kc]])
                    for ti in range(nt):
                        og = nxt(ostg)
                        for q4 in range(4):
                            p = nxt(pA)
                            for j in range(4):
                                kc = q4 * 4 + j
                                k.mm(p, p.t[:, j * 128:(j + 1) * 128], [(xb.t[:, kc, ti * 128:(ti + 1) * 128], self.ident_f.t[:])],
                                     reads=[xbv[kc], self.ident_f], transpose=True, first=(j == 0))
                            if q4 % 2 == 0:
                                k.op(k.act, lambda e, p=p, og=og, q4=q4: e.activation(og.t[:, q4 * 512:(q4 + 1) * 512], p.t[:, :], AF.Copy),
                                     reads=[p], writes=[og])
                            else:
                                k.op(k.dve, lambda e, p=p, og=og, q4=q4: e.tensor_copy(og.t[:, q4 * 512:(q4 + 1) * 512], p.t[:, :]),
                                     reads=[p], writes=[og])
                        k.dma(self.out, self.out.t[t0 + ti * 128:t0 + (ti + 1) * 128, :], og, og.t[:, :])
                for b_ in xbv:
                    xb.r = _compact(xb.r + b_.r + b_.w)
                    b_.r = []
                for b_ in aTv:
                    pass

    def build(self):
        k = self.k
        self.conv_rate = 0
        with contextlib.ExitStack() as st:
            self.consts(st)
            self.conv_setup(st)
            self.conv_until(0, 0)
            self.cv_throttle = True
            self.p0_transpose_in()
            self.p0_mod(st)
            if self.stop_after == "p0":
                return self.finish()
            for l in range(self.L):
                self.conv_until(l, 0)
                if self.stop_after == "conv":
                    return self.finish()
                if "skip_p1" not in self.dbg:
                    self.p1(l)
                if self.stop_after == "p1":
                    return self.finish()
                if "skip_p2" not in self.dbg:
                    self.p2(l)
                if self.stop_after == "p2":
                    return self.finish()
                if "skip_p3" not in self.dbg:
                    self.p3(l)
                if self.stop_after == "p3":
                    return self.finish()
                if "skip_p4" not in self.dbg:
                    self.p4(l)
                if self.stop_after == "p4":
                    return self.finish()
                self.conv_until(l, 1)
                self.p56(l)
            return self.finish()

    def finish(self):
        k = self.k
        toks = []
        for b in [self.out, self.XT, self.QN, self.QR, self.KN, self.KR, self.VM, self.SQ, self.SK, self.SV,
                  self.GQ, self.GK, self.GV, self.GLF, self.GLB, self.GR, self.MIX]:
            toks += b.w
        k.sp.wait(toks)
        k.sp.wait([(k.pe, k.pe.cnt), (k.act, k.act.cnt), (k.dve, k.dve.cnt), (k.pool, k.pool.cnt)])


def build_nc(n_layers=DEPTH, dbg=(), stop_after=None):
    nc = bass.Bass("TRN2", target_bir_lowering=False)
    with contextlib.ExitStack() as es:
        prog = Prog(nc, es, n_layers=n_layers, dbg=dbg, stop_after=stop_after)
        prog.build()
    return nc


def make_in_maps(inputs, cores):
    consts = host_consts()
    maps = []
    for b in cores:
        m = {
            "x": np.ascontiguousarray(inputs["x"][b]),
            "c": np.ascontiguousarray(inputs["c"][b]).reshape(16, 128),
            "ctx": np.ascontiguousarray(inputs["ctx"][b]),
            "c_ctx": np.ascontiguousarray(inputs["c_ctx"]).reshape(16, 128),
        }
        for n in W_SHAPES:
            m[n] = np.ascontiguousarray(inputs[n])
        m.update(consts)
        maps.append(m)
    return maps


def kernel(**inputs):
    inputs = {k_: np.asarray(v) for k_, v in inputs.items()}
    nc = build_nc()
    maps = make_in_maps(inputs, list(range(8)))
    res = run_bass_kernel_spmd(nc, maps, core_ids=list(range(8)))
    return np.stack([np.asarray(r["out"]) for r in res.results], axis=0).astype(np.float32)
```

```python
import contextlib
import numpy as np
import concourse.bass as bass
import concourse.mybir as mybir
from concourse.bass_utils import run_bass_kernel_spmd

F32 = mybir.dt.float32
BF16 = mybir.dt.bfloat16
AF = mybir.ActivationFunctionType
ALU = mybir.AluOpType

D = 2048
NLAT = 4096
NCTX = 256
T = NLAT + NCTX
DEPTH = 4
EPS = 1e-6
FFN = 5632
NFC = FFN // 128
BLOCKS = [(i * 512, 512) for i in range(8)] + [(NLAT, NCTX)]

C_CQ, C_CKV, C_KR, C_SQ, C_SK, C_SV, C_GQ, C_GK, C_GV, C_GLR, C_R = (
    0, 512, 1024, 1088, 1856, 2112, 2368, 2624, 2880, 3392, 3424)
IN_W = 3936
OC_CQ, OC_CKV, OC_KR, OC_SQ, OC_SK, OC_GQ, OC_GK, OC_R, OC_GLF, OC_GLB = 0, 4, 8, 9, 15, 17, 19, 21, 25, 26
N_OC = 27


class DSem:
    def __init__(self, handle):
        self.h = handle
        self.cnt = 0
        self.is_dma = True


class Eng:
    def __init__(self, name, e, sem, is_pe=False):
        self.name = name
        self.e = e
        self.h = sem
        self.cnt = 0
        self.is_dma = False
        self.is_pe = is_pe
        self.seen = {}

    def wait(self, toks):
        best = {}
        for t in toks:
            if t is None:
                continue
            s, v = t
            if s.is_dma:
                v = s.cnt
            elif s is self and self.is_pe:
                continue
            if v > best.get(id(s), (None, 0))[1]:
                best[id(s)] = (s, v)
        for s, v in best.values():
            if self.seen.get(id(s), 0) >= v:
                continue
            self.e.wait_ge(s.h, v)
            self.seen[id(s)] = v

    def done(self, ins):
        self.cnt += 1
        ins.then_inc(self.h, 1)
        return (self, self.cnt)


class Buf:
    def __init__(self, t, name="", dram=False):
        self.t = t
        self.name = name
        self.w = []
        self.r = []
        self.dsem = None
        self.dram = dram
        self.psum = False


def _compact(toks):
    best = {}
    for t in toks:
        if t is None:
            continue
        s, v = t
        if id(s) not in best or best[id(s)][1] < v:
            best[id(s)] = (s, v)
    return list(best.values())


class K:
    def __init__(self, nc, es):
        self.nc = nc
        self.es = es
        mk = lambda n: es.enter_context(nc.semaphore(n))
        self.pe = Eng("pe", nc.tensor, mk("m_pe"), is_pe=True)
        self.act = Eng("act", nc.scalar, mk("m_act"))
        self.dve = Eng("dve", nc.vector, mk("m_dve"))
        self.pool = Eng("pool", nc.gpsimd, mk("m_pool"))
        self.sp = Eng("sp", nc.sync, None)
        self.free_dsems = []
        self.n_dsems = 0
        self.freed = []

    def new_dsem(self):
        if self.free_dsems:
            return self.free_dsems.pop()
        self.n_dsems += 1
        return DSem(self.es.enter_context(self.nc.semaphore(f"d{self.n_dsems}")))

    def release(self, bufs):
        for b in bufs:
            toks = list(b.w) + list(b.r)
            for c in getattr(b, "children", []):
                toks += list(c.w) + list(c.r)
            self.freed = _compact(self.freed + toks)
            if b.dsem is not None:
                self.free_dsems.append(b.dsem)
                b.dsem = None

    def dram(self, name, shape, dt, kind="Internal"):
        t = self.nc.dram_tensor(name, list(shape), dt, kind=kind)
        return Buf(t.ap(), name, dram=True)

    def sb(self, st, name, shape, dt):
        self.uid = getattr(self, "uid", 0) + 1
        name = f"{name}_u{self.uid}"
        b = Buf(st.enter_context(self.nc.sbuf_tensor(name, list(shape), dt)), name)
        b.r = list(self.freed)
        st.callback(lambda: self.release([b]))
        return b

    def ps(self, st, name, shape, dt=F32):
        self.uid = getattr(self, "uid", 0) + 1
        name = f"{name}_u{self.uid}"
        b = Buf(st.enter_context(self.nc.psum_tensor(name, list(shape), dt)), name)
        b.psum = True
        b.r = list(self.freed)
        st.callback(lambda: self.release([b]))
        return b

    def op(self, eng, fn, reads=(), writes=()):
        deps = []
        for b in reads:
            deps += b.w
            if b.psum:
                deps += b.r
        for b in writes:
            deps += b.w
            deps += b.r
        eng.wait(deps)
        tok = eng.done(fn(eng.e))
        for b in writes:
            b.w = [tok]
            b.r = []
        for b in reads:
            if b in writes:
                continue
            b.r = _compact(b.r + [tok])
        return tok

    def mm(self, out_buf, out_ap, terms, reads=(), first=True, last=True, transpose=False, start=None, mark=True):
        pe = self.pe
        deps = []
        if first:
            deps += list(out_buf.w) + list(out_buf.r)
        for b in reads:
            deps += b.w
        pe.wait(deps)
        n = len(terms)
        ins = None
        for i, (l, r) in enumerate(terms):
            if transpose:
                ins = pe.e.transpose(out_ap, l, r)
            else:
                st_ = (first and i == 0) if start is None else (start and i == 0)
                ins = pe.e.matmul(out_ap, l, r, start=st_, stop=(last and i == n - 1))
        if not mark:
            if first:
                out_buf.r = []
            return None
        tok = pe.done(ins)
        out_buf.w = [tok]
        if first:
            out_buf.r = []
        for b in reads:
            b.r = _compact(b.r + [tok])
        return tok

    def dma(self, out_buf, out_ap, in_buf, in_ap, q=None, more=False):
        q = q or self.sp
        in_bufs = in_buf if isinstance(in_buf, (list, tuple)) else [in_buf]
        deps = []
        for ib in in_bufs:
            deps += list(ib.w)
        if not out_buf.dram and not out_buf.w:
            more = False
        if not out_buf.dram and not more:
            deps += list(out_buf.w) + list(out_buf.r)
        q.wait(deps)
        if out_buf.dsem is None:
            out_buf.dsem = self.new_dsem()
        s = out_buf.dsem
        s.cnt += 16
        q.e.dma_start(out=out_ap, in_=in_ap).then_inc(s.h, 16)
        tok = (s, s.cnt)
        out_buf.w = [tok]
        if not out_buf.dram and not more:
            out_buf.r = []
        for ib in in_bufs:
            if not ib.dram:
                ib.r = _compact(ib.r + [tok])
        return tok

    def views(self, buf, n):
        vs = [Buf(buf.t, f"{buf.name}.{i}") for i in range(n)]
        for v in vs:
            v.r = list(buf.r)
            v.w = list(buf.w)
            v.psum = buf.psum
        buf.children = getattr(buf, "children", []) + vs
        return vs


def _rope_tables():
    t = np.arange(NLAT)
    rows = (t // 64).astype(np.float32)
    cols = (t % 64).astype(np.float32)

    def tab(dh):
        half = dh // 2
        freqs = (10000.0 ** (-np.arange(half, dtype=np.float32) / half)).astype(np.float32)
        cos = np.ones((2 * dh, T), np.float32)
        sin = np.zeros((2 * dh, T), np.float32)
        for a, pos in enumerate((rows, cols)):
            ang = (pos[None, :] * freqs[:, None]).astype(np.float32)
            c, s = np.cos(ang).astype(np.float32), np.sin(ang).astype(np.float32)
            base = a * dh
            cos[base:base + half, :NLAT] = c
            cos[base + half:base + dh, :NLAT] = c
            sin[base:base + half, :NLAT] = -s
            sin[base + half:base + dh, :NLAT] = s
        return cos, sin

    cm, sm = tab(32)
    cs, ss = tab(64)
    cm = np.concatenate([cm, cm], 0)
    sm = np.concatenate([sm, sm], 0)
    return np.ascontiguousarray(cm), np.ascontiguousarray(sm), cs, ss


def _perm(dh, n=128):
    half = dh // 2
    p = np.zeros((n, n), np.float32)
    for m in range(n):
        g, o = divmod(m, dh)
        k = g * dh + (o + half) % dh
        p[k, m] = 1.0
    return p


def _swa_masks():
    m = np.zeros((6, 128, 512), np.float32)
    for r in range(6):
        kpos = (r - 1) * 128 + np.arange(128)[:, None]
        qpos = np.arange(512)[None, :]
        ok = np.abs(qpos - kpos) <= 128
        m[r] = np.where(ok, 0.0, -30000.0)
    return m


def _gla_masks():
    tp = np.arange(128)[:, None]
    t = np.arange(128)[None, :]
    same = (tp // 64) == (t // 64)
    m = np.zeros((2, 128, 128), np.float32)
    m[0] = (same & (tp <= t)).astype(np.float32)
    m[1] = (same & (tp > t)).astype(np.float32)
    return m


def host_consts():
    cm, sm, cs, ss = _rope_tables()
    return {
        "k_ident": np.eye(128, dtype=np.float32),
        "k_permm": _perm(32),
        "k_perms": _perm(64),
        "k_cosm": cm, "k_sinm": sm, "k_coss": cs, "k_sins": ss,
        "k_mask": _swa_masks(),
        "k_gmask": _gla_masks(),
    }


W_SHAPES = {
    "w_mod": (DEPTH, D, 6 * D), "b_mod": (DEPTH, 6 * D), "g_mix": (DEPTH, D), "g_ffn": (DEPTH, D),
    "w_in": (DEPTH, D, IN_W), "g_mla_q": (DEPTH, 512), "g_mla_kv": (DEPTH, 512),
    "w_mla_uq": (DEPTH, 512, 1152), "w_mla_ukv": (DEPTH, 512, 1536), "swa_sink": (DEPTH, 6),
    "w_gla_gate_f": (DEPTH, 16, 256), "b_gla_gate_f": (DEPTH, 256),
    "w_gla_gate_b": (DEPTH, 16, 256), "b_gla_gate_b": (DEPTH, 256), "g_gla_out": (DEPTH, 512),
    "w_out": (DEPTH, D, D), "w_ffn_gu": (DEPTH, D, 2 * FFN), "w_ffn_down": (DEPTH, FFN, D),
    "g_final": (D,),
}


class Prog:
    def __init__(self, nc, es, n_layers=DEPTH, dbg=(), stop_after=None):
        self.nc = nc
        self.es = es
        self.k = K(nc, es)
        self.L = n_layers
        self.dbg = set(dbg)
        self.stop_after = stop_after
        import os
        self.p1_stage = int(os.environ.get("P1_STAGE", "0"))
        k = self.k
        ein = lambda n, s: k.dram(n, s, F32, kind="ExternalInput")
        self.x = ein("x", (NLAT, D))
        self.c = ein("c", (16, 128))
        self.ctx = ein("ctx", (NCTX, D))
        self.c_ctx = ein("c_ctx", (16, 128))
        self.W = {n: ein(n, s) for n, s in W_SHAPES.items()}
        self.C = {n: ein(n, v.shape) for n, v in host_consts().items()}
        self.out = k.dram("out", (NLAT, D), F32, kind="ExternalOutput")
        L = self.L

        def scr(n, s, dt):
            return k.dram(n, s, dt, kind=("ExternalOutput" if n in self.dbg else "Internal"))

        self.XT = scr("XT", (16, 128, T), F32)
        self.QN = scr("QN", (6, 128, T), BF16)
        self.QR = scr("QR", (3, 128, T), BF16)
        self.KN = scr("KN", (6, 128, T), BF16)
        self.KR = scr("KR", (128, T), BF16)
        self.VM = scr("VM", (T, 768), BF16)
        self.SQ = scr("SQ", (6, 128, T), BF16)
        self.SK = scr("SK", (2, 128, T), BF16)
        self.SV = scr("SV", (T, 256), BF16)
        self.GQ = scr("GQ", (2, 128, T), F32)
        self.GK = scr("GK", (2, 128, T), F32)
        self.GV = scr("GV", (T, 512), BF16)
        self.GLF = scr("GLF", (2, 128, T), F32)
        self.GLB = scr("GLB", (2, 128, T), F32)
        self.GR = scr("GR", (4, 128, T), F32)
        self.MIX = scr("MIX", (16, 128, T), BF16)
        self.WINFM = [scr(f"WINFM{l}", (N_OC, 128, 16, 128), BF16) for l in range(L)]
        self.WINTM = [scr(f"WINTM{l}", (128, 16, 768), BF16) for l in range(L)]
        self.WUQ = [scr(f"WUQ{l}", (9, 128, 4, 128), BF16) for l in range(L)]
        self.WUKVFM = [scr(f"WUKVFM{l}", (6, 128, 4, 128), BF16) for l in range(L)]
        self.WUKVTM = [scr(f"WUKVTM{l}", (128, 4, 768), BF16) for l in range(L)]
        self.WOUT = [scr(f"WOUT{l}", (16, 128, 16, 128), BF16) for l in range(L)]
        self.WGU = [scr(f"WGU{l}", (88, 128, 16, 128), BF16) for l in range(L)]
        self.WDN = [scr(f"WDN{l}", (16, 128, NFC, 128), BF16) for l in range(L)]
        for l in range(L):
            sh = k.new_dsem()
            for b in (self.WINFM[l], self.WINTM[l], self.WUQ[l], self.WUKVFM[l], self.WUKVTM[l],
                      self.WOUT[l], self.WGU[l], self.WDN[l]):
                b.dsem = sh

    def consts(self, st):
        k = self.k
        self.ident_f = k.sb(st, "ident_f", (128, 128), F32)
        self.ident_b = k.sb(st, "ident_b", (128, 128), BF16)
        self.ones_b = k.sb(st, "ones_b", (128, 128), BF16)
        self.ones_f = k.sb(st, "ones_f", (128, 128), F32)
        self.permm_b = k.sb(st, "permm_b", (128, 128), BF16)
        self.perms_b = k.sb(st, "perms_b", (128, 128), BF16)
        self.mask_b = k.sb(st, "mask_b", (128, 6, 512), BF16)
        with contextlib.ExitStack() as tmp:
            pf = k.sb(tmp, "c_pf", (128, 2, 128), F32)
            mf = k.sb(tmp, "c_mf", (128, 6, 512), F32)
            k.dma(self.ident_f, self.ident_f.t[:], self.C["k_ident"], self.C["k_ident"].t[:, :])
            k.dma(pf, pf.t[:, 0, :], self.C["k_permm"], self.C["k_permm"].t[:, :])
            k.dma(pf, pf.t[:, 1, :], self.C["k_perms"], self.C["k_perms"].t[:, :], more=True)
            k.dma(mf, mf.t[:], self.C["k_mask"], self.C["k_mask"].t.rearrange("r k q -> k r q"))
            k.op(k.dve, lambda e: e.tensor_copy(self.ident_b.t[:], self.ident_f.t[:]),
                 reads=[self.ident_f], writes=[self.ident_b])
            k.op(k.dve, lambda e: e.tensor_copy(self.permm_b.t[:], pf.t[:, 0, :]), reads=[pf], writes=[self.permm_b])
            k.op(k.dve, lambda e: e.tensor_copy(self.perms_b.t[:], pf.t[:, 1, :]), reads=[pf], writes=[self.perms_b])
            k.op(k.dve, lambda e: e.tensor_copy(self.mask_b.t[:], mf.t[:]), reads=[mf], writes=[self.mask_b])
            k.op(k.dve, lambda e: e.memset(self.ones_b.t[:], 1.0), writes=[self.ones_b])
            k.op(k.dve, lambda e: e.memset(self.ones_f.t[:], 1.0), writes=[self.ones_f])
        L = self.L
        self.vecT = k.sb(st, "vecT", (128, 2, 128), F32)
        self.sinkb = k.sb(st, "sinkb", (128, 4 * 6), F32)
        self.eps_c = k.sb(st, "eps_c", (128, 1), F32)
        self.one_c = k.sb(st, "one_c", (128, 1), F32)
        self.negb = [k.sb(st, f"negb{l}", (128, 4), F32) for l in range(L)]
        W = self.W
        with contextlib.ExitStack() as tmp:
            vr = [k.sb(tmp, f"vrows{i}", (128, 128), F32) for i in range(2)]
            self.vcol = {}
            items = []
            for l in range(L):
                for nm, nr in (("g_mix", 16), ("g_ffn", 16), ("g_mla_q", 4), ("g_mla_kv", 4),
                               ("g_gla_out", 4), ("b_gla_gate_f", 2), ("b_gla_gate_b", 2)):
                    items.append((nm, l, nr))
            items.append(("g_final", None, 16))
            row = 0
            for nm, l, nr in items:
                g, r0 = divmod(row, 128)
                if r0 + nr > 128:
                    row = (g + 1) * 128
                    g, r0 = divmod(row, 128)
                src = W[nm].t[l] if l is not None else W[nm].t
                k.dma(vr[g], vr[g].t[r0:r0 + nr, :], W[nm], src.rearrange("(r c) -> r c", c=128), more=True)
                self.vcol[(nm, l)] = (g, r0)
                row += nr
            assert row <= 256
            with contextlib.ExitStack() as pst:
                pp = k.ps(pst, "c_pp", (128, 512))
                for g in range(2):
                    k.mm(pp, pp.t[:, g * 128:(g + 1) * 128], [(vr[g].t[:], self.ident_f.t[:])],
                         reads=[vr[g], self.ident_f], transpose=True)
                k.op(k.dve, lambda e: e.tensor_copy(self.vecT.t[:].rearrange("p g c -> p (g c)"), pp.t[:, 0:256]),
                     reads=[pp], writes=[self.vecT])
            k.op(k.dve, lambda e: e.memset(self.eps_c.t[:], EPS), writes=[self.eps_c])
            k.op(k.dve, lambda e: e.memset(self.one_c.t[:], 1.0), writes=[self.one_c])
            for l in range(L):
                for d_, nm in enumerate(("b_gla_gate_f", "b_gla_gate_b")):
                    g, r0 = self.vcol[(nm, l)]
                    k.op(k.dve, lambda e, l=l, d_=d_, g=g, r0=r0: e.tensor_scalar(
                        self.negb[l].t[:, d_ * 2:d_ * 2 + 2], self.vecT.t[:, g, r0:r0 + 2], -1.0, None, ALU.mult),
                         reads=[self.vecT], writes=[self.negb[l]])
            k.dma(self.sinkb, self.sinkb.t[:], W["swa_sink"],
                  W["swa_sink"].t.rearrange("l h -> (l h)").partition_broadcast(128))
            k.op(k.act, lambda e: e.activation(self.sinkb.t[:], self.sinkb.t[:], AF.Exp),
                 reads=[self.sinkb], writes=[self.sinkb])

    def vec(self, nm, l, j):
        g, r0 = self.vcol[(nm, l)]
        return self.vecT.t[:, g, r0 + j:r0 + j + 1]

    def p0_transpose_in(self):
        k = self.k
        with contextlib.ExitStack() as st:
            xin = [k.sb(st, f"p0_xin{i}", (128, 4, D), F32) for i in range(2)]
            xo = [k.sb(st, f"p0_xo{i}", (128, 16, 512), F32) for i in range(2)]
            pp = [k.ps(st, f"p0_pp{i}", (128, 512)) for i in range(4)]
            xovs = [k.views(b, 16) for b in xo]
            cnt = 0
            for bi, (t0, n) in enumerate(BLOCKS):
                xi = xin[bi % 2]
                xob = xo[bi % 2]
                xov = xovs[bi % 2]
                nt = n // 128
                if t0 < NLAT:
                    src, sb_ = self.x, self.x.t[t0:t0 + n, :]
                else:
                    src, sb_ = self.ctx, self.ctx.t[:, :]
                k.dma(xi, xi.t[:, 0:nt, :], src, sb_.rearrange("(a p) d -> p a d", p=128))
                for kc in range(16):
                    p = pp[cnt % 4]
                    for ti in range(nt):
                        k.mm(p, p.t[:, ti * 128:(ti + 1) * 128],
                             [(xi.t[:, ti, kc * 128:(kc + 1) * 128], self.ident_f.t[:])],
                             reads=[xi, self.ident_f], transpose=True, first=(ti == 0))
                    eng = k.dve if cnt % 2 == 0 else k.act
                    if eng is k.dve:
                        k.op(eng, lambda e, p=p, kc=kc: e.tensor_copy(xob.t[:, kc, :n], p.t[:, :n]), reads=[p], writes=[xov[kc]])
                    else:
                        k.op(eng, lambda e, p=p, kc=kc: e.copy(xob.t[:, kc, :n], p.t[:, :n]), reads=[p], writes=[xov[kc]])
                    cnt += 1
                k.dma(self.XT, self.XT.t[:, :, t0:t0 + n].rearrange("kc p t -> p kc t"), xov, xob.t[:, :, :n])

    def p0_mod(self, st):
        k = self.k
        L = self.L
        self.modT = [k.sb(st, f"modT{l}", (128, 96, 2), F32) for l in range(L)]
        self.Amix = [k.sb(st, f"Amix{l}", (128, 16, 2), F32) for l in range(L)]
        self.Affn = [k.sb(st, f"Affn{l}", (128, 16, 2), F32) for l in range(L)]
        with contextlib.ExitStack() as tmp:
            crow = k.sb(tmp, "m_crow", (32, 128), F32)
            sc = k.sb(tmp, "m_sc", (128, 16, 2), F32)
            wm = [k.sb(tmp, f"m_wm{i}", (128, 4096), F32) for i in range(3)]
            bm = k.sb(tmp, "m_bm", (2, 4096), F32)
            mrow = k.sb(tmp, "m_mrow", (2, 4096), F32)
            pp = [k.ps(tmp, f"m_pp{i}", (128, 512)) for i in range(8)]
            k.dma(crow, crow.t[0:16, :], self.c, self.c.t[:, :])
            k.dma(crow, crow.t[16:32, :], self.c_ctx, self.c_ctx.t[:, :], more=True)
            k.mm(pp[0], pp[0].t[:, 0:32], [(crow.t[:], self.ident_f.t[0:32, 0:32])], reads=[crow, self.ident_f], transpose=True)
            k.op(k.act, lambda e: e.activation(sc.t[:, :, 0], pp[0].t[:, 0:16], AF.Silu), reads=[pp[0]], writes=[sc])
            k.op(k.act, lambda e: e.activation(sc.t[:, :, 1], pp[0].t[:, 16:32], AF.Silu), reads=[pp[0]], writes=[sc])
            cnt = 0
            for l in range(L):
                for cg in range(3):
                    c0 = cg * 4096
                    k.dma(bm, bm.t[0:1, :], self.W["b_mod"], self.W["b_mod"].t[l:l + 1, c0:c0 + 4096])
                    k.dma(bm, bm.t[1:2, :], self.W["b_mod"], self.W["b_mod"].t[l:l + 1, c0:c0 + 4096], more=True)
                    for kc in range(16):
                        w = wm[cnt % 3]
                        cnt += 1
                        k.dma(w, w.t[:], self.W["w_mod"], self.W["w_mod"].t[l, kc * 128:(kc + 1) * 128, c0:c0 + 4096])
                        for j in range(8):
                            k.mm(pp[j], pp[j].t[0:2, :], [(sc.t[:, kc, :], w.t[:, j * 512:(j + 1) * 512])],
                                 reads=[sc, w], first=(kc == 0), last=(kc == 15))
                    for j in range(8):
                        k.op(k.dve, lambda e, j=j: e.tensor_tensor(mrow.t[0:2, j * 512:(j + 1) * 512], pp[j].t[0:2, :],
                                                                   bm.t[0:2, j * 512:(j + 1) * 512], ALU.add),
                             reads=[pp[j], bm], writes=[mrow])
                    for jj in range(32):
                        k.mm(pp[0], pp[0].t[:, jj * 2:jj * 2 + 2],
                             [(mrow.t[0:2, jj * 128:(jj + 1) * 128], self.ident_f.t[0:2, 0:2])],
                             reads=[mrow, self.ident_f], transpose=True, first=(jj == 0))
                    k.op(k.dve, lambda e, l=l, cg=cg: e.tensor_copy(
                        self.modT[l].t[:, cg * 32:(cg + 1) * 32, :].rearrange("p a b -> p (a b)"), pp[0].t[:, 0:64]),
                         reads=[pp[0]], writes=[self.modT[l]])
                for (A, gname, off) in ((self.Amix[l], "g_mix", 16), (self.Affn[l], "g_ffn", 64)):
                    g, r0 = self.vcol[(gname, l)]
                    for b in range(2):
                        k.op(k.dve, lambda e, A=A, b=b, off=off, g=g, r0=r0, l=l: e.scalar_tensor_tensor(
                            A.t[:, :, b], self.modT[l].t[:, off:off + 16, b], 1.0, self.vecT.t[:, g, r0:r0 + 16],
                            ALU.add, ALU.mult), reads=[self.modT[l], self.vecT], writes=[A])

    def conv_flush(self):
        for th in self.cv_pending:
            th()
        self.cv_pending = []

    def conv_pieces(self, l, part):
        k = self.k
        W = self.W
        FMv = lambda buf, a, b, kc: buf.t[a:b, :, kc, :].rearrange("oc p m -> p oc m")

        def piece(src_buf, src_ap, ncols, stores):
            if self.cv_throttle:
                k.pool.wait([(k.pe, k.pe.cnt)])
            for (c0, c1, dbuf, dap, three) in stores:
                s_ap = src_ap[:, c0:c1]
                if three:
                    s_ap = s_ap.rearrange("p (oc m) -> p oc m", m=128)
                k.dma(dbuf, dap, src_buf, s_ap, q=k.pool)

        if part == 0:
            yield from self._conv_early(l, piece, FMv)
        else:
            yield from self._conv_late(l, piece, FMv)
        self.conv_flush()

    def _conv_early(self, l, piece, FMv):
        W = self.W

        win = W["w_in"]
        FM, TM = self.WINFM[l], self.WINTM[l]
        for kc in range(16):
            rows = slice(kc * 128, (kc + 1) * 128)
            piece(win, win.t[l, rows, 0:1024], 1024, [(0, 1024, FM, FMv(FM, 0, 8, kc), True)])
            yield
            piece(win, win.t[l, rows, 1024:2112], 1088, [
                (0, 64, FM, FM.t[OC_KR, :, kc, 0:64], False),
                (0, 64, FM, FM.t[OC_KR, :, kc, 64:128], False),
                (64, 1088, FM, FMv(FM, OC_SQ, OC_SQ + 8, kc), True)])
            yield
            piece(win, win.t[l, rows, 2112:2880], 768, [
                (0, 256, TM, TM.t[:, kc, 0:256], False),
                (256, 768, FM, FMv(FM, OC_GQ, OC_GQ + 4, kc), True)])
            yield
            piece(win, win.t[l, rows, 2880:3936], 1056, [
                (0, 512, TM, TM.t[:, kc, 256:768], False),
                (512, 528, FM, FM.t[OC_GLF, :, kc, 0:16], False),
                (528, 544, FM, FM.t[OC_GLB, :, kc, 0:16], False),
                (544, 1056, FM, FMv(FM, OC_R, OC_R + 4, kc), True)])
            yield
        wq = W["w_mla_uq"]
        for kc in range(4):
            rows = slice(kc * 128, (kc + 1) * 128)
            stores = []
            for h in range(6):
                stores.append((h * 192, h * 192 + 128, self.WUQ[l], self.WUQ[l].t[h, :, kc, :], False))
                stores.append((h * 192 + 128, h * 192 + 192, self.WUQ[l],
                               self.WUQ[l].t[6 + h // 2, :, kc, (h % 2) * 64:(h % 2) * 64 + 64], False))
            piece(wq, wq.t[l, rows, :], 1152, stores)
            yield
        wkv = W["w_mla_ukv"]
        for kc in range(4):
            rows = slice(kc * 128, (kc + 1) * 128)
            for half in range(2):
                stores = []
                for hh in range(3):
                    h = half * 3 + hh
                    stores.append((hh * 256, hh * 256 + 128, self.WUKVFM[l], self.WUKVFM[l].t[h, :, kc, :], False))
                    stores.append((hh * 256 + 128, hh * 256 + 256, self.WUKVTM[l],
                                   self.WUKVTM[l].t[:, kc, h * 128:(h + 1) * 128], False))
                piece(wkv, wkv.t[l, rows, half * 768:(half + 1) * 768], 768, stores)
                yield

    def _conv_late(self, l, piece, FMv):
        W = self.W
        wo = W["w_out"]
        for kc in range(16):
            rows = slice(kc * 128, (kc + 1) * 128)
            for j in range(2):
                piece(wo, wo.t[l, rows, j * 1024:(j + 1) * 1024], 1024,
                      [(0, 1024, self.WOUT[l], FMv(self.WOUT[l], j * 8, j * 8 + 8, kc), True)])
                yield
        wg = W["w_ffn_gu"]
        for kc in range(16):
            rows = slice(kc * 128, (kc + 1) * 128)
            for j in range(11):
                piece(wg, wg.t[l, rows, j * 1024:(j + 1) * 1024], 1024,
                      [(0, 1024, self.WGU[l], FMv(self.WGU[l], j * 8, j * 8 + 8, kc), True)])
                yield
        wd = W["w_ffn_down"]
        for kc in range(NFC):
            rows = slice(kc * 128, (kc + 1) * 128)
            for j in range(2):
                piece(wd, wd.t[l, rows, j * 1024:(j + 1) * 1024], 1024,
                      [(0, 1024, self.WDN[l], FMv(self.WDN[l], j * 8, j * 8 + 8, kc), True)])
                yield

    def conv_setup(self, st):
        k = self.k
        self.cv_i = 0
        self.cv_throttle = False
        self.cv_pending = []
        self.cv_marks = set()
        self.cv_gen = self.conv_all()

    def conv_all(self):
        for l in range(self.L):
            for part in range(2):
                yield from self.conv_pieces(l, part)
                self.cv_marks.add((l, part))

    def conv_until(self, l, part):
        while (l, part) not in self.cv_marks and self.cv_gen is not None:
            self.conv_step(1)

    def conv_step(self, n):
        if self.cv_gen is None:
            return
        for _ in range(n):
            try:
                next(self.cv_gen)
            except StopIteration:
                self.cv_gen = None
                self.conv_flush()
                return

    def conv_finish(self):
        self.conv_step(10 ** 9)

    def rms_rstd(self, src_tile, src_bufs, nchunks, n, dim, sqb, pS, rs, cnt0=0):
        k = self.k
        for j in range(nchunks):
            sq = sqb[(cnt0 + j) % len(sqb)]
            k.op(k.act, lambda e, j=j, sq=sq: e.activation(sq.t[:, :n], src_tile.t[:, j, :n], AF.Square),
                 reads=[src_bufs[j]], writes=[sq])
            k.mm(pS, pS.t[:, :n], [(self.ones_b.t[:], sq.t[:, :n])], reads=[self.ones_b, sq],
                 first=(j == 0), last=(j == nchunks - 1))
        k.op(k.act, lambda e: e.activation(rs.t[:, :n], pS.t[:, :n], AF.Sqrt, bias=self.eps_c.t[:, 0:1], scale=1.0 / dim),
             reads=[pS, self.eps_c], writes=[rs])
        k.op(k.dve, lambda e: e.reciprocal(rs.t[:, :n], rs.t[:, :n]), reads=[rs], writes=[rs])

    def p1(self, l):
        k = self.k
        W = self.W
        L = self.L
        with contextlib.ExitStack() as st:
            xb = k.sb(st, "p1_xb", (128, 16, 512), F32)
            xbv = k.views(xb, 16)
            hT = [k.sb(st, f"p1_hT{i}", (128, 16, 512), BF16) for i in range(2)]
            hTv = [k.views(b, 16) for b in hT]
            sqb = [k.sb(st, f"p1_sq{i}", (128, 512), BF16) for i in range(4)]
            tmpf = [k.sb(st, f"p1_tmp{i}", (128, 512), F32) for i in range(3)]
            rs = k.sb(st, "p1_rs", (128, 512), F32)
            rss = [k.sb(st, f"p1_rs2_{i}", (128, 512), F32) for i in range(2)]
            wfm = [k.sb(st, f"p1_wfm{i}", (128, 16, 128), BF16) for i in range(3)]
            wtm = k.sb(st, "p1_wtm", (128, 16, 768), BF16)
            wuq = k.sb(st, "p1_wuq", (128, 9, 4, 128), BF16)
            wkf = k.sb(st, "p1_wkf", (128, 6, 4, 128), BF16)
            wkt = k.sb(st, "p1_wkt", (128, 4, 768), BF16)
            wgf = k.sb(st, "p1_wgf", (16, 2, 256), F32)
            cqs = [k.sb(st, f"p1_cq{i}", (128, 4, 512), F32) for i in range(2)]
            cqvs = [k.views(b_, 4) for b_ in cqs]
            cqns = [k.sb(st, f"p1_cqn{i}", (128, 4, 512), BF16) for i in range(2)]
            tab = k.sb(st, "p1_tab", (128, 4, 512), F32)
            sbf = [k.sb(st, f"p1_sbf{i}", (128, 512), BF16) for i in range(4)]
            sff = [k.sb(st, f"p1_sff{i}", (128, 512), F32) for i in range(4)]
            stm = [k.sb(st, f"p1_stm{i}", (128, 768), BF16) for i in range(2)]
            glr = k.sb(st, "p1_glr", (16, 2, 512), F32)
            pA = [k.ps(st, f"p1_pA{i}", (128, 512)) for i in range(3)]
            pS = k.ps(st, "p1_pS", (128, 512))
            pW = k.ps(st, "p1_pW", (128, 512))
            pT = [k.ps(st, f"p1_pT{i}", (128, 512)) for i in range(2)]
            pG = k.ps(st, "p1_pG", (128, 512))

            k.dma(wtm, wtm.t[:], self.WINTM[l], self.WINTM[l].t[:, :, :])
            k.dma(wuq, wuq.t[:], self.WUQ[l], self.WUQ[l].t.rearrange("oc p kc m -> p oc kc m"))
            k.dma(wkf, wkf.t[:], self.WUKVFM[l], self.WUKVFM[l].t.rearrange("oc p kc m -> p oc kc m"))
            k.dma(wkt, wkt.t[:], self.WUKVTM[l], self.WUKVTM[l].t[:, :, :])
            k.dma(wgf, wgf.t[:, 0, :], W["w_gla_gate_f"], W["w_gla_gate_f"].t[l])
            k.dma(wgf, wgf.t[:, 1, :], W["w_gla_gate_b"], W["w_gla_gate_b"].t[l], more=True)

            cnt = {"a": 0, "w": 0, "s": 0, "f": 0, "e": 0}

            def nxt(key, lst):
                key = id(lst)
                cnt[key] = cnt.get(key, 0) + 1
                return lst[(cnt[key] - 1) % len(lst)]

            def evac_copy(p, n, dst_buf, dst_ap, func=None):
                cnt["e"] += 1
                if func is not None or cnt["e"] % 2 == 0:
                    f = func if func is not None else AF.Copy
                    k.op(k.act, lambda e: e.activation(dst_ap, p.t[:, :n], f), reads=[p], writes=[dst_buf])
                else:
                    k.op(k.dve, lambda e: e.tensor_copy(dst_ap, p.t[:, :n]), reads=[p], writes=[dst_buf])

            pend = []
            pcount = {"n": 0}

            def store(dst, dst_ap, sbuf, s_ap):
                pend.append((pcount["n"], sbuf, lambda: k.dma(dst, dst_ap, sbuf, s_ap)))

            def flush(upto_tag=None, buf=None):
                while pend:
                    tag, sb_, th = pend[0]
                    need = (upto_tag is not None and tag <= upto_tag) or \
                           (buf is not None and any(e[1] is buf for e in pend))
                    if not need:
                        break
                    pend.pop(0)
                    th()

            def stage(lst):
                b = nxt("x", lst)
                flush(buf=b)
                return b

            ld = {"n": 0, "use": 0}
            total_loads = len(BLOCKS) * N_OC

            def ensure(upto):
                while ld["n"] <= min(upto, total_loads - 1):
                    g = ld["n"]
                    w_ = wfm[g % 3]
                    k.dma(w_, w_.t[:], self.WINFM[l], self.WINFM[l].t[g % N_OC])
                    ld["n"] += 1

            tails = []

            def run_tails():
                while tails:
                    tails.pop(0)()

            def rope(p, n, ti, perm, dst, dst_ap):
                xr = stage(sbf)
                k.op(k.act, lambda e: e.activation(xr.t[:, :n], p.t[:, :n], AF.Copy), reads=[p], writes=[xr])
                run_tails()

                def tail():
                    k.mm(pW, pW.t[:, :n], [(perm.t[:], xr.t[:, :n])], reads=[perm, xr])
                    t1 = stage(sff)
                    k.op(k.dve, lambda e: e.tensor_tensor(t1.t[:, :n], p.t[:, :n], tab.t[:, ti, :n], ALU.mult),
                         reads=[p, tab], writes=[t1])
                    t2 = stage(sff)
                    k.op(k.dve, lambda e: e.tensor_tensor(t2.t[:, :n], pW.t[:, :n], tab.t[:, ti + 1, :n], ALU.mult),
                         reads=[pW, tab], writes=[t2])
                    ob = stage(sbf)
                    k.op(k.dve, lambda e: e.tensor_tensor(ob.t[:, :n], t1.t[:, :n], t2.t[:, :n], ALU.add),
                         reads=[t1, t2], writes=[ob])
                    store(dst, dst_ap, ob, ob.t[:, :n])
                tails.append(tail)

            def prologue(bi):
                t0, n = BLOCKS[bi]
                isc = 1 if t0 >= NLAT else 0
                h = hT[bi % 2]
                hv = hTv[bi % 2]
                k.dma(xb, xb.t[:, :, :n], self.XT, self.XT.t[:, :, t0:t0 + n].rearrange("kc p t -> p kc t"))
                for b_ in xbv:
                    b_.w = list(xb.w)
                self.rms_rstd(xb, xbv, 16, n, D, sqb, pS, rs)
                for kc in range(16):
                    tm_ = nxt("f", tmpf)
                    k.op(k.dve, lambda e, kc=kc, tm_=tm_: e.scalar_tensor_tensor(
                        tm_.t[:, :n], xb.t[:, kc, :n], self.Amix[l].t[:, kc, isc:isc + 1], rs.t[:, :n], ALU.mult, ALU.mult),
                         reads=[xbv[kc], rs, self.Amix[l]], writes=[tm_])
                    sh_ap = self.modT[l].t[:, kc, isc:isc + 1]
                    k.op(k.act, lambda e, kc=kc, tm_=tm_, sh_ap=sh_ap: e.activation(
                        h.t[:, kc, :n], tm_.t[:, :n], AF.Identity, bias=sh_ap), reads=[tm_, self.modT[l]], writes=[hv[kc]])
                for b_ in xbv:
                    xb.r = _compact(xb.r + b_.r)
                    b_.r = []

            prologue(0)
            for bi, (t0, n) in enumerate(BLOCKS):
                isc = 1 if t0 >= NLAT else 0
                tsl = slice(t0, t0 + n)
                nt = n // 128
                h = hT[bi % 2]
                hv = hTv[bi % 2]
                for i, nm in enumerate(("k_cosm", "k_sinm", "k_coss", "k_sins")):
                    k.dma(tab, tab.t[:, i, :n], self.C[nm], self.C[nm].t[:, tsl], more=(i > 0))

                def proj(oc, m=128):
                    g = ld["use"]
                    ld["use"] += 1
                    assert g % N_OC == oc
                    ensure(g + 2)
                    pcount["n"] += 1
                    flush(upto_tag=pcount["n"] - 2)
                    w = wfm[g % 3]
                    p = nxt("a", pA)
                    k.mm(p, p.t[0:m, :n], [(w.t[:, kc, 0:m], h.t[:, kc, :n]) for kc in range(16)], reads=[w] + hv)
                    run_tails()
                    return p

                for which in range(2):
                    oc0 = OC_CQ if which == 0 else OC_CKV
                    for j in range(4):
                        p = proj(oc0 + j)
                        evac_copy(p, n, cqvs[which][j], cqs[which].t[:, j, :n])
                for which in range(2):
                    self.rms_rstd(cqs[which], cqvs[which], 4, n, 512, sqb, pS, rss[which])
                for which in range(2):
                    gname = "g_mla_q" if which == 0 else "g_mla_kv"
                    cq, cqv, cqn, rs2 = cqs[which], cqvs[which], cqns[which], rss[which]
                    for j in range(4):
                        k.op(k.dve, lambda e, j=j: e.scalar_tensor_tensor(
                            cqn.t[:, j, :n], cq.t[:, j, :n], self.vec(gname, l, j), rs2.t[:, :n], ALU.mult, ALU.mult),
                             reads=[cqv[j], rs2, self.vecT], writes=[cqn])
                    if which == 0:
                        for hh in range(6):
                            p = nxt("a", pA)
                            k.mm(p, p.t[:, :n], [(wuq.t[:, hh, kc, :], cqn.t[:, kc, :n]) for kc in range(4)], reads=[wuq, cqn])
                            ob = stage(sbf)
                            evac_copy(p, n, ob, ob.t[:, :n])
                            store(self.QN, self.QN.t[hh, :, tsl], ob, ob.t[:, :n])
                        for pr in range(3):
                            p = nxt("a", pA)
                            k.mm(p, p.t[:, :n], [(wuq.t[:, 6 + pr, kc, :], cqn.t[:, kc, :n]) for kc in range(4)], reads=[wuq, cqn])
                            rope(p, n, 0, self.permm_b, self.QR, self.QR.t[pr, :, tsl])
                        run_tails()
                    else:
                        for hh in range(6):
                            p = nxt("a", pA)
                            k.mm(p, p.t[:, :n], [(wkf.t[:, hh, kc, :], cqn.t[:, kc, :n]) for kc in range(4)], reads=[wkf, cqn])
                            ob = stage(sbf)
                            evac_copy(p, n, ob, ob.t[:, :n])
                            store(self.KN, self.KN.t[hh, :, tsl], ob, ob.t[:, :n])
                        for ti in range(nt):
                            tt = slice(ti * 128, (ti + 1) * 128)
                            k.mm(pT[0], pT[0].t[:, :], [(cqn.t[:, kc, tt], wkt.t[:, kc, 0:512]) for kc in range(4)], reads=[wkt, cqn])
                            k.mm(pT[1], pT[1].t[:, 0:256], [(cqn.t[:, kc, tt], wkt.t[:, kc, 512:768]) for kc in range(4)], reads=[wkt, cqn])
                            so = stage(stm)
                            k.op(k.act, lambda e, so=so: e.activation(so.t[:, 0:512], pT[0].t[:, :], AF.Copy), reads=[pT[0]], writes=[so])
                            k.op(k.dve, lambda e, so=so: e.tensor_copy(so.t[:, 512:768], pT[1].t[:, 0:256]), reads=[pT[1]], writes=[so])
                            store(self.VM, self.VM.t[t0 + ti * 128:t0 + (ti + 1) * 128, :], so, so.t[:, :])
                if bi + 1 < len(BLOCKS):
                    prologue(bi + 1)
                p = proj(OC_KR)
                rope(p, n, 0, self.permm_b, self.KR, self.KR.t[:, tsl])
                for j in range(6):
                    p = proj(OC_SQ + j)
                    rope(p, n, 2, self.perms_b, self.SQ, self.SQ.t[j, :, tsl])
                for j in range(2):
                    p = proj(OC_SK + j)
                    rope(p, n, 2, self.perms_b, self.SK, self.SK.t[j, :, tsl])
                for j in range(4):
                    p = proj(OC_GQ + j)
                    of = stage(sff)
                    evac_copy(p, n, of, of.t[:, :n])
                    dst = self.GQ if j < 2 else self.GK
                    store(dst, dst.t[j % 2, :, tsl], of, of.t[:, :n])
                for j in range(4):
                    p = proj(OC_R + j)
                    of = stage(sff)
                    evac_copy(p, n, of, of.t[:, :n], func=AF.Silu)
                    store(self.GR, self.GR.t[j, :, tsl], of, of.t[:, :n])
                for d_ in range(2):
                    p = proj(OC_GLF + d_, m=16)
                    k.op(k.dve, lambda e, d_=d_, p=p: e.tensor_copy(glr.t[:, d_, :n], p.t[0:16, :n]), reads=[p], writes=[glr])
                    bname = "b_gla_gate_f" if d_ == 0 else "b_gla_gate_b"
                    dst = self.GLF if d_ == 0 else self.GLB
                    for j in range(2):
                        k.mm(pG, pG.t[:, :n], [(wgf.t[:, d_, j * 128:(j + 1) * 128], glr.t[:, d_, :n])], reads=[wgf, glr])
                        of = stage(sff)
                        k.op(k.act, lambda e, of=of, j=j, bname=bname: e.activation(
                            of.t[:, :n], pG.t[:, :n], AF.Exp, bias=self.negb[l].t[:, d_ * 2 + j:d_ * 2 + j + 1], scale=-1.0),
                             reads=[pG, self.negb[l]], writes=[of])
                        k.op(k.act, lambda e, of=of: e.activation(of.t[:, :n], of.t[:, :n], AF.Ln, bias=self.one_c.t[:, 0:1]),
                             reads=[of, self.one_c], writes=[of])
                        store(dst, dst.t[j, :, tsl], of, of.t[:, :n])
                run_tails()
                for ti in range(nt):
                    tt = slice(ti * 128, (ti + 1) * 128)
                    k.mm(pT[0], pT[0].t[:, :], [(h.t[:, kc, tt], wtm.t[:, kc, 0:512]) for kc in range(16)], reads=[wtm] + hv)
                    k.mm(pT[1], pT[1].t[:, 0:256], [(h.t[:, kc, tt], wtm.t[:, kc, 512:768]) for kc in range(16)], reads=[wtm] + hv)
                    so = stage(stm)
                    k.op(k.act, lambda e, so=so: e.activation(so.t[:, 0:512], pT[0].t[:, :], AF.Copy), reads=[pT[0]], writes=[so])
                    k.op(k.dve, lambda e, so=so: e.tensor_copy(so.t[:, 512:768], pT[1].t[:, 0:256]), reads=[pT[1]], writes=[so])
                    rows = slice(t0 + ti * 128, t0 + (ti + 1) * 128)
                    store(self.SV, self.SV.t[rows, :], so, so.t[:, 0:256])
                    store(self.GV, self.GV.t[rows, :], so, so.t[:, 256:768])
            flush(upto_tag=10 ** 9)

    def p2(self, l):
        k = self.k
        last = (l == DEPTH - 1)
        scale = float((128 + 64) ** -0.5)
        with contextlib.ExitStack() as st:
            knT = [k.sb(st, f"p2_kn{i}", (128, T), BF16) for i in range(2)]
            vT = [k.sb(st, f"p2_v{i}", (128, 34, 128), BF16) for i in range(2)]
            krT = k.sb(st, "p2_kr", (128, T), BF16)
            qn = [k.sb(st, f"p2_qn{i}", (128, 512), BF16) for i in range(3)]
            qr = [k.sb(st, f"p2_qr{i}", (128, 512), BF16) for i in range(3)]
            pt = [k.sb(st, f"p2_pt{i}", (128, 512), BF16) for i in range(4)]
            rden = k.sb(st, "p2_rden", (128, 512), F32)
            ob = [k.sb(st, f"p2_ob{i}", (128, 512), BF16) for i in range(2)]
            pS = [k.ps(st, f"p2_pS{i}", (128, 512)) for i in range(3)]
            pO = [k.ps(st, f"p2_pO{i}", (128, 512)) for i in range(2)]
            pD = [k.ps(st, f"p2_pD{i}", (128, 512)) for i in range(2)]
            dacc = [k.sb(st, f"p2_dacc{i}", (128, 512), F32) for i in range(2)]
            k.dma(krT, krT.t[:], self.KR, self.KR.t[:, :])
            blocks = BLOCKS if not last else BLOCKS[:8]
            iters = [(h, t0, n) for h in range(6) for (t0, n) in blocks]

            def load_kv(h):
                kn_, v_ = knT[h % 2], vT[h % 2]
                k.dma(kn_, kn_.t[:], self.KN, self.KN.t[h, :, :])
                k.dma(v_, v_.t[:], self.VM, self.VM.t[:, h * 128:(h + 1) * 128].rearrange("(kt p) d -> p kt d", p=128))

            def load_q(it):
                h, t0, n = iters[it]
                tsl = slice(t0, t0 + n)
                k.dma(qn[it % 3], qn[it % 3].t[:, :n], self.QN, self.QN.t[h, :, tsl])
                k.dma(qr[it % 3], qr[it % 3].t[:, :n], self.QR, self.QR.t[h // 2, :, tsl])

            load_kv(0)
            load_q(0)
            ipt = 0
            pend_store = None
            for it, (h, t0, n) in enumerate(iters):
                kn_, v_ = knT[h % 2], vT[h % 2]
                half = (h % 2) * 64
                tsl = slice(t0, t0 + n)
                qn_, qr_ = qn[it % 3], qr[it % 3]
                po, pd = pO[it % 2], pD[it % 2]
                da = dacc[it % 2]
                o_ = ob[it % 2]
                if it + 1 < len(iters):
                    load_q(it + 1)
                if t0 == 0 and h + 1 < 6:
                    load_kv(h + 1)
                if pend_store is not None:
                    pend_store()
                    pend_store = None
                kts = list(range(34)) if t0 < NLAT else [32, 33]

                def qk(kt, ps_):
                    ks = slice(kt * 128, (kt + 1) * 128)
                    k.mm(ps_, ps_.t[:, :n], [(kn_.t[:, ks], qn_.t[:, :n]),
                                             (krT.t[half:half + 64, ks], qr_.t[half:half + 64, :n])],
                         reads=[kn_, qn_, krT, qr_])

                ps_cur = pS[ipt % 3]
                qk(kts[0], ps_cur)
                if len(kts) > 1:
                    qk(kts[1], pS[(ipt + 1) % 3])
                for i, kt in enumerate(kts):
                    ps_nxt = pS[(ipt + 1) % 3]
                    if i + 2 < len(kts):
                        qk(kts[i + 2], pS[(ipt + 2) % 3])
                    p_ = pt[ipt % 4]
                    k.op(k.act, lambda e, p_=p_, ps_cur=ps_cur: e.activation(p_.t[:, :n], ps_cur.t[:, :n], AF.Exp, scale=scale),
                         reads=[ps_cur], writes=[p_])
                    k.mm(po, po.t[:, :n], [(v_.t[:, kt, :], p_.t[:, :n])], reads=[v_, p_],
                         first=(i == 0), last=(i == len(kts) - 1),
                         mark=(i % 3 == 0 or i == len(kts) - 1))
                    if i % 3 == 0:
                        if i == 0:
                            k.op(k.dve, lambda e, p_=p_, da=da: e.tensor_copy(da.t[:, :n], p_.t[:, :n]), reads=[p_], writes=[da])
                        else:
                            k.op(k.dve, lambda e, p_=p_, da=da: e.tensor_tensor(da.t[:, :n], da.t[:, :n], p_.t[:, :n], ALU.add),
                                 reads=[p_], writes=[da])
                    else:
                        k.mm(pd, pd.t[:, :n], [(self.ones_b.t[:], p_.t[:, :n])], reads=[self.ones_b, p_, v_],
                             first=(i == 1), last=False)
                    ipt += 1
                    ps_cur = ps_nxt
                    if i % 4 == 3:
                        self.conv_step(1)
                k.mm(pd, pd.t[:, :n], [(self.ones_f.t[:], da.t[:, :n])], reads=[self.ones_f, da], first=False, last=True)
                k.op(k.dve, lambda e, pd=pd: e.reciprocal(rden.t[:, :n], pd.t[:, :n]), reads=[pd], writes=[rden])
                k.op(k.dve, lambda e, po=po, o_=o_: e.tensor_tensor(o_.t[:, :n], po.t[:, :n], rden.t[:, :n], ALU.mult),
                     reads=[po, rden], writes=[o_])
                pend_store = (lambda h=h, tsl=tsl, o_=o_, n=n: k.dma(self.MIX, self.MIX.t[h, :, tsl], o_, o_.t[:, :n]))
            if pend_store is not None:
                pend_store()

    def p3(self, l):
        k = self.k
        last = (l == DEPTH - 1)
        scale = float(128 ** -0.5)
        with contextlib.ExitStack() as st:
            kT = [k.sb(st, f"p3_k{i}", (128, T), BF16) for i in range(2)]
            vT = [k.sb(st, f"p3_v{i}", (128, 34, 128), BF16) for i in range(2)]
            q = [k.sb(st, f"p3_q{i}", (128, 512), BF16) for i in range(3)]
            pt = [k.sb(st, f"p3_pt{i}", (128, 512), BF16) for i in range(4)]
            rden = k.sb(st, "p3_rden", (128, 512), F32)
            ob = [k.sb(st, f"p3_ob{i}", (128, 512), BF16) for i in range(2)]
            pS = [k.ps(st, f"p3_pS{i}", (128, 512)) for i in range(3)]
            pO = [k.ps(st, f"p3_pO{i}", (128, 512)) for i in range(2)]
            pD = [k.ps(st, f"p3_pD{i}", (128, 512)) for i in range(2)]
            blocks = BLOCKS if not last else BLOCKS[:8]
            iters = [(g, j, wi, t0, n) for g in range(2) for j in range(3) for wi, (t0, n) in enumerate(blocks)]

            def load_kv(g):
                k_, v_ = kT[g % 2], vT[g % 2]
                k.dma(k_, k_.t[:], self.SK, self.SK.t[g, :, :])
                k.dma(v_, v_.t[:], self.SV, self.SV.t[:, g * 128:(g + 1) * 128].rearrange("(kt p) d -> p kt d", p=128))

            def load_q(it):
                g, j, wi, t0, n = iters[it]
                k.dma(q[it % 3], q[it % 3].t[:, :n], self.SQ, self.SQ.t[g * 3 + j, :, t0:t0 + n])

            load_kv(0)
            load_kv(1)
            load_q(0)
            ipt = 0
            pend_store = None
            for it, (g, j, wi, t0, n) in enumerate(iters):
                k_, v_ = kT[g % 2], vT[g % 2]
                hq = g * 3 + j
                tsl = slice(t0, t0 + n)
                q_ = q[it % 3]
                po, pd = pO[it % 2], pD[it % 2]
                o_ = ob[it % 2]
                if it + 1 < len(iters):
                    load_q(it + 1)
                if pend_store is not None:
                    pend_store()
                    pend_store = None
                if t0 < NLAT:
                    kts = [(4 * wi - 1 + r, r) for r in range(6) if 0 <= 4 * wi - 1 + r < 32]
                    kts += [(32, None), (33, None)]
                else:
                    kts = [(32, None), (33, None)]

                def qk(kt, r, ps_):
                    ks = slice(kt * 128, (kt + 1) * 128)
                    terms = [(k_.t[:, ks], q_.t[:, :n])]
                    rd = [k_, q_]
                    if r is not None:
                        terms.append((self.ident_b.t[:], self.mask_b.t[:, r, :n]))
                        rd += [self.ident_b, self.mask_b]
                    k.mm(ps_, ps_.t[:, :n], terms, reads=rd)

                ps_cur = pS[ipt % 3]
                qk(kts[0][0], kts[0][1], ps_cur)
                if len(kts) > 1:
                    qk(kts[1][0], kts[1][1], pS[(ipt + 1) % 3])
                for i, (kt, r) in enumerate(kts):
                    ps_nxt = pS[(ipt + 1) % 3]
                    if i + 2 < len(kts):
                        qk(kts[i + 2][0], kts[i + 2][1], pS[(ipt + 2) % 3])
                    p_ = pt[ipt % 4]
                    k.op(k.act, lambda e, p_=p_, ps_cur=ps_cur: e.activation(p_.t[:, :n], ps_cur.t[:, :n], AF.Exp, scale=scale),
                         reads=[ps_cur], writes=[p_])
                    k.mm(po, po.t[:, :n], [(v_.t[:, kt, :], p_.t[:, :n])], reads=[v_, p_],
                         first=(i == 0), last=(i == len(kts) - 1), mark=(i == 0 or i == len(kts) - 1))
                    k.mm(pd, pd.t[:, :n], [(self.ones_b.t[:], p_.t[:, :n])], reads=[self.ones_b, p_, v_],
                         first=(i == 0), last=(i == len(kts) - 1))
                    ipt += 1
                    ps_cur = ps_nxt
                    if i == 3:
                        self.conv_step(1)
                sk_ap = self.sinkb.t[:, l * 6 + hq:l * 6 + hq + 1]
                k.op(k.dve, lambda e, pd=pd, sk_ap=sk_ap: e.tensor_scalar(rden.t[:, :n], pd.t[:, :n], sk_ap, None, ALU.add),
                     reads=[pd, self.sinkb], writes=[rden])
                k.op(k.dve, lambda e: e.reciprocal(rden.t[:, :n], rden.t[:, :n]), reads=[rden], writes=[rden])
                k.op(k.dve, lambda e, po=po, o_=o_: e.tensor_tensor(o_.t[:, :n], po.t[:, :n], rden.t[:, :n], ALU.mult),
                     reads=[po, rden], writes=[o_])
                pend_store = (lambda hq=hq, tsl=tsl, o_=o_, n=n: k.dma(self.MIX, self.MIX.t[6 + hq, :, tsl], o_, o_.t[:, :n]))
            if pend_store is not None:
                pend_store()

    def p4(self, l):
        k = self.k
        last = (l == DEPTH - 1)
        SEG = 1024
        with contextlib.ExitStack() as st:
            bsets = []
            for u in range(2):
                bsets.append((
                    [k.sb(st, f"p4_sp{u}_{i}", (128, SEG), F32) for i in range(2)],
                    k.sb(st, f"p4_q32_{u}", (128, SEG), F32),
                    k.sb(st, f"p4_k32_{u}", (128, SEG), F32),
                    k.sb(st, f"p4_e1_{u}", (128, SEG), F32),
                    k.sb(st, f"p4_e2_{u}", (128, SEG), F32),
                    k.sb(st, f"p4_qb_{u}", (128, SEG), BF16),
                    k.sb(st, f"p4_kb_{u}", (128, SEG), BF16),
                    k.sb(st, f"p4_kd_{u}", (128, SEG), BF16),
                    k.sb(st, f"p4_kdT_{u}", (128, SEG // 128, 128), BF16),
                    k.sb(st, f"p4_v_{u}", (128, SEG // 128, 256), BF16),
                    k.sb(st, f"p4_dec_{u}", (128, SEG // 64), F32),
                ))
            self._p4_unit = 0
            S = [k.sb(st, f"p4_S{d}", (128, 128), F32) for d in range(2)]
            Sb = [[k.sb(st, f"p4_Sb{d}_{i}", (128, 128), BF16) for i in range(2)] for d in range(2)]
            at = [k.sb(st, f"p4_at{i}", (128, 128), BF16) for i in range(4)]
            oacc = k.sb(st, "p4_oacc", (128, 2, NLAT), F32)
            oacv = k.views(oacc, 2 * (NLAT // 128))
            gm = k.sb(st, "p4_gm", (128, 2, 128), F32)
            rseg = k.sb(st, "p4_r", (128, 2, 512), F32)
            sq = [k.sb(st, f"p4_sq{i}", (128, 512), BF16) for i in range(2)]
            rs = k.sb(st, "p4_rs", (128, 512), F32)
            yo = [k.sb(st, f"p4_yo{i}", (128, 512), BF16) for i in range(2)]
            pA = [k.ps(st, f"p4_pA{i}", (128, 512)) for i in range(2)]
            pO = [k.ps(st, f"p4_pO{i}", (128, 512)) for i in range(2)]
            pS = k.ps(st, "p4_pS", (128, 512))
            pT = k.ps(st, "p4_pT", (128, 1024), BF16)
            pN = k.ps(st, "p4_pN", (128, 512))
            k.dma(gm, gm.t[:], self.C["k_gmask"], self.C["k_gmask"].t.rearrange("d a b -> a d b"))
            cnt = {}

            def nxt(lst):
                key = id(lst)
                cnt[key] = cnt.get(key, 0) + 1
                return lst[(cnt[key] - 1) % len(lst)]

            def gla_pass(pr, d, t0, ntok, add_to_acc, acc0):
                GL = self.GLF if d == 0 else self.GLB
                nseg = (ntok + SEG - 1) // SEG
                segs = list(range(nseg))
                if d == 1:
                    segs = segs[::-1]
                def seg_body(sg, bs):
                    spb, qf32, kf32, e1, e2, qb_, kb_, kd_, kdT, vseg, dec = bs
                    s0 = t0 + sg * SEG
                    ns = min(SEG, ntok - sg * SEG)
                    nch = ns // 64
                    ntile = ns // 128
                    ssl = slice(s0, s0 + ns)
                    a, b = spb
                    k.dma(a, a.t[:, :ns], GL, GL.t[pr, :, ssl])
                    k.dma(qf32, qf32.t[:, :ns], self.GQ, self.GQ.t[pr, :, ssl])
                    k.dma(kf32, kf32.t[:, :ns], self.GK, self.GK.t[pr, :, ssl])
                    k.dma(vseg, vseg.t[:, :ntile, :], self.GV,
                          self.GV.t[ssl, pr * 256:(pr + 1) * 256].rearrange("(a p) c -> p a c", p=128))
                    v3 = lambda buf: buf.t[:, :ns].rearrange("p (c l) -> p c l", l=64)
                    s_ = 1
                    while s_ < 64:
                        A3, B3 = v3(a), v3(b)
                        if d == 0:
                            k.op(k.dve, lambda e, A3=A3, B3=B3, s_=s_: e.tensor_tensor(B3[:, :, s_:], A3[:, :, s_:], A3[:, :, :64 - s_], ALU.add),
                                 reads=[a], writes=[b])
                            k.op(k.act, lambda e, A3=A3, B3=B3, s_=s_: e.copy(B3[:, :, :s_], A3[:, :, :s_]), reads=[a], writes=[b])
                        else:
                            k.op(k.dve, lambda e, A3=A3, B3=B3, s_=s_: e.tensor_tensor(B3[:, :, :64 - s_], A3[:, :, :64 - s_], A3[:, :, s_:], ALU.add),
                                 reads=[a], writes=[b])
                            k.op(k.act, lambda e, A3=A3, B3=B3, s_=s_: e.copy(B3[:, :, 64 - s_:], A3[:, :, 64 - s_:]), reads=[a], writes=[b])
                        a, b = b, a
                        s_ *= 2
                    cs = a
                    k.op(k.act, lambda e: e.activation(e1.t[:, :ns], cs.t[:, :ns], AF.Exp, scale=-1.0 / 16), reads=[cs], writes=[e1])
                    k.op(k.act, lambda e: e.activation(e2.t[:, :ns], cs.t[:, :ns], AF.Exp, scale=1.0 / 16), reads=[cs], writes=[e2])
                    e13 = e1.t[:, :ns].rearrange("p (c l) -> p c l", l=64)
                    edge = 63 if d == 0 else 0
                    k.op(k.act, lambda e: e.copy(dec.t[:, :nch], e13[:, :, edge]), reads=[e1], writes=[dec])
                    k.op(k.dve, lambda e: e.scalar_tensor_tensor(qb_.t[:, :ns], qf32.t[:, :ns], 0.125, e1.t[:, :ns], ALU.mult, ALU.mult),
                         reads=[qf32, e1], writes=[qb_])
                    k.op(k.dve, lambda e: e.tensor_tensor(kf32.t[:, :ns], kf32.t[:, :ns], e2.t[:, :ns], ALU.mult),
                         reads=[kf32, e2], writes=[kf32])
                    k.op(k.act, lambda e: e.copy(kb_.t[:, :ns], kf32.t[:, :ns]), reads=[kf32], writes=[kb_])
                    k3 = kf32.t[:, :ns].rearrange("p (c l) -> p c l", l=64)
                    kd3 = kd_.t[:, :ns].rearrange("p (c l) -> p c l", l=64)
                    k.op(k.dve, lambda e: e.tensor_tensor(kd3, k3, dec.t[:, :nch].unsqueeze(2).to_broadcast([128, nch, 64]), ALU.mult),
                         reads=[kf32, dec], writes=[kd_])
                    for ti in range(ntile):
                        k.mm(pT, pT.t[:, ti * 128:(ti + 1) * 128], [(kd_.t[:, ti * 128:(ti + 1) * 128], self.ident_b.t[:])],
                             reads=[kd_, self.ident_b], transpose=True, first=(ti == 0))
                    k.op(k.act, lambda e: e.activation(kdT.t[:, :ntile, :].rearrange("p a c -> p (a c)"), pT.t[:, :ntile * 128], AF.Copy),
                         reads=[pT], writes=[kdT])
                    self.conv_step(4)
                    yield
                    tiles = list(range(ntile))
                    if d == 1:
                        tiles = tiles[::-1]
                    for ti in tiles:
                        tsl = slice(ti * 128, (ti + 1) * 128)
                        gt = (s0 - t0) // 128 + ti
                        for hh in range(2):
                            hp = slice(hh * 64, (hh + 1) * 64)
                            k.mm(pA[hh], pA[hh].t[:, 0:128], [(kb_.t[hp, tsl], qb_.t[hp, tsl])], reads=[kb_, qb_])
                        ats = []
                        for hh in range(2):
                            a_ = nxt(at)
                            k.op(k.dve, lambda e, a_=a_, hh=hh: e.tensor_tensor(a_.t[:], pA[hh].t[:, 0:128], gm.t[:, d, :], ALU.mult),
                                 reads=[pA[hh], gm], writes=[a_])
                            ats.append(a_)
                        chunks = [0, 1] if d == 0 else [1, 0]
                        for hh in range(2):
                            k.mm(pO[hh], pO[hh].t[:, 0:128], [(vseg.t[:, ti, hh * 128:(hh + 1) * 128], ats[hh].t[:])],
                                 reads=[vseg, ats[hh]], first=True, last=False)
                        for ci, c in enumerate(chunks):
                            csl = slice(ti * 128 + c * 64, ti * 128 + (c + 1) * 64)
                            sb_cur = self._p4_sb[d]
                            for hh in range(2):
                                hp = slice(hh * 64, (hh + 1) * 64)
                                k.mm(pO[hh], pO[hh].t[:, c * 64:(c + 1) * 64], [(sb_cur.t[hp, :], qb_.t[hp, csl])],
                                     reads=[sb_cur, qb_], first=False, last=(ci == 1))
                            cp = slice(c * 64, (c + 1) * 64)
                            for hh in range(2):
                                k.mm(pS, pS.t[hh * 64:(hh + 1) * 64, 0:128],
                                     [(kdT.t[cp, ti, hh * 64:(hh + 1) * 64], vseg.t[cp, ti, hh * 128:(hh + 1) * 128])],
                                     reads=[kdT, vseg], first=(hh == 0), start=True)
                            cidx = ti * 2 + c
                            nb = nxt(Sb[d])
                            k.op(k.dve, lambda e, cidx=cidx, nb=nb: e.scalar_tensor_tensor(
                                nb.t[:], S[d].t[:], dec.t[:, cidx:cidx + 1], pS.t[:, 0:128], ALU.mult, ALU.add),
                                 reads=[S[d], dec, pS], writes=[nb])
                            k.op(k.dve, lambda e, cidx=cidx: e.scalar_tensor_tensor(
                                S[d].t[:], S[d].t[:], dec.t[:, cidx:cidx + 1], pS.t[:, 0:128], ALU.mult, ALU.add),
                                 reads=[S[d], dec, pS], writes=[S[d]])
                            self._p4_sb[d] = nb
                        for hh in range(2):
                            ov = oacv[hh * (NLAT // 128) + acc0 + gt]
                            o_ap = oacc.t[:, hh, (acc0 + gt) * 128:(acc0 + gt + 1) * 128]
                            if add_to_acc:
                                k.op(k.dve, lambda e, hh=hh, o_ap=o_ap: e.tensor_tensor(o_ap, pO[hh].t[:, 0:128], o_ap, ALU.add),
                                     reads=[pO[hh]], writes=[ov])
                            else:
                                k.op(k.act, lambda e, hh=hh, o_ap=o_ap: e.activation(o_ap, pO[hh].t[:, 0:128], AF.Copy),
                                     reads=[pO[hh]], writes=[ov])

                gens = []
                for sg in segs:
                    gens.append(seg_body(sg, bsets[self._p4_unit % 2]))
                    self._p4_unit += 1
                return gens

            def finalize(pr, t0, ntok, acc0):
                for b0 in range(0, ntok, 512):
                    n = min(512, ntok - b0)
                    tsl = slice(t0 + b0, t0 + b0 + n)
                    k.dma(rseg, rseg.t[:, :, :n], self.GR, self.GR.t[2 * pr:2 * pr + 2, :, tsl].rearrange("j p t -> p j t"))
                    for hh in range(2):
                        a0 = acc0 * 128 + b0
                        o_ap = oacc.t[:, hh, a0:a0 + n]
                        ovs = [oacv[hh * (NLAT // 128) + acc0 + b0 // 128 + i] for i in range(n // 128)]
                        s_ = nxt(sq)
                        k.op(k.act, lambda e, s_=s_, o_ap=o_ap: e.activation(s_.t[:, :n], o_ap, AF.Square), reads=ovs, writes=[s_])
                        k.mm(pN, pN.t[:, :n], [(self.ones_b.t[:], s_.t[:, :n])], reads=[self.ones_b, s_])
                        k.op(k.act, lambda e: e.activation(rs.t[:, :n], pN.t[:, :n], AF.Sqrt, bias=self.eps_c.t[:, 0:1], scale=1.0 / 128),
                             reads=[pN, self.eps_c], writes=[rs])
                        k.op(k.dve, lambda e: e.reciprocal(rs.t[:, :n], rs.t[:, :n]), reads=[rs], writes=[rs])
                        k.op(k.dve, lambda e, o_ap=o_ap, hh=hh: e.scalar_tensor_tensor(
                            rs.t[:, :n], o_ap, self.vec("g_gla_out", l, 2 * pr + hh), rs.t[:, :n], ALU.mult, ALU.mult),
                             reads=ovs + [rs, self.vecT], writes=[rs])
                        y_ = nxt(yo)
                        k.op(k.dve, lambda e, y_=y_, hh=hh: e.tensor_tensor(y_.t[:, :n], rs.t[:, :n], rseg.t[:, hh, :n], ALU.mult),
                             reads=[rs, rseg], writes=[y_])
                        k.dma(self.MIX, self.MIX.t[12 + 2 * pr + hh, :, tsl], y_, y_.t[:, :n])

            for pr in range(2):
                self._p4_sb = [None, None]
                for d in range(2):
                    k.op(k.dve, lambda e, d=d: e.memset(S[d].t[:], 0.0), writes=[S[d]])
                    nb = nxt(Sb[d])
                    k.op(k.dve, lambda e, nb=nb: e.memset(nb.t[:], 0.0), writes=[nb])
                    self._p4_sb[d] = nb
                units = []
                units += gla_pass(pr, 0, NLAT, NCTX, False, 0)
                units += gla_pass(pr, 1, NLAT, NCTX, True, 0)
                if not last:
                    units.append(("fin", NLAT, NCTX))
                units += gla_pass(pr, 0, 0, NLAT, False, 0)
                units += gla_pass(pr, 1, 0, NLAT, True, 0)
                units.append(("fin", 0, NLAT))
                started = set()
                for i, u in enumerate(units):
                    if isinstance(u, tuple):
                        finalize(pr, u[1], u[2], 0)
                        continue
                    if i not in started:
                        next(u)
                        started.add(i)
                    for j in range(i + 1, len(units)):
                        if not isinstance(units[j], tuple):
                            if j not in started:
                                next(units[j])
                                started.add(j)
                            break
                    for _ in u:
                        pass
                self.conv_step(4)

    def p56(self, l):
        k = self.k
        last = (l == DEPTH - 1)
        final = (l == self.L - 1)
        with contextlib.ExitStack() as st:
            xb = k.sb(st, "p5_xb", (128, 16, 512), F32)
            xbv = k.views(xb, 16)
            mh = k.sb(st, "p5_mh", (128, 16, 512), BF16)
            mhv = k.views(mh, 16)
            aT = k.sb(st, "p5_aT", (128, NFC, 512), BF16)
            aTv = k.views(aT, NFC)
            wo = [k.sb(st, f"p5_wo{i}", (128, 16, 128), BF16) for i in range(3)]
            wg = [k.sb(st, f"p5_wg{i}", (128, 16, 128), BF16) for i in range(2)]
            wu = [k.sb(st, f"p5_wu{i}", (128, 16, 128), BF16) for i in range(2)]
            wd = [k.sb(st, f"p5_wd{i}", (128, NFC, 128), BF16) for i in range(2)]
            sg = [k.sb(st, f"p5_sg{i}", (128, 512), F32) for i in range(2)]
            sqb = [k.sb(st, f"p5_sq{i}", (128, 512), BF16) for i in range(4)]
            tmpf = [k.sb(st, f"p5_tmp{i}", (128, 512), F32) for i in range(3)]
            rs = k.sb(st, "p5_rs", (128, 512), F32)
            ostg = [k.sb(st, f"p5_os{i}", (128, D), F32) for i in range(2)] if final else None
            pA = [k.ps(st, f"p5_pA{i}", (128, 512)) for i in range(2)]
            pG = [k.ps(st, f"p5_pG{i}", (128, 512)) for i in range(2)]
            pU = [k.ps(st, f"p5_pU{i}", (128, 512)) for i in range(2)]
            pS = k.ps(st, "p5_pS", (128, 512))
            cnt = {}

            def nxt(lst):
                key = id(lst)
                cnt[key] = cnt.get(key, 0) + 1
                return lst[(cnt[key] - 1) % len(lst)]

            valid = [bi for bi, (t0, n) in enumerate(BLOCKS) if not (t0 >= NLAT and last)]
            pOP = pA + pG + pU

            def load_mix(bi):
                t0_, n_ = BLOCKS[bi]
                for b_ in mhv:
                    mh.r = _compact(mh.r + b_.r + b_.w)
                k.dma(mh, mh.t[:, :, :n_], self.MIX, self.MIX.t[:, :, t0_:t0_ + n_].rearrange("kc p t -> p kc t"))
                for b_ in mhv:
                    b_.w = list(mh.w)
                    b_.r = []

            for bi, (t0, n) in enumerate(BLOCKS):
                isc = 1 if t0 >= NLAT else 0
                if isc and last:
                    continue
                tsl = slice(t0, t0 + n)
                nt = n // 128
                k.dma(xb, xb.t[:, :, :n], self.XT, self.XT.t[:, :, tsl].rearrange("kc p t -> p kc t"))
                for b_ in xbv:
                    b_.w = list(xb.w)
                if bi == valid[0]:
                    load_mix(bi)
                for oc in range(16):
                    w = nxt(wo)
                    k.dma(w, w.t[:], self.WOUT[l], self.WOUT[l].t[oc])
                    p = nxt(pOP)
                    k.mm(p, p.t[:, :n], [(w.t[:, kc, :], mh.t[:, kc, :n]) for kc in range(16)], reads=[w] + mhv)
                    g_ap = self.modT[l].t[:, 32 + oc, isc:isc + 1]
                    k.op(k.dve, lambda e, p=p, oc=oc, g_ap=g_ap: e.scalar_tensor_tensor(
                        xb.t[:, oc, :n], p.t[:, :n], g_ap, xb.t[:, oc, :n], ALU.mult, ALU.add),
                         reads=[p, self.modT[l]], writes=[xbv[oc]])
                self.rms_rstd(xb, xbv, 16, n, D, sqb, pS, rs)
                for kc in range(16):
                    tm_ = nxt(tmpf)
                    k.op(k.dve, lambda e, kc=kc, tm_=tm_: e.scalar_tensor_tensor(
                        tm_.t[:, :n], xb.t[:, kc, :n], self.Affn[l].t[:, kc, isc:isc + 1], rs.t[:, :n], ALU.mult, ALU.mult),
                         reads=[xbv[kc], rs, self.Affn[l]], writes=[tm_])
                    sh_ap = self.modT[l].t[:, 48 + kc, isc:isc + 1]
                    k.op(k.act, lambda e, kc=kc, tm_=tm_, sh_ap=sh_ap: e.activation(
                        mh.t[:, kc, :n], tm_.t[:, :n], AF.Identity, bias=sh_ap), reads=[tm_, self.modT[l]], writes=[mhv[kc]])
                for fc in range(NFC):
                    w1, w2 = nxt(wg), nxt(wu)
                    k.dma(w1, w1.t[:], self.WGU[l], self.WGU[l].t[fc])
                    k.dma(w2, w2.t[:], self.WGU[l], self.WGU[l].t[NFC + fc])
                    p1, p2 = nxt(pG), nxt(pU)
                    k.mm(p1, p1.t[:, :n], [(w1.t[:, kc, :], mh.t[:, kc, :n]) for kc in range(16)], reads=[w1] + mhv)
                    k.mm(p2, p2.t[:, :n], [(w2.t[:, kc, :], mh.t[:, kc, :n]) for kc in range(16)], reads=[w2] + mhv)
                    s_ = nxt(sg)
                    k.op(k.act, lambda e, s_=s_, p1=p1: e.activation(s_.t[:, :n], p1.t[:, :n], AF.Silu), reads=[p1], writes=[s_])
                    k.op(k.dve, lambda e, s_=s_, p2=p2, fc=fc: e.tensor_tensor(aT.t[:, fc, :n], s_.t[:, :n], p2.t[:, :n], ALU.mult),
                         reads=[s_, p2], writes=[aTv[fc]])
                for oc in range(16):
                    if oc == 2 and valid.index(bi) + 1 < len(valid):
                        load_mix(valid[valid.index(bi) + 1])
                    w = nxt(wd)
                    k.dma(w, w.t[:], self.WDN[l], self.WDN[l].t[oc])
                    p = nxt(pA)
                    k.mm(p, p.t[:, :n], [(w.t[:, fc, :], aT.t[:, fc, :n]) for fc in range(NFC)], reads=[w] + aTv)
                    g_ap = self.modT[l].t[:, 80 + oc, isc:isc + 1]
                    k.op(k.dve, lambda e, p=p, oc=oc, g_ap=g_ap: e.scalar_tensor_tensor(
                        xb.t[:, oc, :n], p.t[:, :n], g_ap, xb.t[:, oc, :n], ALU.mult, ALU.add),
                         reads=[p, self.modT[l]], writes=[xbv[oc]])
                if not final or "XT" in self.dbg:
                    k.dma(self.XT, self.XT.t[:, :, tsl].rearrange("kc p t -> p kc t"), xbv, xb.t[:, :, :n])
                if final and not isc:
                    self.rms_rstd(xb, xbv, 16, n, D, sqb, pS, rs)
                    for kc in range(16):
                        k.op(k.dve, lambda e, kc=kc: e.scalar_tensor_tensor(
                            xb.t[:, kc, :n], xb.t[:, kc, :n], self.vec("g_final", None, kc), rs.t[:, :n], ALU.mult, ALU.mult),
                             reads=[rs, self.vecT], writes=[xbv[kc]])
                    for ti in range(nt):
                        og = nxt(ostg)
                        for q4 in range(4):
                            p = nxt(pA)
                            for j in range(4):
                                kc = q4 * 4 + j
                                k.mm(p, p.t[:, j * 128:(j + 1) * 128], [(xb.t[:, kc, ti * 128:(ti + 1) * 128], self.ident_f.t[:])],
                                     reads=[xbv[kc], self.ident_f], transpose=True, first=(j == 0))
                            if q4 % 2 == 0:
                                k.op(k.act, lambda e, p=p, og=og, q4=q4: e.activation(og.t[:, q4 * 512:(q4 + 1) * 512], p.t[:, :], AF.Copy),
                                     reads=[p], writes=[og])
                            else:
                                k.op(k.dve, lambda e, p=p, og=og, q4=q4: e.tensor_copy(og.t[:, q4 * 512:(q4 + 1) * 512], p.t[:, :]),
                                     reads=[p], writes=[og])
                        k.dma(self.out, self.out.t[t0 + ti * 128:t0 + (ti + 1) * 128, :], og, og.t[:, :])
                for b_ in xbv:
                    xb.r = _compact(xb.r + b_.r + b_.w)
                    b_.r = []
                for b_ in aTv:
                    pass

    def build(self):
        k = self.k
        self.conv_rate = 0
        with contextlib.ExitStack() as st:
            self.consts(st)
            self.conv_setup(st)
            self.conv_until(0, 0)
            self.cv_throttle = True
            self.p0_transpose_in()
            self.p0_mod(st)
            if self.stop_after == "p0":
                return self.finish()
            for l in range(self.L):
                self.conv_until(l, 0)
                if self.stop_after == "conv":
                    return self.finish()
                if "skip_p1" not in self.dbg:
                    self.p1(l)
                if self.stop_after == "p1":
                    return self.finish()
                if "skip_p2" not in self.dbg:
                    self.p2(l)
                if self.stop_after == "p2":
                    return self.finish()
                if "skip_p3" not in self.dbg:
                    self.p3(l)
                if self.stop_after == "p3":
                    return self.finish()
                if "skip_p4" not in self.dbg:
                    self.p4(l)
                if self.stop_after == "p4":
                    return self.finish()
                self.conv_until(l, 1)
                self.p56(l)
            return self.finish()

    def finish(self):
        k = self.k
        toks = []
        for b in [self.out, self.XT, self.QN, self.QR, self.KN, self.KR, self.VM, self.SQ, self.SK, self.SV,
                  self.GQ, self.GK, self.GV, self.GLF, self.GLB, self.GR, self.MIX]:
            toks += b.w
        k.sp.wait(toks)
        k.sp.wait([(k.pe, k.pe.cnt), (k.act, k.act.cnt), (k.dve, k.dve.cnt), (k.pool, k.pool.cnt)])


def build_nc(n_layers=DEPTH, dbg=(), stop_after=None):
    nc = bass.Bass("TRN2", target_bir_lowering=False)
    with contextlib.ExitStack() as es:
        prog = Prog(nc, es, n_layers=n_layers, dbg=dbg, stop_after=stop_after)
        prog.build()
    return nc


def make_in_maps(inputs, cores):
    consts = host_consts()
    maps = []
    for b in cores:
        m = {
            "x": np.ascontiguousarray(inputs["x"][b]),
            "c": np.ascontiguousarray(inputs["c"][b]).reshape(16, 128),
            "ctx": np.ascontiguousarray(inputs["ctx"][b]),
            "c_ctx": np.ascontiguousarray(inputs["c_ctx"]).reshape(16, 128),
        }
        for n in W_SHAPES:
            m[n] = np.ascontiguousarray(inputs[n])
        m.update(consts)
        maps.append(m)
    return maps


def kernel(**inputs):
    inputs = {k_: np.asarray(v) for k_, v in inputs.items()}
    nc = build_nc()
    maps = make_in_maps(inputs, list(range(8)))
    res = run_bass_kernel_spmd(nc, maps, core_ids=list(range(8)))
    return np.stack([np.asarray(r["out"]) for r in res.results], axis=0).astype(np.float32)
```

```python
import contextlib
import numpy as np
import concourse.bass as bass
import concourse.mybir as mybir
from concourse.bass_utils import run_bass_kernel_spmd

F32 = mybir.dt.float32
BF16 = mybir.dt.bfloat16
AF = mybir.ActivationFunctionType
ALU = mybir.AluOpType

D = 2048
NLAT = 4096
NCTX = 256
T = NLAT + NCTX
DEPTH = 4
EPS = 1e-6
FFN = 5632
NFC = FFN // 128
BLOCKS = [(i * 512, 512) for i in range(8)] + [(NLAT, NCTX)]

C_CQ, C_CKV, C_KR, C_SQ, C_SK, C_SV, C_GQ, C_GK, C_GV, C_GLR, C_R = (
    0, 512, 1024, 1088, 1856, 2112, 2368, 2624, 2880, 3392, 3424)
IN_W = 3936
OC_CQ, OC_CKV, OC_KR, OC_SQ, OC_SK, OC_GQ, OC_GK, OC_R, OC_GLF, OC_GLB = 0, 4, 8, 9, 15, 17, 19, 21, 25, 26
N_OC = 27


class DSem:
    def __init__(self, handle):
        self.h = handle
        self.cnt = 0
        self.is_dma = True


class Eng:
    def __init__(self, name, e, sem, is_pe=False):
        self.name = name
        self.e = e
        self.h = sem
        self.cnt = 0
        self.is_dma = False
        self.is_pe = is_pe
        self.seen = {}

    def wait(self, toks):
        best = {}
        for t in toks:
            if t is None:
                continue
            s, v = t
            if s.is_dma:
                v = s.cnt
            elif s is self and self.is_pe:
                continue
            if v > best.get(id(s), (None, 0))[1]:
                best[id(s)] = (s, v)
        for s, v in best.values():
            if self.seen.get(id(s), 0) >= v:
                continue
            self.e.wait_ge(s.h, v)
            self.seen[id(s)] = v

    def done(self, ins):
        self.cnt += 1
        ins.then_inc(self.h, 1)
        return (self, self.cnt)


class Buf:
    def __init__(self, t, name="", dram=False):
        self.t = t
        self.name = name
        self.w = []
        self.r = []
        self.dsem = None
        self.dram = dram
        self.psum = False


def _compact(toks):
    best = {}
    for t in toks:
        if t is None:
            continue
        s, v = t
        if id(s) not in best or best[id(s)][1] < v:
            best[id(s)] = (s, v)
    return list(best.values())


class K:
    def __init__(self, nc, es):
        self.nc = nc
        self.es = es
        mk = lambda n: es.enter_context(nc.semaphore(n))
        self.pe = Eng("pe", nc.tensor, mk("m_pe"), is_pe=True)
        self.act = Eng("act", nc.scalar, mk("m_act"))
        self.dve = Eng("dve", nc.vector, mk("m_dve"))
        self.pool = Eng("pool", nc.gpsimd, mk("m_pool"))
        self.sp = Eng("sp", nc.sync, None)
        self.free_dsems = []
        self.n_dsems = 0
        self.freed = []

    def new_dsem(self):
        if self.free_dsems:
            return self.free_dsems.pop()
        self.n_dsems += 1
        return DSem(self.es.enter_context(self.nc.semaphore(f"d{self.n_dsems}")))

    def release(self, bufs):
        for b in bufs:
            toks = list(b.w) + list(b.r)
            for c in getattr(b, "children", []):
                toks += list(c.w) + list(c.r)
            self.freed = _compact(self.freed + toks)
            if b.dsem is not None:
                self.free_dsems.append(b.dsem)
                b.dsem = None

    def dram(self, name, shape, dt, kind="Internal"):
        t = self.nc.dram_tensor(name, list(shape), dt, kind=kind)
        return Buf(t.ap(), name, dram=True)

    def sb(self, st, name, shape, dt):
        self.uid = getattr(self, "uid", 0) + 1
        name = f"{name}_u{self.uid}"
        b = Buf(st.enter_context(self.nc.sbuf_tensor(name, list(shape), dt)), name)
        b.r = list(self.freed)
        st.callback(lambda: self.release([b]))
        return b

    def ps(self, st, name, shape, dt=F32):
        self.uid = getattr(self, "uid", 0) + 1
        name = f"{name}_u{self.uid}"
        b = Buf(st.enter_context(self.nc.psum_tensor(name, list(shape), dt)), name)
        b.psum = True
        b.r = list(self.freed)
        st.callback(lambda: self.release([b]))
        return b

    def op(self, eng, fn, reads=(), writes=()):
        deps = []
        for b in reads:
            deps += b.w
            if b.psum:
                deps += b.r
        for b in writes:
            deps += b.w
            deps += b.r
        eng.wait(deps)
        tok = eng.done(fn(eng.e))
        for b in writes:
            b.w = [tok]
            b.r = []
        for b in reads:
            if b in writes:
                continue
            b.r = _compact(b.r + [tok])
        return tok

    def mm(self, out_buf, out_ap, terms, reads=(), first=True, last=True, transpose=False, start=None, mark=True):
        pe = self.pe
        deps = []
        if first:
            deps += list(out_buf.w) + list(out_buf.r)
        for b in reads:
            deps += b.w
        pe.wait(deps)
        n = len(terms)
        ins = None
        for i, (l, r) in enumerate(terms):
            if transpose:
                ins = pe.e.transpose(out_ap, l, r)
            else:
                st_ = (first and i == 0) if start is None else (start and i == 0)
                ins = pe.e.matmul(out_ap, l, r, start=st_, stop=(last and i == n - 1))
        if not mark:
            if first:
                out_buf.r = []
            return None
        tok = pe.done(ins)
        out_buf.w = [tok]
        if first:
            out_buf.r = []
        for b in reads:
            b.r = _compact(b.r + [tok])
        return tok

    def dma(self, out_buf, out_ap, in_buf, in_ap, q=None, more=False):
        q = q or self.sp
        in_bufs = in_buf if isinstance(in_buf, (list, tuple)) else [in_buf]
        deps = []
        for ib in in_bufs:
            deps += list(ib.w)
        if not out_buf.dram and not out_buf.w:
            more = False
        if not out_buf.dram and not more:
            deps += list(out_buf.w) + list(out_buf.r)
        q.wait(deps)
        if out_buf.dsem is None:
            out_buf.dsem = self.new_dsem()
        s = out_buf.dsem
        s.cnt += 16
        q.e.dma_start(out=out_ap, in_=in_ap).then_inc(s.h, 16)
        tok = (s, s.cnt)
        out_buf.w = [tok]
        if not out_buf.dram and not more:
            out_buf.r = []
        for ib in in_bufs:
            if not ib.dram:
                ib.r = _compact(ib.r + [tok])
        return tok

    def views(self, buf, n):
        vs = [Buf(buf.t, f"{buf.name}.{i}") for i in range(n)]
        for v in vs:
            v.r = list(buf.r)
            v.w = list(buf.w)
            v.psum = buf.psum
        buf.children = getattr(buf, "children", []) + vs
        return vs


def _rope_tables():
    t = np.arange(NLAT)
    rows = (t // 64).astype(np.float32)
    cols = (t % 64).astype(np.float32)

    def tab(dh):
        half = dh // 2
        freqs = (10000.0 ** (-np.arange(half, dtype=np.float32) / half)).astype(np.float32)
        cos = np.ones((2 * dh, T), np.float32)
        sin = np.zeros((2 * dh, T), np.float32)
        for a, pos in enumerate((rows, cols)):
            ang = (pos[None, :] * freqs[:, None]).astype(np.float32)
            c, s = np.cos(ang).astype(np.float32), np.sin(ang).astype(np.float32)
            base = a * dh
            cos[base:base + half, :NLAT] = c
            cos[base + half:base + dh, :NLAT] = c
            sin[base:base + half, :NLAT] = -s
            sin[base + half:base + dh, :NLAT] = s
        return cos, sin

    cm, sm = tab(32)
    cs, ss = tab(64)
    cm = np.concatenate([cm, cm], 0)
    sm = np.concatenate([sm, sm], 0)
    return np.ascontiguousarray(cm), np.ascontiguousarray(sm), cs, ss


def _perm(dh, n=128):
    half = dh // 2
    p = np.zeros((n, n), np.float32)
    for m in range(n):
        g, o = divmod(m, dh)
        k = g * dh + (o + half) % dh
        p[k, m] = 1.0
    return p


def _swa_masks():
    m = np.zeros((6, 128, 512), np.float32)
    for r in range(6):
        kpos = (r - 1) * 128 + np.arange(128)[:, None]
        qpos = np.arange(512)[None, :]
        ok = np.abs(qpos - kpos) <= 128
        m[r] = np.where(ok, 0.0, -30000.0)
    return m


def _gla_masks():
    tp = np.arange(128)[:, None]
    t = np.arange(128)[None, :]
    same = (tp // 64) == (t // 64)
    m = np.zeros((2, 128, 128), np.float32)
    m[0] = (same & (tp <= t)).astype(np.float32)
    m[1] = (same & (tp > t)).astype(np.float32)
    return m


def host_consts():
    cm, sm, cs, ss = _rope_tables()
    return {
        "k_ident": np.eye(128, dtype=np.float32),
        "k_permm": _perm(32),
        "k_perms": _perm(64),
        "k_cosm": cm, "k_sinm": sm, "k_coss": cs, "k_sins": ss,
        "k_mask": _swa_masks(),
        "k_gmask": _gla_masks(),
    }


W_SHAPES = {
    "w_mod": (DEPTH, D, 6 * D), "b_mod": (DEPTH, 6 * D), "g_mix": (DEPTH, D), "g_ffn": (DEPTH, D),
    "w_in": (DEPTH, D, IN_W), "g_mla_q": (DEPTH, 512), "g_mla_kv": (DEPTH, 512),
    "w_mla_uq": (DEPTH, 512, 1152), "w_mla_ukv": (DEPTH, 512, 1536), "swa_sink": (DEPTH, 6),
    "w_gla_gate_f": (DEPTH, 16, 256), "b_gla_gate_f": (DEPTH, 256),
    "w_gla_gate_b": (DEPTH, 16, 256), "b_gla_gate_b": (DEPTH, 256), "g_gla_out": (DEPTH, 512),
    "w_out": (DEPTH, D, D), "w_ffn_gu": (DEPTH, D, 2 * FFN), "w_ffn_down": (DEPTH, FFN, D),
    "g_final": (D,),
}


class Prog:
    def __init__(self, nc, es, n_layers=DEPTH, dbg=(), stop_after=None):
        self.nc = nc
        self.es = es
        self.k = K(nc, es)
        self.L = n_layers
        self.dbg = set(dbg)
        self.stop_after = stop_after
        import os
        self.p1_stage = int(os.environ.get("P1_STAGE", "0"))
        k = self.k
        ein = lambda n, s: k.dram(n, s, F32, kind="ExternalInput")
        self.x = ein("x", (NLAT, D))
        self.c = ein("c", (16, 128))
        self.ctx = ein("ctx", (NCTX, D))
        self.c_ctx = ein("c_ctx", (16, 128))
        self.W = {n: ein(n, s) for n, s in W_SHAPES.items()}
        self.C = {n: ein(n, v.shape) for n, v in host_consts().items()}
        self.out = k.dram("out", (NLAT, D), F32, kind="ExternalOutput")
        L = self.L

        def scr(n, s, dt):
            return k.dram(n, s, dt, kind=("ExternalOutput" if n in self.dbg else "Internal"))

        self.XT = scr("XT", (16, 128, T), F32)
        self.QN = scr("QN", (6, 128, T), BF16)
        self.QR = scr("QR", (3, 128, T), BF16)
        self.KN = scr("KN", (6, 128, T), BF16)
        self.KR = scr("KR", (128, T), BF16)
        self.VM = scr("VM", (T, 768), BF16)
        self.SQ = scr("SQ", (6, 128, T), BF16)
        self.SK = scr("SK", (2, 128, T), BF16)
        self.SV = scr("SV", (T, 256), BF16)
        self.GQ = scr("GQ", (2, 128, T), F32)
        self.GK = scr("GK", (2, 128, T), F32)
        self.GV = scr("GV", (T, 512), BF16)
        self.GLF = scr("GLF", (2, 128, T), F32)
        self.GLB = scr("GLB", (2, 128, T), F32)
        self.GR = scr("GR", (4, 128, T), F32)
        self.MIX = scr("MIX", (16, 128, T), BF16)
        self.WINFM = [scr(f"WINFM{l}", (N_OC, 128, 16, 128), BF16) for l in range(L)]
        self.WINTM = [scr(f"WINTM{l}", (128, 16, 768), BF16) for l in range(L)]
        self.WUQ = [scr(f"WUQ{l}", (9, 128, 4, 128), BF16) for l in range(L)]
        self.WUKVFM = [scr(f"WUKVFM{l}", (6, 128, 4, 128), BF16) for l in range(L)]
        self.WUKVTM = [scr(f"WUKVTM{l}", (128, 4, 768), BF16) for l in range(L)]
        self.WOUT = [scr(f"WOUT{l}", (16, 128, 16, 128), BF16) for l in range(L)]
        self.WGU = [scr(f"WGU{l}", (88, 128, 16, 128), BF16) for l in range(L)]
        self.WDN = [scr(f"WDN{l}", (16, 128, NFC, 128), BF16) for l in range(L)]
        for l in range(L):
            sh = k.new_dsem()
            for b in (self.WINFM[l], self.WINTM[l], self.WUQ[l], self.WUKVFM[l], self.WUKVTM[l],
                      self.WOUT[l], self.WGU[l], self.WDN[l]):
                b.dsem = sh

    def consts(self, st):
        k = self.k
        self.ident_f = k.sb(st, "ident_f", (128, 128), F32)
        self.ident_b = k.sb(st, "ident_b", (128, 128), BF16)
        self.ones_b = k.sb(st, "ones_b", (128, 128), BF16)
        self.ones_f = k.sb(st, "ones_f", (128, 128), F32)
        self.permm_b = k.sb(st, "permm_b", (128, 128), BF16)
        self.perms_b = k.sb(st, "perms_b", (128, 128), BF16)
        self.mask_b = k.sb(st, "mask_b", (128, 6, 512), BF16)
        with contextlib.ExitStack() as tmp:
            pf = k.sb(tmp, "c_pf", (128, 2, 128), F32)
            mf = k.sb(tmp, "c_mf", (128, 6, 512), F32)
            k.dma(self.ident_f, self.ident_f.t[:], self.C["k_ident"], self.C["k_ident"].t[:, :])
            k.dma(pf, pf.t[:, 0, :], self.C["k_permm"], self.C["k_permm"].t[:, :])
            k.dma(pf, pf.t[:, 1, :], self.C["k_perms"], self.C["k_perms"].t[:, :], more=True)
            k.dma(mf, mf.t[:], self.C["k_mask"], self.C["k_mask"].t.rearrange("r k q -> k r q"))
            k.op(k.dve, lambda e: e.tensor_copy(self.ident_b.t[:], self.ident_f.t[:]),
                 reads=[self.ident_f], writes=[self.ident_b])
            k.op(k.dve, lambda e: e.tensor_copy(self.permm_b.t[:], pf.t[:, 0, :]), reads=[pf], writes=[self.permm_b])
            k.op(k.dve, lambda e: e.tensor_copy(self.perms_b.t[:], pf.t[:, 1, :]), reads=[pf], writes=[self.perms_b])
            k.op(k.dve, lambda e: e.tensor_copy(self.mask_b.t[:], mf.t[:]), reads=[mf], writes=[self.mask_b])
            k.op(k.dve, lambda e: e.memset(self.ones_b.t[:], 1.0), writes=[self.ones_b])
            k.op(k.dve, lambda e: e.memset(self.ones_f.t[:], 1.0), writes=[self.ones_f])
        L = self.L
        self.vecT = k.sb(st, "vecT", (128, 2, 128), F32)
        self.sinkb = k.sb(st, "sinkb", (128, 4 * 6), F32)
        self.eps_c = k.sb(st, "eps_c", (128, 1), F32)
        self.one_c = k.sb(st, "one_c", (128, 1), F32)
        self.negb = [k.sb(st, f"negb{l}", (128, 4), F32) for l in range(L)]
        W = self.W
        with contextlib.ExitStack() as tmp:
            vr = [k.sb(tmp, f"vrows{i}", (128, 128), F32) for i in range(2)]
            self.vcol = {}
            items = []
            for l in range(L):
                for nm, nr in (("g_mix", 16), ("g_ffn", 16), ("g_mla_q", 4), ("g_mla_kv", 4),
                               ("g_gla_out", 4), ("b_gla_gate_f", 2), ("b_gla_gate_b", 2)):
                    items.append((nm, l, nr))
            items.append(("g_final", None, 16))
            row = 0
            for nm, l, nr in items:
                g, r0 = divmod(row, 128)
                if r0 + nr > 128:
                    row = (g + 1) * 128
                    g, r0 = divmod(row, 128)
                src = W[nm].t[l] if l is not None else W[nm].t
                k.dma(vr[g], vr[g].t[r0:r0 + nr, :], W[nm], src.rearrange("(r c) -> r c", c=128), more=True)
                self.vcol[(nm, l)] = (g, r0)
                row += nr
            assert row <= 256
            with contextlib.ExitStack() as pst:
                pp = k.ps(pst, "c_pp", (128, 512))
                for g in range(2):
                    k.mm(pp, pp.t[:, g * 128:(g + 1) * 128], [(vr[g].t[:], self.ident_f.t[:])],
                         reads=[vr[g], self.ident_f], transpose=True)
                k.op(k.dve, lambda e: e.tensor_copy(self.vecT.t[:].rearrange("p g c -> p (g c)"), pp.t[:, 0:256]),
                     reads=[pp], writes=[self.vecT])
            k.op(k.dve, lambda e: e.memset(self.eps_c.t[:], EPS), writes=[self.eps_c])
            k.op(k.dve, lambda e: e.memset(self.one_c.t[:], 1.0), writes=[self.one_c])
            for l in range(L):
                for d_, nm in enumerate(("b_gla_gate_f", "b_gla_gate_b")):
                    g, r0 = self.vcol[(nm, l)]
                    k.op(k.dve, lambda e, l=l, d_=d_, g=g, r0=r0: e.tensor_scalar(
                        self.negb[l].t[:, d_ * 2:d_ * 2 + 2], self.vecT.t[:, g, r0:r0 + 2], -1.0, None, ALU.mult),
                         reads=[self.vecT], writes=[self.negb[l]])
            k.dma(self.sinkb, self.sinkb.t[:], W["swa_sink"],
                  W["swa_sink"].t.rearrange("l h -> (l h)").partition_broadcast(128))
            k.op(k.act, lambda e: e.activation(self.sinkb.t[:], self.sinkb.t[:], AF.Exp),
                 reads=[self.sinkb], writes=[self.sinkb])

    def vec(self, nm, l, j):
        g, r0 = self.vcol[(nm, l)]
        return self.vecT.t[:, g, r0 + j:r0 + j + 1]

    def p0_transpose_in(self):
        k = self.k
        with contextlib.ExitStack() as st:
            xin = [k.sb(st, f"p0_xin{i}", (128, 4, D), F32) for i in range(2)]
            xo = [k.sb(st, f"p0_xo{i}", (128, 16, 512), F32) for i in range(2)]
            pp = [k.ps(st, f"p0_pp{i}", (128, 512)) for i in range(4)]
            xovs = [k.views(b, 16) for b in xo]
            cnt = 0
            for bi, (t0, n) in enumerate(BLOCKS):
                xi = xin[bi % 2]
                xob = xo[bi % 2]
                xov = xovs[bi % 2]
                nt = n // 128
                if t0 < NLAT:
                    src, sb_ = self.x, self.x.t[t0:t0 + n, :]
                else:
                    src, sb_ = self.ctx, self.ctx.t[:, :]
                k.dma(xi, xi.t[:, 0:nt, :], src, sb_.rearrange("(a p) d -> p a d", p=128))
                for kc in range(16):
                    p = pp[cnt % 4]
                    for ti in range(nt):
                        k.mm(p, p.t[:, ti * 128:(ti + 1) * 128],
                             [(xi.t[:, ti, kc * 128:(kc + 1) * 128], self.ident_f.t[:])],
                             reads=[xi, self.ident_f], transpose=True, first=(ti == 0))
                    eng = k.dve if cnt % 2 == 0 else k.act
                    if eng is k.dve:
                        k.op(eng, lambda e, p=p, kc=kc: e.tensor_copy(xob.t[:, kc, :n], p.t[:, :n]), reads=[p], writes=[xov[kc]])
                    else:
                        k.op(eng, lambda e, p=p, kc=kc: e.copy(xob.t[:, kc, :n], p.t[:, :n]), reads=[p], writes=[xov[kc]])
                    cnt += 1
                k.dma(self.XT, self.XT.t[:, :, t0:t0 + n].rearrange("kc p t -> p kc t"), xov, xob.t[:, :, :n])

    def p0_mod(self, st):
        k = self.k
        L = self.L
        self.modT = [k.sb(st, f"modT{l}", (128, 96, 2), F32) for l in range(L)]
        self.Amix = [k.sb(st, f"Amix{l}", (128, 16, 2), F32) for l in range(L)]
        self.Affn = [k.sb(st, f"Affn{l}", (128, 16, 2), F32) for l in range(L)]
        with contextlib.ExitStack() as tmp:
            crow = k.sb(tmp, "m_crow", (32, 128), F32)
            sc = k.sb(tmp, "m_sc", (128, 16, 2), F32)
            wm = [k.sb(tmp, f"m_wm{i}", (128, 4096), F32) for i in range(3)]
            bm = k.sb(tmp, "m_bm", (2, 4096), F32)
            mrow = k.sb(tmp, "m_mrow", (2, 4096), F32)
            pp = [k.ps(tmp, f"m_pp{i}", (128, 512)) for i in range(8)]
            k.dma(crow, crow.t[0:16, :], self.c, self.c.t[:, :])
            k.dma(crow, crow.t[16:32, :], self.c_ctx, self.c_ctx.t[:, :], more=True)
            k.mm(pp[0], pp[0].t[:, 0:32], [(crow.t[:], self.ident_f.t[0:32, 0:32])], reads=[crow, self.ident_f], transpose=True)
            k.op(k.act, lambda e: e.activation(sc.t[:, :, 0], pp[0].t[:, 0:16], AF.Silu), reads=[pp[0]], writes=[sc])
            k.op(k.act, lambda e: e.activation(sc.t[:, :, 1], pp[0].t[:, 16:32], AF.Silu), reads=[pp[0]], writes=[sc])
            cnt = 0
            for l in range(L):
                for cg in range(3):
                    c0 = cg * 4096
                    k.dma(bm, bm.t[0:1, :], self.W["b_mod"], self.W["b_mod"].t[l:l + 1, c0:c0 + 4096])
                    k.dma(bm, bm.t[1:2, :], self.W["b_mod"], self.W["b_mod"].t[l:l + 1, c0:c0 + 4096], more=True)
                    for kc in range(16):
                        w = wm[cnt % 3]
                        cnt += 1
                        k.dma(w, w.t[:], self.W["w_mod"], self.W["w_mod"].t[l, kc * 128:(kc + 1) * 128, c0:c0 + 4096])
                        for j in range(8):
                            k.mm(pp[j], pp[j].t[0:2, :], [(sc.t[:, kc, :], w.t[:, j * 512:(j + 1) * 512])],
                                 reads=[sc, w], first=(kc == 0), last=(kc == 15))
                    for j in range(8):
                        k.op(k.dve, lambda e, j=j: e.tensor_tensor(mrow.t[0:2, j * 512:(j + 1) * 512], pp[j].t[0:2, :],
                                                                   bm.t[0:2, j * 512:(j + 1) * 512], ALU.add),
                             reads=[pp[j], bm], writes=[mrow])
                    for jj in range(32):
                        k.mm(pp[0], pp[0].t[:, jj * 2:jj * 2 + 2],
                             [(mrow.t[0:2, jj * 128:(jj + 1) * 128], self.ident_f.t[0:2, 0:2])],
                             reads=[mrow, self.ident_f], transpose=True, first=(jj == 0))
                    k.op(k.dve, lambda e, l=l, cg=cg: e.tensor_copy(
                        self.modT[l].t[:, cg * 32:(cg + 1) * 32, :].rearrange("p a b -> p (a b)"), pp[0].t[:, 0:64]),
                         reads=[pp[0]], writes=[self.modT[l]])
                for (A, gname, off) in ((self.Amix[l], "g_mix", 16), (self.Affn[l], "g_ffn", 64)):
                    g, r0 = self.vcol[(gname, l)]
                    for b in range(2):
                        k.op(k.dve, lambda e, A=A, b=b, off=off, g=g, r0=r0, l=l: e.scalar_tensor_tensor(
                            A.t[:, :, b], self.modT[l].t[:, off:off + 16, b], 1.0, self.vecT.t[:, g, r0:r0 + 16],
                            ALU.add, ALU.mult), reads=[self.modT[l], self.vecT], writes=[A])

    def conv_flush(self):
        for th in self.cv_pending:
            th()
        self.cv_pending = []

    def conv_pieces(self, l, part):
        k = self.k
        W = self.W
        FMv = lambda buf, a, b, kc: buf.t[a:b, :, kc, :].rearrange("oc p m -> p oc m")

        def piece(src_buf, src_ap, ncols, stores):
            if self.cv_throttle:
                k.pool.wait([(k.pe, k.pe.cnt)])
            for (c0, c1, dbuf, dap, three) in stores:
                s_ap = src_ap[:, c0:c1]
                if three:
                    s_ap = s_ap.rearrange("p (oc m) -> p oc m", m=128)
                k.dma(dbuf, dap, src_buf, s_ap, q=k.pool)

        if part == 0:
            yield from self._conv_early(l, piece, FMv)
        else:
            yield from self._conv_late(l, piece, FMv)
        self.conv_flush()

    def _conv_early(self, l, piece, FMv):
        W = self.W

        win = W["w_in"]
        FM, TM = self.WINFM[l], self.WINTM[l]
        for kc in range(16):
            rows = slice(kc * 128, (kc + 1) * 128)
            piece(win, win.t[l, rows, 0:1024], 1024, [(0, 1024, FM, FMv(FM, 0, 8, kc), True)])
            yield
            piece(win, win.t[l, rows, 1024:2112], 1088, [
                (0, 64, FM, FM.t[OC_KR, :, kc, 0:64], False),
                (0, 64, FM, FM.t[OC_KR, :, kc, 64:128], False),
                (64, 1088, FM, FMv(FM, OC_SQ, OC_SQ + 8, kc), True)])
            yield
            piece(win, win.t[l, rows, 2112:2880], 768, [
                (0, 256, TM, TM.t[:, kc, 0:256], False),
                (256, 768, FM, FMv(FM, OC_GQ, OC_GQ + 4, kc), True)])
            yield
            piece(win, win.t[l, rows, 2880:3936], 1056, [
                (0, 512, TM, TM.t[:, kc, 256:768], False),
                (512, 528, FM, FM.t[OC_GLF, :, kc, 0:16], False),
                (528, 544, FM, FM.t[OC_GLB, :, kc, 0:16], False),
                (544, 1056, FM, FMv(FM, OC_R, OC_R + 4, kc), True)])
            yield
        wq = W["w_mla_uq"]
        for kc in range(4):
            rows = slice(kc * 128, (kc + 1) * 128)
            stores = []
            for h in range(6):
                stores.append((h * 192, h * 192 + 128, self.WUQ[l], self.WUQ[l].t[h, :, kc, :], False))
                stores.append((h * 192 + 128, h * 192 + 192, self.WUQ[l],
                               self.WUQ[l].t[6 + h // 2, :, kc, (h % 2) * 64:(h % 2) * 64 + 64], False))
            piece(wq, wq.t[l, rows, :], 1152, stores)
            yield
        wkv = W["w_mla_ukv"]
        for kc in range(4):
            rows = slice(kc * 128, (kc + 1) * 128)
            for half in range(2):
                stores = []
                for hh in range(3):
                    h = half * 3 + hh
                    stores.append((hh * 256, hh * 256 + 128, self.WUKVFM[l], self.WUKVFM[l].t[h, :, kc, :], False))
                    stores.append((hh * 256 + 128, hh * 256 + 256, self.WUKVTM[l],
                                   self.WUKVTM[l].t[:, kc, h * 128:(h + 1) * 128], False))
                piece(wkv, wkv.t[l, rows, half * 768:(half + 1) * 768], 768, stores)
                yield

    def _conv_late(self, l, piece, FMv):
        W = self.W
        wo = W["w_out"]
        for kc in range(16):
            rows = slice(kc * 128, (kc + 1) * 128)
            for j in range(2):
                piece(wo, wo.t[l, rows, j * 1024:(j + 1) * 1024], 1024,
                      [(0, 1024, self.WOUT[l], FMv(self.WOUT[l], j * 8, j * 8 + 8, kc), True)])
                yield
        wg = W["w_ffn_gu"]
        for kc in range(16):
            rows = slice(kc * 128, (kc + 1) * 128)
            for j in range(11):
                piece(wg, wg.t[l, rows, j * 1024:(j + 1) * 1024], 1024,
                      [(0, 1024, self.WGU[l], FMv(self.WGU[l], j * 8, j * 8 + 8, kc), True)])
                yield
        wd = W["w_ffn_down"]
        for kc in range(NFC):
            rows = slice(kc * 128, (kc + 1) * 128)
            for j in range(2):
                piece(wd, wd.t[l, rows, j * 1024:(j + 1) * 1024], 1024,
                      [(0, 1024, self.WDN[l], FMv(self.WDN[l], j * 8, j * 8 + 8, kc), True)])
                yield

    def conv_setup(self, st):
        k = self.k
        self.cv_i = 0
        self.cv_throttle = False
        self.cv_pending = []
        self.cv_marks = set()
        self.cv_gen = self.conv_all()

    def conv_all(self):
        for l in range(self.L):
            for part in range(2):
                yield from self.conv_pieces(l, part)
                self.cv_marks.add((l, part))

    def conv_until(self, l, part):
        while (l, part) not in self.cv_marks and self.cv_gen is not None:
            self.conv_step(1)

    def conv_step(self, n):
        if self.cv_gen is None:
            return
        for _ in range(n):
            try:
                next(self.cv_gen)
            except StopIteration:
                self.cv_gen = None
                self.conv_flush()
                return

    def conv_finish(self):
        self.conv_step(10 ** 9)

    def rms_rstd(self, src_tile, src_bufs, nchunks, n, dim, sqb, pS, rs, cnt0=0):
        k = self.k
        for j in range(nchunks):
            sq = sqb[(cnt0 + j) % len(sqb)]
            k.op(k.act, lambda e, j=j, sq=sq: e.activation(sq.t[:, :n], src_tile.t[:, j, :n], AF.Square),
                 reads=[src_bufs[j]], writes=[sq])
            k.mm(pS, pS.t[:, :n], [(self.ones_b.t[:], sq.t[:, :n])], reads=[self.ones_b, sq],
                 first=(j == 0), last=(j == nchunks - 1))
        k.op(k.act, lambda e: e.activation(rs.t[:, :n], pS.t[:, :n], AF.Sqrt, bias=self.eps_c.t[:, 0:1], scale=1.0 / dim),
             reads=[pS, self.eps_c], writes=[rs])
        k.op(k.dve, lambda e: e.reciprocal(rs.t[:, :n], rs.t[:, :n]), reads=[rs], writes=[rs])

    def p1(self, l):
        k = self.k
        W = self.W
        L = self.L
        with contextlib.ExitStack() as st:
            xb = k.sb(st, "p1_xb", (128, 16, 512), F32)
            xbv = k.views(xb, 16)
            hT = [k.sb(st, f"p1_hT{i}", (128, 16, 512), BF16) for i in range(2)]
            hTv = [k.views(b, 16) for b in hT]
            sqb = [k.sb(st, f"p1_sq{i}", (128, 512), BF16) for i in range(4)]
            tmpf = [k.sb(st, f"p1_tmp{i}", (128, 512), F32) for i in range(3)]
            rs = k.sb(st, "p1_rs", (128, 512), F32)
            rss = [k.sb(st, f"p1_rs2_{i}", (128, 512), F32) for i in range(2)]
            wfm = [k.sb(st, f"p1_wfm{i}", (128, 16, 128), BF16) for i in range(3)]
            wtm = k.sb(st, "p1_wtm", (128, 16, 768), BF16)
            wuq = k.sb(st, "p1_wuq", (128, 9, 4, 128), BF16)
            wkf = k.sb(st, "p1_wkf", (128, 6, 4, 128), BF16)
            wkt = k.sb(st, "p1_wkt", (128, 4, 768), BF16)
            wgf = k.sb(st, "p1_wgf", (16, 2, 256), F32)
            cqs = [k.sb(st, f"p1_cq{i}", (128, 4, 512), F32) for i in range(2)]
            cqvs = [k.views(b_, 4) for b_ in cqs]
            cqns = [k.sb(st, f"p1_cqn{i}", (128, 4, 512), BF16) for i in range(2)]
            tab = k.sb(st, "p1_tab", (128, 4, 512), F32)
            sbf = [k.sb(st, f"p1_sbf{i}", (128, 512), BF16) for i in range(4)]
            sff = [k.sb(st, f"p1_sff{i}", (128, 512), F32) for i in range(4)]
            stm = [k.sb(st, f"p1_stm{i}", (128, 768), BF16) for i in range(2)]
            glr = k.sb(st, "p1_glr", (16, 2, 512), F32)
            pA = [k.ps(st, f"p1_pA{i}", (128, 512)) for i in range(3)]
            pS = k.ps(st, "p1_pS", (128, 512))
            pW = k.ps(st, "p1_pW", (128, 512))
            pT = [k.ps(st, f"p1_pT{i}", (128, 512)) for i in range(2)]
            pG = k.ps(st, "p1_pG", (128, 512))

            k.dma(wtm, wtm.t[:], self.WINTM[l], self.WINTM[l].t[:, :, :])
            k.dma(wuq, wuq.t[:], self.WUQ[l], self.WUQ[l].t.rearrange("oc p kc m -> p oc kc m"))
            k.dma(wkf, wkf.t[:], self.WUKVFM[l], self.WUKVFM[l].t.rearrange("oc p kc m -> p oc kc m"))
            k.dma(wkt, wkt.t[:], self.WUKVTM[l], self.WUKVTM[l].t[:, :, :])
            k.dma(wgf, wgf.t[:, 0, :], W["w_gla_gate_f"], W["w_gla_gate_f"].t[l])
            k.dma(wgf, wgf.t[:, 1, :], W["w_gla_gate_b"], W["w_gla_gate_b"].t[l], more=True)

            cnt = {"a": 0, "w": 0, "s": 0, "f": 0, "e": 0}

            def nxt(key, lst):
                key = id(lst)
                cnt[key] = cnt.get(key, 0) + 1
                return lst[(cnt[key] - 1) % len(lst)]

            def evac_copy(p, n, dst_buf, dst_ap, func=None):
                cnt["e"] += 1
                if func is not None or cnt["e"] % 2 == 0:
                    f = func if func is not None else AF.Copy
                    k.op(k.act, lambda e: e.activation(dst_ap, p.t[:, :n], f), reads=[p], writes=[dst_buf])
                else:
                    k.op(k.dve, lambda e: e.tensor_copy(dst_ap, p.t[:, :n]), reads=[p], writes=[dst_buf])

            pend = []
            pcount = {"n": 0}

            def store(dst, dst_ap, sbuf, s_ap):
                pend.append((pcount["n"], sbuf, lambda: k.dma(dst, dst_ap, sbuf, s_ap)))

            def flush(upto_tag=None, buf=None):
                while pend:
                    tag, sb_, th = pend[0]
                    need = (upto_tag is not None and tag <= upto_tag) or \
                           (buf is not None and any(e[1] is buf for e in pend))
                    if not need:
                        break
                    pend.pop(0)
                    th()

            def stage(lst):
                b = nxt("x", lst)
                flush(buf=b)
                return b

            ld = {"n": 0, "use": 0}
            total_loads = len(BLOCKS) * N_OC

            def ensure(upto):
                while ld["n"] <= min(upto, total_loads - 1):
                    g = ld["n"]
                    w_ = wfm[g % 3]
                    k.dma(w_, w_.t[:], self.WINFM[l], self.WINFM[l].t[g % N_OC])
                    ld["n"] += 1

            tails = []

            def run_tails():
                while tails:
                    tails.pop(0)()

            def rope(p, n, ti, perm, dst, dst_ap):
                xr = stage(sbf)
                k.op(k.act, lambda e: e.activation(xr.t[:, :n], p.t[:, :n], AF.Copy), reads=[p], writes=[xr])
                run_tails()

                def tail():
                    k.mm(pW, pW.t[:, :n], [(perm.t[:], xr.t[:, :n])], reads=[perm, xr])
                    t1 = stage(sff)
                    k.op(k.dve, lambda e: e.tensor_tensor(t1.t[:, :n], p.t[:, :n], tab.t[:, ti, :n], ALU.mult),
                         reads=[p, tab], writes=[t1])
                    t2 = stage(sff)
                    k.op(k.dve, lambda e: e.tensor_tensor(t2.t[:, :n], pW.t[:, :n], tab.t[:, ti + 1, :n], ALU.mult),
                         reads=[pW, tab], writes=[t2])
                    ob = stage(sbf)
                    k.op(k.dve, lambda e: e.tensor_tensor(ob.t[:, :n], t1.t[:, :n], t2.t[:, :n], ALU.add),
                         reads=[t1, t2], writes=[ob])
                    store(dst, dst_ap, ob, ob.t[:, :n])
                tails.append(tail)

            def prologue(bi):
                t0, n = BLOCKS[bi]
                isc = 1 if t0 >= NLAT else 0
                h = hT[bi % 2]
                hv = hTv[bi % 2]
                k.dma(xb, xb.t[:, :, :n], self.XT, self.XT.t[:, :, t0:t0 + n].rearrange("kc p t -> p kc t"))
                for b_ in xbv:
                    b_.w = list(xb.w)
                self.rms_rstd(xb, xbv, 16, n, D, sqb, pS, rs)
                for kc in range(16):
                    tm_ = nxt("f", tmpf)
                    k.op(k.dve, lambda e, kc=kc, tm_=tm_: e.scalar_tensor_tensor(
                        tm_.t[:, :n], xb.t[:, kc, :n], self.Amix[l].t[:, kc, isc:isc + 1], rs.t[:, :n], ALU.mult, ALU.mult),
                         reads=[xbv[kc], rs, self.Amix[l]], writes=[tm_])
                    sh_ap = self.modT[l].t[:, kc, isc:isc + 1]
                    k.op(k.act, lambda e, kc=kc, tm_=tm_, sh_ap=sh_ap: e.activation(
                        h.t[:, kc, :n], tm_.t[:, :n], AF.Identity, bias=sh_ap), reads=[tm_, self.modT[l]], writes=[hv[kc]])
                for b_ in xbv:
                    xb.r = _compact(xb.r + b_.r)
                    b_.r = []

            prologue(0)
            for bi, (t0, n) in enumerate(BLOCKS):
                isc = 1 if t0 >= NLAT else 0
                tsl = slice(t0, t0 + n)
                nt = n // 128
                h = hT[bi % 2]
                hv = hTv[bi % 2]
                for i, nm in enumerate(("k_cosm", "k_sinm", "k_coss", "k_sins")):
                    k.dma(tab, tab.t[:, i, :n], self.C[nm], self.C[nm].t[:, tsl], more=(i > 0))

                def proj(oc, m=128):
                    g = ld["use"]
                    ld["use"] += 1
                    assert g % N_OC == oc
                    ensure(g + 2)
                    pcount["n"] += 1
                    flush(upto_tag=pcount["n"] - 2)
                    w = wfm[g % 3]
                    p = nxt("a", pA)
                    k.mm(p, p.t[0:m, :n], [(w.t[:, kc, 0:m], h.t[:, kc, :n]) for kc in range(16)], reads=[w] + hv)
                    run_tails()
                    return p

                for which in range(2):
                    oc0 = OC_CQ if which == 0 else OC_CKV
                    for j in range(4):
                        p = proj(oc0 + j)
                        evac_copy(p, n, cqvs[which][j], cqs[which].t[:, j, :n])
                for which in range(2):
                    self.rms_rstd(cqs[which], cqvs[which], 4, n, 512, sqb, pS, rss[which])
                for which in range(2):
                    gname = "g_mla_q" if which == 0 else "g_mla_kv"
                    cq, cqv, cqn, rs2 = cqs[which], cqvs[which], cqns[which], rss[which]
                    for j in range(4):
                        k.op(k.dve, lambda e, j=j: e.scalar_tensor_tensor(
                            cqn.t[:, j, :n], cq.t[:, j, :n], self.vec(gname, l, j), rs2.t[:, :n], ALU.mult, ALU.mult),
                             reads=[cqv[j], rs2, self.vecT], writes=[cqn])
                    if which == 0:
                        for hh in range(6):
                            p = nxt("a", pA)
                            k.mm(p, p.t[:, :n], [(wuq.t[:, hh, kc, :], cqn.t[:, kc, :n]) for kc in range(4)], reads=[wuq, cqn])
                            ob = stage(sbf)
                            evac_copy(p, n, ob, ob.t[:, :n])
                            store(self.QN, self.QN.t[hh, :, tsl], ob, ob.t[:, :n])
                        for pr in range(3):
                            p = nxt("a", pA)
                            k.mm(p, p.t[:, :n], [(wuq.t[:, 6 + pr, kc, :], cqn.t[:, kc, :n]) for kc in range(4)], reads=[wuq, cqn])
                            rope(p, n, 0, self.permm_b, self.QR, self.QR.t[pr, :, tsl])
                        run_tails()
                    else:
                        for hh in range(6):
                            p = nxt("a", pA)
                            k.mm(p, p.t[:, :n], [(wkf.t[:, hh, kc, :], cqn.t[:, kc, :n]) for kc in range(4)], reads=[wkf, cqn])
                            ob = stage(sbf)
                            evac_copy(p, n, ob, ob.t[:, :n])
                            store(self.KN, self.KN.t[hh, :, tsl], ob, ob.t[:, :n])
                        for ti in range(nt):
                            tt = slice(ti * 128, (ti + 1) * 128)
                            k.mm(pT[0], pT[0].t[:, :], [(cqn.t[:, kc, tt], wkt.t[:, kc, 0:512]) for kc in range(4)], reads=[wkt, cqn])
                            k.mm(pT[1], pT[1].t[:, 0:256], [(cqn.t[:, kc, tt], wkt.t[:, kc, 512:768]) for kc in range(4)], reads=[wkt, cqn])
                            so = stage(stm)
                            k.op(k.act, lambda e, so=so: e.activation(so.t[:, 0:512], pT[0].t[:, :], AF.Copy), reads=[pT[0]], writes=[so])
                            k.op(k.dve, lambda e, so=so: e.tensor_copy(so.t[:, 512:768], pT[1].t[:, 0:256]), reads=[pT[1]], writes=[so])
                            store(self.VM, self.VM.t[t0 + ti * 128:t0 + (ti + 1) * 128, :], so, so.t[:, :])
                if bi + 1 < len(BLOCKS):
                    prologue(bi + 1)
                p = proj(OC_KR)
                rope(p, n, 0, self.permm_b, self.KR, self.KR.t[:, tsl])
                for j in range(6):
                    p = proj(OC_SQ + j)
                    rope(p, n, 2, self.perms_b, self.SQ, self.SQ.t[j, :, tsl])
                for j in range(2):
                    p = proj(OC_SK + j)
                    rope(p, n, 2, self.perms_b, self.SK, self.SK.t[j, :, tsl])
                for j in range(4):
                    p = proj(OC_GQ + j)
                    of = stage(sff)
                    evac_copy(p, n, of, of.t[:, :n])
                    dst = self.GQ if j < 2 else self.GK
                    store(dst, dst.t[j % 2, :, tsl], of, of.t[:, :n])
                for j in range(4):
                    p = proj(OC_R + j)
                    of = stage(sff)
                    evac_copy(p, n, of, of.t[:, :n], func=AF.Silu)
                    store(self.GR, self.GR.t[j, :, tsl], of, of.t[:, :n])
                for d_ in range(2):
                    p = proj(OC_GLF + d_, m=16)
                    k.op(k.dve, lambda e, d_=d_, p=p: e.tensor_copy(glr.t[:, d_, :n], p.t[0:16, :n]), reads=[p], writes=[glr])
                    bname = "b_gla_gate_f" if d_ == 0 else "b_gla_gate_b"
                    dst = self.GLF if d_ == 0 else self.GLB
                    for j in range(2):
                        k.mm(pG, pG.t[:, :n], [(wgf.t[:, d_, j * 128:(j + 1) * 128], glr.t[:, d_, :n])], reads=[wgf, glr])
                        of = stage(sff)
                        k.op(k.act, lambda e, of=of, j=j, bname=bname: e.activation(
                            of.t[:, :n], pG.t[:, :n], AF.Exp, bias=self.negb[l].t[:, d_ * 2 + j:d_ * 2 + j + 1], scale=-1.0),
                             reads=[pG, self.negb[l]], writes=[of])
                        k.op(k.act, lambda e, of=of: e.activation(of.t[:, :n], of.t[:, :n], AF.Ln, bias=self.one_c.t[:, 0:1]),
                             reads=[of, self.one_c], writes=[of])
                        store(dst, dst.t[j, :, tsl], of, of.t[:, :n])
                run_tails()
                for ti in range(nt):
                    tt = slice(ti * 128, (ti + 1) * 128)
                    k.mm(pT[0], pT[0].t[:, :], [(h.t[:, kc, tt], wtm.t[:, kc, 0:512]) for kc in range(16)], reads=[wtm] + hv)
                    k.mm(pT[1], pT[1].t[:, 0:256], [(h.t[:, kc, tt], wtm.t[:, kc, 512:768]) for kc in range(16)], reads=[wtm] + hv)
                    so = stage(stm)
                    k.op(k.act, lambda e, so=so: e.activation(so.t[:, 0:512], pT[0].t[:, :], AF.Copy), reads=[pT[0]], writes=[so])
                    k.op(k.dve, lambda e, so=so: e.tensor_copy(so.t[:, 512:768], pT[1].t[:, 0:256]), reads=[pT[1]], writes=[so])
                    rows = slice(t0 + ti * 128, t0 + (ti + 1) * 128)
                    store(self.SV, self.SV.t[rows, :], so, so.t[:, 0:256])
                    store(self.GV, self.GV.t[rows, :], so, so.t[:, 256:768])
            flush(upto_tag=10 ** 9)

    def p2(self, l):
        k = self.k
        last = (l == DEPTH - 1)
        scale = float((128 + 64) ** -0.5)
        with contextlib.ExitStack() as st:
            knT = [k.sb(st, f"p2_kn{i}", (128, T), BF16) for i in range(2)]
            vT = [k.sb(st, f"p2_v{i}", (128, 34, 128), BF16) for i in range(2)]
            krT = k.sb(st, "p2_kr", (128, T), BF16)
            qn = [k.sb(st, f"p2_qn{i}", (128, 512), BF16) for i in range(3)]
            qr = [k.sb(st, f"p2_qr{i}", (128, 512), BF16) for i in range(3)]
            pt = [k.sb(st, f"p2_pt{i}", (128, 512), BF16) for i in range(4)]
            rden = k.sb(st, "p2_rden", (128, 512), F32)
            ob = [k.sb(st, f"p2_ob{i}", (128, 512), BF16) for i in range(2)]
            pS = [k.ps(st, f"p2_pS{i}", (128, 512)) for i in range(3)]
            pO = [k.ps(st, f"p2_pO{i}", (128, 512)) for i in range(2)]
            pD = [k.ps(st, f"p2_pD{i}", (128, 512)) for i in range(2)]
            dacc = [k.sb(st, f"p2_dacc{i}", (128, 512), F32) for i in range(2)]
            k.dma(krT, krT.t[:], self.KR, self.KR.t[:, :])
            blocks = BLOCKS if not last else BLOCKS[:8]
            iters = [(h, t0, n) for h in range(6) for (t0, n) in blocks]

            def load_kv(h):
                kn_, v_ = knT[h % 2], vT[h % 2]
                k.dma(kn_, kn_.t[:], self.KN, self.KN.t[h, :, :])
                k.dma(v_, v_.t[:], self.VM, self.VM.t[:, h * 128:(h + 1) * 128].rearrange("(kt p) d -> p kt d", p=128))

            def load_q(it):
                h, t0, n = iters[it]
                tsl = slice(t0, t0 + n)
                k.dma(qn[it % 3], qn[it % 3].t[:, :n], self.QN, self.QN.t[h, :, tsl])
                k.dma(qr[it % 3], qr[it % 3].t[:, :n], self.QR, self.QR.t[h // 2, :, tsl])

            load_kv(0)
            load_q(0)
            ipt = 0
            pend_store = None
            for it, (h, t0, n) in enumerate(iters):
                kn_, v_ = knT[h % 2], vT[h % 2]
                half = (h % 2) * 64
                tsl = slice(t0, t0 + n)
                qn_, qr_ = qn[it % 3], qr[it % 3]
                po, pd = pO[it % 2], pD[it % 2]
                da = dacc[it % 2]
                o_ = ob[it % 2]
                if it + 1 < len(iters):
                    load_q(it + 1)
                if t0 == 0 and h + 1 < 6:
                    load_kv(h + 1)
                if pend_store is not None:
                    pend_store()
                    pend_store = None
                kts = list(range(34)) if t0 < NLAT else [32, 33]

                def qk(kt, ps_):
                    ks = slice(kt * 128, (kt + 1) * 128)
                    k.mm(ps_, ps_.t[:, :n], [(kn_.t[:, ks], qn_.t[:, :n]),
                                             (krT.t[half:half + 64, ks], qr_.t[half:half + 64, :n])],
                         reads=[kn_, qn_, krT, qr_])

                ps_cur = pS[ipt % 3]
                qk(kts[0], ps_cur)
                if len(kts) > 1:
                    qk(kts[1], pS[(ipt + 1) % 3])
                for i, kt in enumerate(kts):
                    ps_nxt = pS[(ipt + 1) % 3]
                    if i + 2 < len(kts):
                        qk(kts[i + 2], pS[(ipt + 2) % 3])
                    p_ = pt[ipt % 4]
                    k.op(k.act, lambda e, p_=p_, ps_cur=ps_cur: e.activation(p_.t[:, :n], ps_cur.t[:, :n], AF.Exp, scale=scale),
                         reads=[ps_cur], writes=[p_])
                    k.mm(po, po.t[:, :n], [(v_.t[:, kt, :], p_.t[:, :n])], reads=[v_, p_],
                         first=(i == 0), last=(i == len(kts) - 1),
                         mark=(i % 3 == 0 or i == len(kts) - 1))
                    if i % 3 == 0:
                        if i == 0:
                            k.op(k.dve, lambda e, p_=p_, da=da: e.tensor_copy(da.t[:, :n], p_.t[:, :n]), reads=[p_], writes=[da])
                        else:
                            k.op(k.dve, lambda e, p_=p_, da=da: e.tensor_tensor(da.t[:, :n], da.t[:, :n], p_.t[:, :n], ALU.add),
                                 reads=[p_], writes=[da])
                    else:
                        k.mm(pd, pd.t[:, :n], [(self.ones_b.t[:], p_.t[:, :n])], reads=[self.ones_b, p_, v_],
                             first=(i == 1), last=False)
                    ipt += 1
                    ps_cur = ps_nxt
                    if i % 4 == 3:
                        self.conv_step(1)
                k.mm(pd, pd.t[:, :n], [(self.ones_f.t[:], da.t[:, :n])], reads=[self.ones_f, da], first=False, last=True)
                k.op(k.dve, lambda e, pd=pd: e.reciprocal(rden.t[:, :n], pd.t[:, :n]), reads=[pd], writes=[rden])
                k.op(k.dve, lambda e, po=po, o_=o_: e.tensor_tensor(o_.t[:, :n], po.t[:, :n], rden.t[:, :n], ALU.mult),
                     reads=[po, rden], writes=[o_])
                pend_store = (lambda h=h, tsl=tsl, o_=o_, n=n: k.dma(self.MIX, self.MIX.t[h, :, tsl], o_, o_.t[:, :n]))
            if pend_store is not None:
                pend_store()

    def p3(self, l):
        k = self.k
        last = (l == DEPTH - 1)
        scale = float(128 ** -0.5)
        with contextlib.ExitStack() as st:
            kT = [k.sb(st, f"p3_k{i}", (128, T), BF16) for i in range(2)]
            vT = [k.sb(st, f"p3_v{i}", (128, 34, 128), BF16) for i in range(2)]
            q = [k.sb(st, f"p3_q{i}", (128, 512), BF16) for i in range(3)]
            pt = [k.sb(st, f"p3_pt{i}", (128, 512), BF16) for i in range(4)]
            rden = k.sb(st, "p3_rden", (128, 512), F32)
            ob = [k.sb(st, f"p3_ob{i}", (128, 512), BF16) for i in range(2)]
            pS = [k.ps(st, f"p3_pS{i}", (128, 512)) for i in range(3)]
            pO = [k.ps(st, f"p3_pO{i}", (128, 512)) for i in range(2)]
            pD = [k.ps(st, f"p3_pD{i}", (128, 512)) for i in range(2)]
            blocks = BLOCKS if not last else BLOCKS[:8]
            iters = [(g, j, wi, t0, n) for g in range(2) for j in range(3) for wi, (t0, n) in enumerate(blocks)]

            def load_kv(g):
                k_, v_ = kT[g % 2], vT[g % 2]
                k.dma(k_, k_.t[:], self.SK, self.SK.t[g, :, :])
                k.dma(v_, v_.t[:], self.SV, self.SV.t[:, g * 128:(g + 1) * 128].rearrange("(kt p) d -> p kt d", p=128))

            def load_q(it):
                g, j, wi, t0, n = iters[it]
                k.dma(q[it % 3], q[it % 3].t[:, :n], self.SQ, self.SQ.t[g * 3 + j, :, t0:t0 + n])

            load_kv(0)
            load_kv(1)
            load_q(0)
            ipt = 0
            pend_store = None
            for it, (g, j, wi, t0, n) in enumerate(iters):
                k_, v_ = kT[g % 2], vT[g % 2]
                hq = g * 3 + j
                tsl = slice(t0, t0 + n)
                q_ = q[it % 3]
                po, pd = pO[it % 2], pD[it % 2]
                o_ = ob[it % 2]
                if it + 1 < len(iters):
                    load_q(it + 1)
                if pend_store is not None:
                    pend_store()
                    pend_store = None
                if t0 < NLAT:
                    kts = [(4 * wi - 1 + r, r) for r in range(6) if 0 <= 4 * wi - 1 + r < 32]
                    kts += [(32, None), (33, None)]
                else:
                    kts = [(32, None), (33, None)]

                def qk(kt, r, ps_):
                    ks = slice(kt * 128, (kt + 1) * 128)
                    terms = [(k_.t[:, ks], q_.t[:, :n])]
                    rd = [k_, q_]
                    if r is not None:
                        terms.append((self.ident_b.t[:], self.mask_b.t[:, r, :n]))
                        rd += [self.ident_b, self.mask_b]
                    k.mm(ps_, ps_.t[:, :n], terms, reads=rd)

                ps_cur = pS[ipt % 3]
                qk(kts[0][0], kts[0][1], ps_cur)
                if len(kts) > 1:
                    qk(kts[1][0], kts[1][1], pS[(ipt + 1) % 3])
                for i, (kt, r) in enumerate(kts):
                    ps_nxt = pS[(ipt + 1) % 3]
                    if i + 2 < len(kts):
                        qk(kts[i + 2][0], kts[i + 2][1], pS[(ipt + 2) % 3])
                    p_ = pt[ipt % 4]
                    k.op(k.act, lambda e, p_=p_, ps_cur=ps_cur: e.activation(p_.t[:, :n], ps_cur.t[:, :n], AF.Exp, scale=scale),
                         reads=[ps_cur], writes=[p_])
                    k.mm(po, po.t[:, :n], [(v_.t[:, kt, :], p_.t[:, :n])], reads=[v_, p_],
                         first=(i == 0), last=(i == len(kts) - 1), mark=(i == 0 or i == len(kts) - 1))
                    k.mm(pd, pd.t[:, :n], [(self.ones_b.t[:], p_.t[:, :n])], reads=[self.ones_b, p_, v_],
                         first=(i == 0), last=(i == len(kts) - 1))
                    ipt += 1
                    ps_cur = ps_nxt
                    if i == 3:
                        self.conv_step(1)
                sk_ap = self.sinkb.t[:, l * 6 + hq:l * 6 + hq + 1]
                k.op(k.dve, lambda e, pd=pd, sk_ap=sk_ap: e.tensor_scalar(rden.t[:, :n], pd.t[:, :n], sk_ap, None, ALU.add),
                     reads=[pd, self.sinkb], writes=[rden])
                k.op(k.dve, lambda e: e.reciprocal(rden.t[:, :n], rden.t[:, :n]), reads=[rden], writes=[rden])
                k.op(k.dve, lambda e, po=po, o_=o_: e.tensor_tensor(o_.t[:, :n], po.t[:, :n], rden.t[:, :n], ALU.mult),
                     reads=[po, rden], writes=[o_])
                pend_store = (lambda hq=hq, tsl=tsl, o_=o_, n=n: k.dma(self.MIX, self.MIX.t[6 + hq, :, tsl], o_, o_.t[:, :n]))
            if pend_store is not None:
                pend_store()

    def p4(self, l):
        k = self.k
        last = (l == DEPTH - 1)
        SEG = 1024
        with contextlib.ExitStack() as st:
            bsets = []
            for u in range(2):
                bsets.append((
                    [k.sb(st, f"p4_sp{u}_{i}", (128, SEG), F32) for i in range(2)],
                    k.sb(st, f"p4_q32_{u}", (128, SEG), F32),
                    k.sb(st, f"p4_k32_{u}", (128, SEG), F32),
                    k.sb(st, f"p4_e1_{u}", (128, SEG), F32),
                    k.sb(st, f"p4_e2_{u}", (128, SEG), F32),
                    k.sb(st, f"p4_qb_{u}", (128, SEG), BF16),
                    k.sb(st, f"p4_kb_{u}", (128, SEG), BF16),
                    k.sb(st, f"p4_kd_{u}", (128, SEG), BF16),
                    k.sb(st, f"p4_kdT_{u}", (128, SEG // 128, 128), BF16),
                    k.sb(st, f"p4_v_{u}", (128, SEG // 128, 256), BF16),
                    k.sb(st, f"p4_dec_{u}", (128, SEG // 64), F32),
                ))
            self._p4_unit = 0
            S = [k.sb(st, f"p4_S{d}", (128, 128), F32) for d in range(2)]
            Sb = [[k.sb(st, f"p4_Sb{d}_{i}", (128, 128), BF16) for i in range(2)] for d in range(2)]
            at = [k.sb(st, f"p4_at{i}", (128, 128), BF16) for i in range(4)]
            oacc = k.sb(st, "p4_oacc", (128, 2, NLAT), F32)
            oacv = k.views(oacc, 2 * (NLAT // 128))
            gm = k.sb(st, "p4_gm", (128, 2, 128), F32)
            rseg = k.sb(st, "p4_r", (128, 2, 512), F32)
            sq = [k.sb(st, f"p4_sq{i}", (128, 512), BF16) for i in range(2)]
            rs = k.sb(st, "p4_rs", (128, 512), F32)
            yo = [k.sb(st, f"p4_yo{i}", (128, 512), BF16) for i in range(2)]
            pA = [k.ps(st, f"p4_pA{i}", (128, 512)) for i in range(2)]
            pO = [k.ps(st, f"p4_pO{i}", (128, 512)) for i in range(2)]
            pS = k.ps(st, "p4_pS", (128, 512))
            pT = k.ps(st, "p4_pT", (128, 1024), BF16)
            pN = k.ps(st, "p4_pN", (128, 512))
            k.dma(gm, gm.t[:], self.C["k_gmask"], self.C["k_gmask"].t.rearrange("d a b -> a d b"))
            cnt = {}

            def nxt(lst):
                key = id(lst)
                cnt[key] = cnt.get(key, 0) + 1
                return lst[(cnt[key] - 1) % len(lst)]

            def gla_pass(pr, d, t0, ntok, add_to_acc, acc0):
                GL = self.GLF if d == 0 else self.GLB
                nseg = (ntok + SEG - 1) // SEG
                segs = list(range(nseg))
                if d == 1:
                    segs = segs[::-1]
                def seg_body(sg, bs):
                    spb, qf32, kf32, e1, e2, qb_, kb_, kd_, kdT, vseg, dec = bs
                    s0 = t0 + sg * SEG
                    ns = min(SEG, ntok - sg * SEG)
                    nch = ns // 64
                    ntile = ns // 128
                    ssl = slice(s0, s0 + ns)
                    a, b = spb
                    k.dma(a, a.t[:, :ns], GL, GL.t[pr, :, ssl])
                    k.dma(qf32, qf32.t[:, :ns], self.GQ, self.GQ.t[pr, :, ssl])
                    k.dma(kf32, kf32.t[:, :ns], self.GK, self.GK.t[pr, :, ssl])
                    k.dma(vseg, vseg.t[:, :ntile, :], self.GV,
                          self.GV.t[ssl, pr * 256:(pr + 1) * 256].rearrange("(a p) c -> p a c", p=128))
                    v3 = lambda buf: buf.t[:, :ns].rearrange("p (c l) -> p c l", l=64)
                    s_ = 1
                    while s_ < 64:
                        A3, B3 = v3(a), v3(b)
                        if d == 0:
                            k.op(k.dve, lambda e, A3=A3, B3=B3, s_=s_: e.tensor_tensor(B3[:, :, s_:], A3[:, :, s_:], A3[:, :, :64 - s_], ALU.add),
                                 reads=[a], writes=[b])
                            k.op(k.act, lambda e, A3=A3, B3=B3, s_=s_: e.copy(B3[:, :, :s_], A3[:, :, :s_]), reads=[a], writes=[b])
                        else:
                            k.op(k.dve, lambda e, A3=A3, B3=B3, s_=s_: e.tensor_tensor(B3[:, :, :64 - s_], A3[:, :, :64 - s_], A3[:, :, s_:], ALU.add),
                                 reads=[a], writes=[b])
                            k.op(k.act, lambda e, A3=A3, B3=B3, s_=s_: e.copy(B3[:, :, 64 - s_:], A3[:, :, 64 - s_:]), reads=[a], writes=[b])
                        a, b = b, a
                        s_ *= 2
                    cs = a
                    k.op(k.act, lambda e: e.activation(e1.t[:, :ns], cs.t[:, :ns], AF.Exp, scale=-1.0 / 16), reads=[cs], writes=[e1])
                    k.op(k.act, lambda e: e.activation(e2.t[:, :ns], cs.t[:, :ns], AF.Exp, scale=1.0 / 16), reads=[cs], writes=[e2])
                    e13 = e1.t[:, :ns].rearrange("p (c l) -> p c l", l=64)
                    edge = 63 if d == 0 else 0
                    k.op(k.act, lambda e: e.copy(dec.t[:, :nch], e13[:, :, edge]), reads=[e1], writes=[dec])
                    k.op(k.dve, lambda e: e.scalar_tensor_tensor(qb_.t[:, :ns], qf32.t[:, :ns], 0.125, e1.t[:, :ns], ALU.mult, ALU.mult),
                         reads=[qf32, e1], writes=[qb_])
                    k.op(k.dve, lambda e: e.tensor_tensor(kf32.t[:, :ns], kf32.t[:, :ns], e2.t[:, :ns], ALU.mult),
                         reads=[kf32, e2], writes=[kf32])
                    k.op(k.act, lambda e: e.copy(kb_.t[:, :ns], kf32.t[:, :ns]), reads=[kf32], writes=[kb_])
                    k3 = kf32.t[:, :ns].rearrange("p (c l) -> p c l", l=64)
                    kd3 = kd_.t[:, :ns].rearrange("p (c l) -> p c l", l=64)
                    k.op(k.dve, lambda e: e.tensor_tensor(kd3, k3, dec.t[:, :nch].unsqueeze(2).to_broadcast([128, nch, 64]), ALU.mult),
                         reads=[kf32, dec], writes=[kd_])
                    for ti in range(ntile):
                        k.mm(pT, pT.t[:, ti * 128:(ti + 1) * 128], [(kd_.t[:, ti * 128:(ti + 1) * 128], self.ident_b.t[:])],
                             reads=[kd_, self.ident_b], transpose=True, first=(ti == 0))
                    k.op(k.act, lambda e: e.activation(kdT.t[:, :ntile, :].rearrange("p a c -> p (a c)"), pT.t[:, :ntile * 128], AF.Copy),
                         reads=[pT], writes=[kdT])
                    self.conv_step(4)
                    yield
                    tiles = list(range(ntile))
                    if d == 1:
                        tiles = tiles[::-1]
                    for ti in tiles:
                        tsl = slice(ti * 128, (ti + 1) * 128)
                        gt = (s0 - t0) // 128 + ti
                        for hh in range(2):
                            hp = slice(hh * 64, (hh + 1) * 64)
                            k.mm(pA[hh], pA[hh].t[:, 0:128], [(kb_.t[hp, tsl], qb_.t[hp, tsl])], reads=[kb_, qb_])
                        ats = []
                        for hh in range(2):
                            a_ = nxt(at)
                            k.op(k.dve, lambda e, a_=a_, hh=hh: e.tensor_tensor(a_.t[:], pA[hh].t[:, 0:128], gm.t[:, d, :], ALU.mult),
                                 reads=[pA[hh], gm], writes=[a_])
                            ats.append(a_)
                        chunks = [0, 1] if d == 0 else [1, 0]
                        for hh in range(2):
                            k.mm(pO[hh], pO[hh].t[:, 0:128], [(vseg.t[:, ti, hh * 128:(hh + 1) * 128], ats[hh].t[:])],
                                 reads=[vseg, ats[hh]], first=True, last=False)
                        for ci, c in enumerate(chunks):
                            csl = slice(ti * 128 + c * 64, ti * 128 + (c + 1) * 64)
                            sb_cur = self._p4_sb[d]
                            for hh in range(2):
                                hp = slice(hh * 64, (hh + 1) * 64)
                                k.mm(pO[hh], pO[hh].t[:, c * 64:(c + 1) * 64], [(sb_cur.t[hp, :], qb_.t[hp, csl])],
                                     reads=[sb_cur, qb_], first=False, last=(ci == 1))
                            cp = slice(c * 64, (c + 1) * 64)
                            for hh in range(2):
                                k.mm(pS, pS.t[hh * 64:(hh + 1) * 64, 0:128],
                                     [(kdT.t[cp, ti, hh * 64:(hh + 1) * 64], vseg.t[cp, ti, hh * 128:(hh + 1) * 128])],
                                     reads=[kdT, vseg], first=(hh == 0), start=True)
                            cidx = ti * 2 + c
                            nb = nxt(Sb[d])
                            k.op(k.dve, lambda e, cidx=cidx, nb=nb: e.scalar_tensor_tensor(
                                nb.t[:], S[d].t[:], dec.t[:, cidx:cidx + 1], pS.t[:, 0:128], ALU.mult, ALU.add),
                                 reads=[S[d], dec, pS], writes=[nb])
                            k.op(k.dve, lambda e, cidx=cidx: e.scalar_tensor_tensor(
                                S[d].t[:], S[d].t[:], dec.t[:, cidx:cidx + 1], pS.t[:, 0:128], ALU.mult, ALU.add),
                                 reads=[S[d], dec, pS], writes=[S[d]])
                            self._p4_sb[d] = nb
                        for hh in range(2):
                            ov = oacv[hh * (NLAT // 128) + acc0 + gt]
                            o_ap = oacc.t[:, hh, (acc0 + gt) * 128:(acc0 + gt + 1) * 128]
                            if add_to_acc:
                                k.op(k.dve, lambda e, hh=hh, o_ap=o_ap: e.tensor_tensor(o_ap, pO[hh].t[:, 0:128], o_ap, ALU.add),
                                     reads=[pO[hh]], writes=[ov])
                            else:
                                k.op(k.act, lambda e, hh=hh, o_ap=o_ap: e.activation(o_ap, pO[hh].t[:, 0:128], AF.Copy),
                                     reads=[pO[hh]], writes=[ov])

                gens = []
                for sg in segs:
                    gens.append(seg_body(sg, bsets[self._p4_unit % 2]))
                    self._p4_unit += 1
                return gens

            def finalize(pr, t0, ntok, acc0):
                for b0 in range(0, ntok, 512):
                    n = min(512, ntok - b0)
                    tsl = slice(t0 + b0, t0 + b0 + n)
                    k.dma(rseg, rseg.t[:, :, :n], self.GR, self.GR.t[2 * pr:2 * pr + 2, :, tsl].rearrange("j p t -> p j t"))
                    for hh in range(2):
                        a0 = acc0 * 128 + b0
                        o_ap = oacc.t[:, hh, a0:a0 + n]
                        ovs = [oacv[hh * (NLAT // 128) + acc0 + b0 // 128 + i] for i in range(n // 128)]
                        s_ = nxt(sq)
                        k.op(k.act, lambda e, s_=s_, o_ap=o_ap: e.activation(s_.t[:, :n], o_ap, AF.Square), reads=ovs, writes=[s_])
                        k.mm(pN, pN.t[:, :n], [(self.ones_b.t[:], s_.t[:, :n])], reads=[self.ones_b, s_])
                        k.op(k.act, lambda e: e.activation(rs.t[:, :n], pN.t[:, :n], AF.Sqrt, bias=self.eps_c.t[:, 0:1], scale=1.0 / 128),
                             reads=[pN, self.eps_c], writes=[rs])
                        k.op(k.dve, lambda e: e.reciprocal(rs.t[:, :n], rs.t[:, :n]), reads=[rs], writes=[rs])
                        k.op(k.dve, lambda e, o_ap=o_ap, hh=hh: e.scalar_tensor_tensor(
                            rs.t[:, :n], o_ap, self.vec("g_gla_out", l, 2 * pr + hh), rs.t[:, :n], ALU.mult, ALU.mult),
                             reads=ovs + [rs, self.vecT], writes=[rs])
                        y_ = nxt(yo)
                        k.op(k.dve, lambda e, y_=y_, hh=hh: e.tensor_tensor(y_.t[:, :n], rs.t[:, :n], rseg.t[:, hh, :n], ALU.mult),
                             reads=[rs, rseg], writes=[y_])
                        k.dma(self.MIX, self.MIX.t[12 + 2 * pr + hh, :, tsl], y_, y_.t[:, :n])

            for pr in range(2):
                self._p4_sb = [None, None]
                for d in range(2):
                    k.op(k.dve, lambda e, d=d: e.memset(S[d].t[:], 0.0), writes=[S[d]])
                    nb = nxt(Sb[d])
                    k.op(k.dve, lambda e, nb=nb: e.memset(nb.t[:], 0.0), writes=[nb])
                    self._p4_sb[d] = nb
                units = []
                units += gla_pass(pr, 0, NLAT, NCTX, False, 0)
                units += gla_pass(pr, 1, NLAT, NCTX, True, 0)
                if not last:
                    units.append(("fin", NLAT, NCTX))
                units += gla_pass(pr, 0, 0, NLAT, False, 0)
                units += gla_pass(pr, 1, 0, NLAT, True, 0)
                units.append(("fin", 0, NLAT))
                started = set()
                for i, u in enumerate(units):
                    if isinstance(u, tuple):
                        finalize(pr, u[1], u[2], 0)
                        continue
                    if i not in started:
                        next(u)
                        started.add(i)
                    for j in range(i + 1, len(units)):
                        if not isinstance(units[j], tuple):
                            if j not in started:
                                next(units[j])
                                started.add(j)
                            break
                    for _ in u:
                        pass
                self.conv_step(4)

    def p56(self, l):
        k = self.k
        last = (l == DEPTH - 1)
        final = (l == self.L - 1)
        with contextlib.ExitStack() as st:
            xbs = [k.sb(st, f"p5_xb{i}", (128, 16, 512), F32) for i in range(1 if final else 2)]
            xbvs = [k.views(b_, 16) for b_ in xbs]
            mh = k.sb(st, "p5_mh", (128, 16, 512), BF16)
            mhv = k.views(mh, 16)
            aT = k.sb(st, "p5_aT", (128, NFC, 512), BF16)
            aTv = k.views(aT, NFC)
            wo = [k.sb(st, f"p5_wo{i}", (128, 16, 128), BF16) for i in range(3)]
            wg = [k.sb(st, f"p5_wg{i}", (128, 16, 128), BF16) for i in range(2)]
            wu = [k.sb(st, f"p5_wu{i}", (128, 16, 128), BF16) for i in range(2)]
            wd = [k.sb(st, f"p5_wd{i}", (128, NFC, 128), BF16) for i in range(2)]
            sg = [k.sb(st, f"p5_sg{i}", (128, 512), F32) for i in range(2)]
            sqb = [k.sb(st, f"p5_sq{i}", (128, 512), BF16) for i in range(4)]
            tmpf = [k.sb(st, f"p5_tmp{i}", (128, 512), F32) for i in range(3)]
            rs = k.sb(st, "p5_rs", (128, 512), F32)
            ostg = [k.sb(st, f"p5_os{i}", (128, D), F32) for i in range(2)] if final else None
            pA = [k.ps(st, f"p5_pA{i}", (128, 512)) for i in range(2)]
            pG = [k.ps(st, f"p5_pG{i}", (128, 512)) for i in range(2)]
            pU = [k.ps(st, f"p5_pU{i}", (128, 512)) for i in range(2)]
            pS = k.ps(st, "p5_pS", (128, 512))
            cnt = {}

            def nxt(lst):
                key = id(lst)
                cnt[key] = cnt.get(key, 0) + 1
                return lst[(cnt[key] - 1) % len(lst)]

            valid = [bi for bi, (t0, n) in enumerate(BLOCKS) if not (t0 >= NLAT and last)]
            pOP = pA + pG + pU

            def load_x(vi_):
                t0_, n_ = BLOCKS[valid[vi_]]
                xb_ = xbs[vi_ % len(xbs)]
                k.dma(xb_, xb_.t[:, :, :n_], self.XT, self.XT.t[:, :, t0_:t0_ + n_].rearrange("kc p t -> p kc t"))
                for b_ in xbvs[vi_ % len(xbs)]:
                    b_.w = list(xb_.w)

            def load_mix(bi):
                t0_, n_ = BLOCKS[bi]
                for b_ in mhv:
                    mh.r = _compact(mh.r + b_.r + b_.w)
                k.dma(mh, mh.t[:, :, :n_], self.MIX, self.MIX.t[:, :, t0_:t0_ + n_].rearrange("kc p t -> p kc t"))
                for b_ in mhv:
                    b_.w = list(mh.w)
                    b_.r = []

            for bi, (t0, n) in enumerate(BLOCKS):
                isc = 1 if t0 >= NLAT else 0
                if isc and last:
                    continue
                tsl = slice(t0, t0 + n)
                nt = n // 128
                vi = valid.index(bi)
                if len(xbs) == 1 or vi == 0:
                    load_x(vi)
                xb = xbs[vi % len(xbs)]
                xbv = xbvs[vi % len(xbs)]
                if bi == valid[0]:
                    load_mix(bi)
                for oc in range(16):
                    w = nxt(wo)
                    k.dma(w, w.t[:], self.WOUT[l], self.WOUT[l].t[oc])
                    p = nxt(pOP)
                    k.mm(p, p.t[:, :n], [(w.t[:, kc, :], mh.t[:, kc, :n]) for kc in range(16)], reads=[w] + mhv)
                    g_ap = self.modT[l].t[:, 32 + oc, isc:isc + 1]
                    k.op(k.dve, lambda e, p=p, oc=oc, g_ap=g_ap: e.scalar_tensor_tensor(
                        xb.t[:, oc, :n], p.t[:, :n], g_ap, xb.t[:, oc, :n], ALU.mult, ALU.add),
                         reads=[p, self.modT[l]], writes=[xbv[oc]])
                self.rms_rstd(xb, xbv, 16, n, D, sqb, pS, rs)
                for kc in range(16):
                    tm_ = nxt(tmpf)
                    k.op(k.dve, lambda e, kc=kc, tm_=tm_: e.scalar_tensor_tensor(
                        tm_.t[:, :n], xb.t[:, kc, :n], self.Affn[l].t[:, kc, isc:isc + 1], rs.t[:, :n], ALU.mult, ALU.mult),
                         reads=[xbv[kc], rs, self.Affn[l]], writes=[tm_])
                    sh_ap = self.modT[l].t[:, 48 + kc, isc:isc + 1]
                    k.op(k.act, lambda e, kc=kc, tm_=tm_, sh_ap=sh_ap: e.activation(
                        mh.t[:, kc, :n], tm_.t[:, :n], AF.Identity, bias=sh_ap), reads=[tm_, self.modT[l]], writes=[mhv[kc]])
                if len(xbs) == 2 and vi + 1 < len(valid):
                    load_x(vi + 1)
                for fc in range(NFC):
                    w1, w2 = nxt(wg), nxt(wu)
                    k.dma(w1, w1.t[:], self.WGU[l], self.WGU[l].t[fc])
                    k.dma(w2, w2.t[:], self.WGU[l], self.WGU[l].t[NFC + fc])
                    p1, p2 = nxt(pG), nxt(pU)
                    k.mm(p1, p1.t[:, :n], [(w1.t[:, kc, :], mh.t[:, kc, :n]) for kc in range(16)], reads=[w1] + mhv)
                    k.mm(p2, p2.t[:, :n], [(w2.t[:, kc, :], mh.t[:, kc, :n]) for kc in range(16)], reads=[w2] + mhv)
                    s_ = nxt(sg)
                    k.op(k.act, lambda e, s_=s_, p1=p1: e.activation(s_.t[:, :n], p1.t[:, :n], AF.Silu), reads=[p1], writes=[s_])
                    k.op(k.dve, lambda e, s_=s_, p2=p2, fc=fc: e.tensor_tensor(aT.t[:, fc, :n], s_.t[:, :n], p2.t[:, :n], ALU.mult),
                         reads=[s_, p2], writes=[aTv[fc]])
                for oc in range(16):
                    if oc == 2 and valid.index(bi) + 1 < len(valid):
                        load_mix(valid[valid.index(bi) + 1])
                    w = nxt(wd)
                    k.dma(w, w.t[:], self.WDN[l], self.WDN[l].t[oc])
                    p = nxt(pA)
                    k.mm(p, p.t[:, :n], [(w.t[:, fc, :], aT.t[:, fc, :n]) for fc in range(NFC)], reads=[w] + aTv)
                    g_ap = self.modT[l].t[:, 80 + oc, isc:isc + 1]
                    k.op(k.dve, lambda e, p=p, oc=oc, g_ap=g_ap: e.scalar_tensor_tensor(
                        xb.t[:, oc, :n], p.t[:, :n], g_ap, xb.t[:, oc, :n], ALU.mult, ALU.add),
                         reads=[p, self.modT[l]], writes=[xbv[oc]])
                if not final or "XT" in self.dbg:
                    k.dma(self.XT, self.XT.t[:, :, tsl].rearrange("kc p t -> p kc t"), xbv, xb.t[:, :, :n])
                if final and not isc:
                    self.rms_rstd(xb, xbv, 16, n, D, sqb, pS, rs)
                    for kc in range(16):
                        k.op(k.dve, lambda e, kc=kc: e.scalar_tensor_tensor(
                            xb.t[:, kc, :n], xb.t[:, kc, :n], self.vec("g_final", None, kc), rs.t[:, :n], ALU.mult, ALU.mult),
                             reads=[rs, self.vecT], writes=[xbv[kc]])
                    for ti in range(nt):
                        og = nxt(ostg)
                        for q4 in range(4):
                            p = nxt(pA)
                            for j in range(4):
                                kc = q4 * 4 + j
                                k.mm(p, p.t[:, j * 128:(j + 1) * 128], [(xb.t[:, kc, ti * 128:(ti + 1) * 128], self.ident_f.t[:])],
                                     reads=[xbv[kc], self.ident_f], transpose=True, first=(j == 0))
                            if q4 % 2 == 0:
                                k.op(k.act, lambda e, p=p, og=og, q4=q4: e.activation(og.t[:, q4 * 512:(q4 + 1) * 512], p.t[:, :], AF.Copy),
                                     reads=[p], writes=[og])
                            else:
                                k.op(k.dve, lambda e, p=p, og=og, q4=q4: e.tensor_copy(og.t[:, q4 * 512:(q4 + 1) * 512], p.t[:, :]),
                                     reads=[p], writes=[og])
                        k.dma(self.out, self.out.t[t0 + ti * 128:t0 + (ti + 1) * 128, :], og, og.t[:, :])
                for b_ in xbv:
                    xb.r = _compact(xb.r + b_.r + b_.w)
                    b_.r = []
                for b_ in aTv:
                    pass

    def build(self):
        k = self.k
        self.conv_rate = 0
        with contextlib.ExitStack() as st:
            self.consts(st)
            self.conv_setup(st)
            self.conv_until(0, 0)
            self.cv_throttle = True
            self.p0_transpose_in()
            self.p0_mod(st)
            if self.stop_after == "p0":
                return self.finish()
            for l in range(self.L):
                self.conv_until(l, 0)
                if self.stop_after == "conv":
                    return self.finish()
                if "skip_p1" not in self.dbg:
                    self.p1(l)
                if self.stop_after == "p1":
                    return self.finish()
                if "skip_p2" not in self.dbg:
                    self.p2(l)
                if self.stop_after == "p2":
                    return self.finish()
                if "skip_p3" not in self.dbg:
                    self.p3(l)
                if self.stop_after == "p3":
                    return self.finish()
                if "skip_p4" not in self.dbg:
                    self.p4(l)
                if self.stop_after == "p4":
                    return self.finish()
                self.conv_until(l, 1)
                self.p56(l)
            return self.finish()

    def finish(self):
        k = self.k
        toks = []
        for b in [self.out, self.XT, self.QN, self.QR, self.KN, self.KR, self.VM, self.SQ, self.SK, self.SV,
                  self.GQ, self.GK, self.GV, self.GLF, self.GLB, self.GR, self.MIX]:
            toks += b.w
        k.sp.wait(toks)
        k.sp.wait([(k.pe, k.pe.cnt), (k.act, k.act.cnt), (k.dve, k.dve.cnt), (k.pool, k.pool.cnt)])


def build_nc(n_layers=DEPTH, dbg=(), stop_after=None):
    nc = bass.Bass("TRN2", target_bir_lowering=False)
    with contextlib.ExitStack() as es:
        prog = Prog(nc, es, n_layers=n_layers, dbg=dbg, stop_after=stop_after)
        prog.build()
    return nc


def make_in_maps(inputs, cores):
    consts = host_consts()
    maps = []
    for b in cores:
        m = {
            "x": np.ascontiguousarray(inputs["x"][b]),
            "c": np.ascontiguousarray(inputs["c"][b]).reshape(16, 128),
            "ctx": np.ascontiguousarray(inputs["ctx"][b]),
            "c_ctx": np.ascontiguousarray(inputs["c_ctx"]).reshape(16, 128),
        }
        for n in W_SHAPES:
            m[n] = np.ascontiguousarray(inputs[n])
        m.update(consts)
        maps.append(m)
    return maps


def kernel(**inputs):
    inputs = {k_: np.asarray(v) for k_, v in inputs.items()}
    nc = build_nc()
    maps = make_in_maps(inputs, list(range(8)))
    res = run_bass_kernel_spmd(nc, maps, core_ids=list(range(8)))
    return np.stack([np.asarray(r["out"]) for r in res.results], axis=0).astype(np.float32)
```
